# Optimizing a Trainium2 kernel written in Bass

```python
import math
import jax, jax.numpy as jnp
from jax import lax
import numpy as np

D_MODEL = 1024
BATCH = 8
SEQ = 8192
DEPTH = 2
DEC_BATCH = 4
DEC_SEQ = 4096
PAST_LEN = 128

GRID_W = 64
Q_BLOCK = 128
ROPE_THETA = 10000.0
EPS = 1e-6

D_FF = 2816

N_BRANCH = 3
BRANCH_W = 512

GQA_HEADS = 8
GQA_KV_HEADS = 2
GQA_GROUPS = GQA_HEADS // GQA_KV_HEADS
GQA_HEAD_DIM = 64

HY_WIDTH = 512
HY_ORDER = 64
HY_EMB = 33
HY_BANDS = (HY_EMB - 1) // 2
HY_TARGET = 1e-2
HY_FAST = 0.3
HY_SLOW = 1.5
HY_SHORT = 3

MLA_HEADS = 8
MLA_Q_RANK = 256
MLA_KV_RANK = 128
MLA_NOPE = 64
MLA_ROPE = 32
MLA_V = 64

IN_WIDTHS = (
    GQA_HEADS * GQA_HEAD_DIM,
    GQA_KV_HEADS * GQA_HEAD_DIM,
    GQA_KV_HEADS * GQA_HEAD_DIM,
    3 * HY_WIDTH,
    MLA_Q_RANK,
    MLA_KV_RANK,
    MLA_ROPE,
    N_BRANCH * D_MODEL,
)
IN_WIDTH = sum(IN_WIDTHS)
IN_SPLITS = tuple(int(c) for c in np.cumsum(IN_WIDTHS)[:-1])

kernel_name = "hybrid_gqa_hyena_mla_macaron_encoder"


def rmsnorm(x, g):
    x32 = x.astype(jnp.float32)
    y = x32 * lax.rsqrt(jnp.mean(x32 * x32, axis=-1, keepdims=True) + EPS)
    return (y * g.astype(jnp.float32)).astype(x.dtype)


def swiglu(x, w_gate, w_up, w_down):
    return (jax.nn.silu(x @ w_gate) * (x @ w_up)) @ w_down


def axial_rope(L, d_rot):
    rows = L // GRID_W
    row = jnp.repeat(jnp.arange(rows, dtype=jnp.float32), GRID_W)
    col = jnp.tile(jnp.arange(GRID_W, dtype=jnp.float32), rows)
    n_freq = d_rot // 4
    inv = ROPE_THETA ** (-jnp.arange(n_freq, dtype=jnp.float32) / n_freq)
    ang = jnp.concatenate([row[:, None] * inv, col[:, None] * inv], axis=-1)
    return jnp.cos(ang), jnp.sin(ang)


def apply_rope(x, cos, sin):
    xp = x.reshape(x.shape[:-1] + (-1, 2))
    x0, x1 = xp[..., 0], xp[..., 1]
    c = cos[None, :, None, :].astype(x.dtype)
    s = sin[None, :, None, :].astype(x.dtype)
    return jnp.stack([x0 * c - x1 * s, x0 * s + x1 * c], axis=-1).reshape(x.shape)


def block_attention(q, k, v, scale):
    B, L, Hk, G, dk = q.shape
    dv = v.shape[-1]
    nb = L // Q_BLOCK
    qb = jnp.moveaxis(q.reshape(B, nb, Q_BLOCK, Hk, G, dk), 1, 0)

    def attend(qblk):
        s = jnp.einsum("bqkgd,bskd->bkgqs", qblk, k,
                       preferred_element_type=jnp.float32) * scale
        p = jax.nn.softmax(s, axis=-1).astype(v.dtype)
        return jnp.einsum("bkgqs,bskd->bqkgd", p, v)

    o = lax.map(attend, qb)
    return jnp.moveaxis(o, 0, 1).reshape(B, L, Hk, G, dv)


def gqa_branch(q, k, v, g_q, g_k, cos, sin):
    B, L, _ = q.shape
    q = apply_rope(rmsnorm(q.reshape(B, L, GQA_HEADS, GQA_HEAD_DIM), g_q), cos, sin)
    k = apply_rope(rmsnorm(k.reshape(B, L, GQA_KV_HEADS, GQA_HEAD_DIM), g_k), cos, sin)
    v = v.reshape(B, L, GQA_KV_HEADS, GQA_HEAD_DIM)
    q = q.reshape(B, L, GQA_KV_HEADS, GQA_GROUPS, GQA_HEAD_DIM)
    o = block_attention(q, k, v, GQA_HEAD_DIM ** -0.5)
    return o.reshape(B, L, GQA_HEADS * GQA_HEAD_DIM)


def hyena_positions(L):
    t = jnp.linspace(0.0, 1.0, L, dtype=jnp.float32)[:, None]
    w = 2.0 * math.pi * jnp.arange(L, dtype=jnp.float32)[:, None] / L
    f = jnp.linspace(1e-4, HY_BANDS - 1, HY_BANDS, dtype=jnp.float32)[None, :]
    z = jnp.concatenate([t, jnp.cos(f * w), -jnp.sin(f * w)], axis=-1)
    max_decay = math.log(HY_TARGET) / HY_FAST
    min_decay = math.log(HY_TARGET) / HY_SLOW
    deltas = jnp.abs(jnp.linspace(min_decay, max_decay, HY_WIDTH, dtype=jnp.float32))
    window = jnp.exp(-t * deltas[None, :])
    return z, window


def hyena_filter(z, window, w1, b1, w2, b2, w3, freq):
    f32 = jnp.float32
    fr = freq.astype(f32)
    h = jnp.sin(fr * (z @ w1.astype(f32) + b1.astype(f32)))
    h = jnp.sin(fr * (h @ w2.astype(f32) + b2.astype(f32)))
    h = (h @ w3.astype(f32)).reshape(-1, 2, HY_WIDTH) * window[:, None, :]
    fwd, bwd = h[:, 0], h[:, 1]
    return jnp.concatenate([fwd, jnp.zeros((1, HY_WIDTH), f32), bwd[:0:-1]], axis=0)


def hyena_branch(hy, z_pos, window, w_short, b_short, w1, b1, w2, b2, w3, freq, bias):
    B, L, _ = hy.shape
    hp = jnp.pad(hy, ((0, 0), (1, 1), (0, 0)))
    hy = hp[:, :-2] * w_short[0] + hp[:, 1:-1] * w_short[1] + hp[:, 2:] * w_short[2] + b_short
    x0, x1, v = jnp.split(hy, 3, axis=-1)
    kc = hyena_filter(z_pos, window, w1, b1, w2, b2, w3, freq)
    s = (x1 * v).astype(jnp.float32)
    spec = jnp.fft.rfft(s, n=2 * L, axis=1) * jnp.fft.rfft(kc, axis=0)[None]
    y = jnp.fft.irfft(spec, n=2 * L, axis=1)[:, :L] + s * bias.astype(jnp.float32)
    return (x0.astype(jnp.float32) * y).astype(hy.dtype)


def mla_branch(cq, ckv, kr, g_q, w_uq, g_kv, w_ukv, cos, sin):
    B, L, _ = cq.shape
    qh = (rmsnorm(cq, g_q) @ w_uq).reshape(B, L, MLA_HEADS, MLA_NOPE + MLA_ROPE)
    q_nope, q_rope = qh[..., :MLA_NOPE], qh[..., MLA_NOPE:]
    q_rope = apply_rope(q_rope, cos, sin)
    kvh = (rmsnorm(ckv, g_kv) @ w_ukv).reshape(B, L, MLA_HEADS, MLA_NOPE + MLA_V)
    k_nope, vh = kvh[..., :MLA_NOPE], kvh[..., MLA_NOPE:]
    k_rope = apply_rope(kr.reshape(B, L, 1, MLA_ROPE), cos, sin)
    qf = jnp.concatenate([q_nope, q_rope], axis=-1)[:, :, :, None, :]
    kf = jnp.concatenate([k_nope, jnp.broadcast_to(k_rope, (B, L, MLA_HEADS, MLA_ROPE))], axis=-1)
    o = block_attention(qf, kf, vh, (MLA_NOPE + MLA_ROPE) ** -0.5)
    return o.reshape(B, L, MLA_HEADS * MLA_V)


def encoder(x, params):
    (g_ffn1, w_ffn1_gate, w_ffn1_up, w_ffn1_down, g_mix, w_in, g_qnorm, g_knorm,
     w_hy_short, b_hy_short, w_hy_f1, b_hy_f1, w_hy_f2, b_hy_f2, w_hy_f3, hy_sin_freq, hy_bias,
     g_mla_q, w_mla_uq, g_mla_kv, w_mla_ukv, w_branch, w_out,
     g_ffn2, w_ffn2_gate, w_ffn2_up, w_ffn2_down, g_final) = params
    B, L, _ = x.shape
    cos_a, sin_a = axial_rope(L, GQA_HEAD_DIM)
    cos_m, sin_m = axial_rope(L, MLA_ROPE)
    z_pos, window = hyena_positions(L)
    for l in range(DEPTH):
        x = x + 0.5 * swiglu(rmsnorm(x, g_ffn1[l]), w_ffn1_gate[l], w_ffn1_up[l], w_ffn1_down[l])
        u = rmsnorm(x, g_mix[l])
        zc = u @ w_in[l]
        q, k, v, hy, cq, ckv, kr, gl = jnp.split(zc, IN_SPLITS, axis=-1)
        y_a = gqa_branch(q, k, v, g_qnorm[l], g_knorm[l], cos_a, sin_a)
        y_b = hyena_branch(hy, z_pos, window, w_hy_short[l], b_hy_short[l], w_hy_f1[l], b_hy_f1[l],
                           w_hy_f2[l], b_hy_f2[l], w_hy_f3[l], hy_sin_freq[l], hy_bias[l])
        y_c = mla_branch(cq, ckv, kr, g_mla_q[l], w_mla_uq[l], g_mla_kv[l], w_mla_ukv[l], cos_m, sin_m)
        gates = jax.nn.sigmoid(gl.reshape(B, L, N_BRANCH, D_MODEL))
        merged = (gates[:, :, 0] * (y_a @ w_branch[l, 0])
                  + gates[:, :, 1] * (y_b @ w_branch[l, 1])
                  + gates[:, :, 2] * (y_c @ w_branch[l, 2]))
        x = x + merged @ w_out[l]
        x = x + 0.5 * swiglu(rmsnorm(x, g_ffn2[l]), w_ffn2_gate[l], w_ffn2_up[l], w_ffn2_down[l])
    return rmsnorm(x, g_final)


def setup_inputs(seed: int = 0) -> dict:
    key = jax.random.key(seed)
    ks = jax.random.split(key, 40)
    cnt = [0]

    def nxt():
        cnt[0] += 1
        return ks[cnt[0] - 1]

    def nrm(shape, scale):
        return scale * jax.random.normal(nxt(), shape, jnp.float32)

    def gain(shape):
        return 1.0 + 0.01 * jax.random.normal(nxt(), shape, jnp.float32)

    D = D_MODEL
    return {
        "x_prompt": nrm((BATCH, SEQ, D), 1.0),
        "x_sample": nrm((DEC_BATCH, DEC_SEQ, D), 1.0),
        "g_ffn1": gain((DEPTH, D)),
        "w_ffn1_gate": nrm((DEPTH, D, D_FF), D ** -0.5),
        "w_ffn1_up": nrm((DEPTH, D, D_FF), D ** -0.5),
        "w_ffn1_down": nrm((DEPTH, D_FF, D), D_FF ** -0.5),
        "g_mix": gain((DEPTH, D)),
        "w_in": nrm((DEPTH, D, IN_WIDTH), D ** -0.5),
        "g_qnorm": gain((DEPTH, GQA_HEAD_DIM)),
        "g_knorm": gain((DEPTH, GQA_HEAD_DIM)),
        "w_hy_short": nrm((DEPTH, HY_SHORT, 3 * HY_WIDTH), HY_SHORT ** -0.5),
        "b_hy_short": nrm((DEPTH, 3 * HY_WIDTH), 0.02),
        "w_hy_f1": nrm((DEPTH, HY_EMB, HY_ORDER), HY_EMB ** -0.5),
        "b_hy_f1": nrm((DEPTH, HY_ORDER), 0.02),
        "w_hy_f2": nrm((DEPTH, HY_ORDER, HY_ORDER), HY_ORDER ** -0.5),
        "b_hy_f2": nrm((DEPTH, HY_ORDER), 0.02),
        "w_hy_f3": nrm((DEPTH, HY_ORDER, 2 * HY_WIDTH), 0.02),
        "hy_sin_freq": gain((DEPTH, HY_ORDER)),
        "hy_bias": nrm((DEPTH, HY_WIDTH), 0.5),
        "g_mla_q": gain((DEPTH, MLA_Q_RANK)),
        "w_mla_uq": nrm((DEPTH, MLA_Q_RANK, MLA_HEADS * (MLA_NOPE + MLA_ROPE)), MLA_Q_RANK ** -0.5),
        "g_mla_kv": gain((DEPTH, MLA_KV_RANK)),
        "w_mla_ukv": nrm((DEPTH, MLA_KV_RANK, MLA_HEADS * (MLA_NOPE + MLA_V)), MLA_KV_RANK ** -0.5),
        "w_branch": nrm((DEPTH, N_BRANCH, BRANCH_W, D), BRANCH_W ** -0.5),
        "w_out": nrm((DEPTH, D, D), D ** -0.5),
        "g_ffn2": gain((DEPTH, D)),
        "w_ffn2_gate": nrm((DEPTH, D, D_FF), D ** -0.5),
        "w_ffn2_up": nrm((DEPTH, D, D_FF), D ** -0.5),
        "w_ffn2_down": nrm((DEPTH, D_FF, D), D_FF ** -0.5),
        "g_final": gain((D,)),
    }


def reference(x_prompt, x_sample, g_ffn1, w_ffn1_gate, w_ffn1_up, w_ffn1_down, g_mix, w_in,
              g_qnorm, g_knorm, w_hy_short, b_hy_short, w_hy_f1, b_hy_f1, w_hy_f2, b_hy_f2,
              w_hy_f3, hy_sin_freq, hy_bias, g_mla_q, w_mla_uq, g_mla_kv, w_mla_ukv,
              w_branch, w_out, g_ffn2, w_ffn2_gate, w_ffn2_up, w_ffn2_down, g_final):
    params = (g_ffn1, w_ffn1_gate, w_ffn1_up, w_ffn1_down, g_mix, w_in, g_qnorm, g_knorm,
              w_hy_short, b_hy_short, w_hy_f1, b_hy_f1, w_hy_f2, b_hy_f2, w_hy_f3, hy_sin_freq,
              hy_bias, g_mla_q, w_mla_uq, g_mla_kv, w_mla_ukv, w_branch, w_out,
              g_ffn2, w_ffn2_gate, w_ffn2_up, w_ffn2_down, g_final)
    y_prompt = encoder(x_prompt, params)
    y_sample = encoder(x_sample, params)
    return (y_prompt, y_sample)
```

```python
import math
import numpy as np
import concourse.bass as bass
import concourse.mybir as mybir
from concourse.bass_utils import run_bass_kernel_spmd

F32 = mybir.dt.float32
BF16 = mybir.dt.bfloat16
ALU = mybir.AluOpType
AF = mybir.ActivationFunctionType
AX = mybir.AxisListType

ENGINES = ("pe", "dve", "act", "pool", "sp")


class Buf:
    __slots__ = ("name", "writers", "readers", "dsem", "depoch", "dlast")

    def __init__(self, name):
        self.name = name
        self.writers = []
        self.readers = []
        self.dsem = None
        self.depoch = -1
        self.dlast = None


class Op:
    __slots__ = ("eng", "fn", "deps", "is_dma", "semkey", "val", "signal", "waits", "snap")

    def __init__(self, eng, fn, is_dma=False):
        self.eng = eng
        self.fn = fn
        self.deps = set()
        self.is_dma = is_dma
        self.semkey = None
        self.val = 0
        self.signal = False
        self.waits = []
        self.snap = None


class Sched:
    def __init__(self, nc):
        self.nc = nc
        self.ops = []
        self.last = {e: None for e in ENGINES}
        self.slot_count = []
        self.slot_free = []
        self.slot_used = []
        self.epoch = 0
        self.dma_since_barrier = []
        self.barbuf = Buf("BAR")

    def _track(self, op, reads, writes):
        for b in reads:
            op.deps.update(b.writers)
        for b in writes:
            op.deps.update(b.writers)
            op.deps.update(b.readers)
        for b in reads:
            b.readers.append(op)
        for b in writes:
            b.writers = [op]
            b.readers = []
        op.deps.discard(op)

    def op(self, eng, fn, reads=(), writes=()):
        o = Op(eng, fn)
        self._track(o, reads, writes)
        self.ops.append(o)
        if fn is not None:
            self.last[eng] = o
        return o

    def dma(self, eng, fn, sembuf, n, reads=(), writes=()):
        o = Op(eng, fn, is_dma=True)
        self._track(o, reads, writes)
        if sembuf.depoch != self.epoch:
            if self.slot_free:
                sembuf.dsem = self.slot_free.pop()
            else:
                sembuf.dsem = len(self.slot_count)
                self.slot_count.append(0)
            self.slot_used.append(sembuf.dsem)
            sembuf.depoch = self.epoch
            sembuf.dlast = None
        if sembuf.dlast is not None:
            o.deps.add(sembuf.dlast)
        self.slot_count[sembuf.dsem] += 16 * n
        sembuf.dlast = o
        o.semkey = ("D", sembuf.dsem)
        o.val = self.slot_count[sembuf.dsem]
        o.signal = True
        self.ops.append(o)
        self.dma_since_barrier.append(o)
        return o

    def barrier(self, memset_fn):
        x = Op("pool", memset_fn)
        for e in ENGINES:
            if self.last[e] is not None:
                x.deps.add(self.last[e])
        x.deps.update(self.dma_since_barrier)
        self.dma_since_barrier = []
        self.slot_free.extend(self.slot_used)
        self.slot_used = []
        self.epoch += 1
        self.ops.append(x)
        self.last["pool"] = x
        for e in ENGINES:
            if e == "pool":
                continue
            o = Op(e, None)
            o.deps.add(x)
            self.ops.append(o)
        return x

    def finalize(self, stack):
        nc = self.nc
        for o in self.ops:
            for d in o.deps:
                d.signal = True
        cnt = {e: 0 for e in ENGINES}
        known = {e: {} for e in ENGINES}
        snaps = {}
        for o in self.ops:
            E = o.eng
            k = known[E]
            waits = {}
            for d in o.deps:
                if d.is_dma:
                    key, val = d.semkey, d.val
                else:
                    if d.eng == "pe" and E == "pe":
                        continue
                    key, val = ("E", d.eng), d.val
                if k.get(key, 0) >= val:
                    continue
                if waits.get(key, 0) < val:
                    waits[key] = val
            for key, val in waits.items():
                if k.get(key, 0) < val:
                    k[key] = val
                sn = snaps.get((key, val))
                if sn is not None:
                    for k2, v2 in sn.items():
                        if k.get(k2, 0) < v2:
                            k[k2] = v2
            o.waits = list(waits.items())
            if o.is_dma:
                snaps[(o.semkey, o.val)] = dict(k)
            elif o.signal and o.fn is not None:
                cnt[E] += 1
                o.val = cnt[E]
                snaps[(("E", E), o.val)] = dict(k)
            elif o.signal and o.fn is None:
                raise RuntimeError("wait-only op cannot signal")
        self.esem = {e: stack.enter_context(nc.semaphore("s_" + e)) for e in ENGINES}
        self.dsem = [stack.enter_context(nc.semaphore("d%d" % i)) for i in range(len(self.slot_count))]
        self.stats = dict(cnt)

    def _sem(self, key):
        return self.esem[key[1]] if key[0] == "E" else self.dsem[key[1]]

    def emit(self, block):
        per = {e: [o for o in self.ops if o.eng == e] for e in ENGINES}

        def run(engobj, E):
            for o in per[E]:
                for key, val in o.waits:
                    engobj.wait_ge(self._sem(key), val)
                if o.fn is None:
                    continue
                if o.is_dma:
                    o.fn(engobj, self._sem(o.semkey))
                else:
                    ins = o.fn(engobj)
                    if o.signal:
                        ins.then_inc(self.esem[E], 1)

        @block.tensor
        def _(t):
            run(t, "pe")

        @block.vector
        def _(v):
            run(v, "dve")

        @block.scalar
        def _(s):
            run(s, "act")

        @block.gpsimd
        def _(g):
            run(g, "pool")

        @block.sync
        def _(s):
            run(s, "sp")


class Tile:
    __slots__ = ("t", "b", "shape")

    def __init__(self, t, b, shape):
        self.t = t
        self.b = b
        self.shape = shape

    def __getitem__(self, k):
        return self.t[k]


def dsize(dt):
    return mybir.dt.size(dt)


class Ctx:
    def __init__(self, nc):
        self.nc = nc
        self.S = Sched(nc)
        self.uid = 0
        self.base = 16640
        self.cur = 16640
        self.cap = 229376
        self.psum = []
        self.ps_pairs = {}
        for i in range(4):
            t2 = nc.alloc_psum_tensor("psd%d" % i, [128, 1024], F32)
            for h in range(2):
                nm = "ps%d" % (2 * i + h)
                self.psum.append(Tile(t2[:, h * 512:(h + 1) * 512], Buf(nm), [128, 512]))
            self.ps_pairs["ps%d" % (2 * i)] = t2[:, :]
        self.ps_rr = 0

    def alloc(self, shape, dtype, name="t"):
        per = dsize(dtype)
        for s in shape[1:]:
            per *= s
        per = (per + 31) // 32 * 32
        assert self.cur + per <= self.cap, ("SBUF overflow", name, self.cur, per)
        self.uid += 1
        nm = "%s_%d" % (name, self.uid)
        t = self.nc.alloc_sbuf_tensor_at(nm, list(shape), dtype, offset=self.cur)
        self.cur += per
        return Tile(t, Buf(nm), list(shape))

    def persist(self):
        self.base = self.cur

    def phase_reset(self):
        self.cur = self.base

    def ps(self):
        allowed = getattr(self, "ps_allowed", None)
        if allowed:
            p = self.psum[allowed[self.ps_rr % len(allowed)]]
        else:
            p = self.psum[self.ps_rr % 8]
        self.ps_rr += 1
        return p

    def mm(self, out_ap, pairs, reads, writes):
        pairs = list(pairs)

        def fn(e):
            n = len(pairs)
            ins = None
            for i, (l, r) in enumerate(pairs):
                ins = e.matmul(out_ap, l, r, start=(i == 0), stop=(i == n - 1))
            return ins
        return self.S.op("pe", fn, reads, writes)

    def mm_acc(self, out_ap, lhsT, rhs, start, stop, reads, writes):
        def fn(e):
            return e.matmul(out_ap, lhsT, rhs, start=start, stop=stop)
        return self.S.op("pe", fn, reads, writes)

    def ps_pair_ap(self, pa):
        return self.ps_pairs[pa.b.name]

    def transposes(self, items, ident_ap, reads, writes):
        items = list(items)

        def fn(e):
            ins = None
            for o, i in items:
                ins = e.transpose(o, i, ident_ap)
            return ins
        return self.S.op("pe", fn, reads, writes)

    def act(self, out_ap, in_ap, func, reads, writes, bias=0.0, scale=1.0, eng="act"):
        def fn(e):
            return e.activation(out=out_ap, in_=in_ap, func=func, bias=bias, scale=scale)
        return self.S.op("act", fn, reads, writes)

    def tt(self, eng, out_ap, a_ap, b_ap, op, reads, writes):
        def fn(e):
            return e.tensor_tensor(out=out_ap, in0=a_ap, in1=b_ap, op=op)
        return self.S.op(eng, fn, reads, writes)

    def ts(self, eng, out_ap, a_ap, s1, s2, op0, op1, reads, writes):
        def fn(e):
            if op1 is None:
                return e.tensor_scalar(out=out_ap, in0=a_ap, scalar1=s1, scalar2=None, op0=op0)
            return e.tensor_scalar(out=out_ap, in0=a_ap, scalar1=s1, scalar2=s2, op0=op0, op1=op1)
        return self.S.op(eng, fn, reads, writes)

    def stt(self, eng, out_ap, a_ap, s, b_ap, op0, op1, reads, writes):
        def fn(e):
            return e.scalar_tensor_tensor(out=out_ap, in0=a_ap, scalar=s, in1=b_ap, op0=op0, op1=op1)
        return self.S.op(eng, fn, reads, writes)

    def copy(self, eng, out_ap, in_ap, reads, writes):
        if eng == "act":
            def fn(e):
                return e.activation(out=out_ap, in_=in_ap, func=AF.Copy)
        else:
            def fn(e):
                return e.tensor_copy(out=out_ap, in_=in_ap)
        return self.S.op(eng, fn, reads, writes)

    def recip(self, eng, out_ap, in_ap, reads, writes):
        def fn(e):
            return e.reciprocal(out=out_ap, in_=in_ap)
        return self.S.op(eng, fn, reads, writes)

    def memset(self, eng, ap, val, writes):
        def fn(e):
            return e.memset(ap, val)
        return self.S.op(eng, fn, (), writes)

    def dma(self, q, pairs, sembuf, reads=(), writes=(), slow=False):
        pairs = list(pairs)

        def fn(e, sem):
            for o, i in pairs:
                if slow:
                    e.dma_start(out=o, in_=i, allow_slow_non_contiguous=True).then_inc(sem, 16)
                else:
                    e.dma_start(out=o, in_=i).then_inc(sem, 16)
        return self.S.dma(q, fn, sembuf, len(pairs), reads, writes)

    def barrier(self):
        bt = self.bar_tile

        def fn(e):
            return e.memset(bt[:, :], 0.0)
        self.S.barrier(fn)


D = 1024
KC = 8
DFF = 2816
FC = 22
EPS = 1e-6
TT = 256
IN_W = 5792
GRID_W = 64


class WLoader:
    def __init__(self, cx, ncols=1024, nst=3):
        self.cx = cx
        self.ncols = ncols
        self.st = [cx.alloc([128, ncols], F32, "wst") for _ in range(nst)]
        self.i = 0
        self.engs = ("dve", "pool", "act")
        self.parts = []
        self.jt = cx.alloc([128, 8], F32, "wjoin")

    def join(self, targets):
        jt = self.jt

        def fn(e):
            return e.memset(jt[:, :], 0.0)
        self.cx.S.op("dve", fn, list(self.parts), list(targets) + [jt.b])
        self.parts = []

    def load(self, dst_ap, dst_buf_writes, src_ap, rows, cols, scale_ap=None, scale_reads=(), eng=None, negate=False):
        cx = self.cx
        st = self.st[self.i % len(self.st)]
        if eng is None:
            eng = self.engs[self.i % 3]
        self.i += 1
        cx.dma("sp", [(st[0:rows, 0:cols], src_ap)], st.b, writes=[st.b])
        src = st[0:rows, 0:cols]
        pb = Buf("wpart")
        self.parts.append(pb)
        dst_buf_writes = [pb]
        if scale_ap is not None:
            if eng == "act":
                def fn(e):
                    return e.activation(out=dst_ap, in_=src, func=AF.Copy, scale=scale_ap)
            elif eng == "pool":
                shp = list(dst_ap.shape)

                def fn(e):
                    return e.tensor_tensor(out=dst_ap, in0=src, in1=scale_ap.to_broadcast(shp), op=ALU.mult)
            else:
                def fn(e):
                    return e.tensor_scalar(out=dst_ap, in0=src, scalar1=scale_ap, scalar2=None, op0=ALU.mult)
            cx.S.op(eng, fn, [st.b] + list(scale_reads), dst_buf_writes)
        else:
            cx.copy(eng, dst_ap, src, [st.b], dst_buf_writes)
        return st


def rms_stats(cx, x_ap_flat, x_buf, sq, ones_ap, const_reads, nk, T, rstd, dim, sq_view):
    cx.tt("pool", sq_view(None), x_ap_flat, x_ap_flat, ALU.mult, [x_buf], [sq.b])
    ps = cx.ps()
    cx.mm(ps[:, 0:T], [(ones_ap, sq_view(k)) for k in range(nk)], [sq.b] + list(const_reads), [ps.b])
    rsqrt_from_psum(cx, rstd[:, 0:T], rstd.b, ps[:, 0:T], ps.b, float(dim))


def rsqrt_from_psum(cx, out_ap, out_b, ps_ap, ps_b, dim):
    npart = out_ap.shape[0]
    cx.act(out_ap, ps_ap, AF.Sqrt, [ps_b, cx.consts["cbuf"]], [out_b], bias=cx.consts["eps"][0:npart, 0:1], scale=1.0 / dim)
    cx.recip("dve", out_ap, out_ap, [out_b], [out_b])


def ffn_pass(cx, seqs, g_ap, wg_ap, wu_ap, wd_ap, consts, src_tok=False, final=None):
    T = TT
    cx.phase_reset()
    wg = cx.alloc([128, KC, DFF], BF16, "wg")
    wu = cx.alloc([128, KC, DFF], BF16, "wu")
    wd = cx.alloc([128, FC, D], BF16, "wd")
    gt = cx.alloc([128, KC], F32, "g")
    xT = [cx.alloc([128, KC, T], F32, "xT") for _ in range(2)]
    uu = [cx.alloc([128, KC, T], BF16, "u") for _ in range(2)]
    sq = cx.alloc([128, KC, T], BF16, "sq")
    rstd = cx.alloc([128, T], F32, "rstd")
    hT = cx.alloc([128, FC, T], BF16, "hT")
    sil = [cx.alloc([128, T], F32, "sil") for _ in range(2)]
    need_tok = src_tok or (final is not None)
    src = dst = None
    if need_tok:
        xtok = [cx.alloc([128, D], F32, "xtok") for _ in range(2)]
    if final is not None:
        gf = cx.alloc([128, KC], F32, "gf")
        xn = cx.alloc([128, KC, T], F32, "xn")
    wl = WLoader(cx)
    ones_ap = consts["ones"][:, :]
    ident_ap = consts["ident"][:, :]
    cb = [consts["cbuf"]]

    cx.dma("sp", [(gt[:, :], g_ap.rearrange("(k p) -> p k", p=128))], gt.b, writes=[gt.b], slow=True)
    if final is not None:
        cx.dma("sp", [(gf[:, :], final.rearrange("(k p) -> p k", p=128))], gf.b, writes=[gf.b], slow=True)
    for k in range(KC):
        for c0 in range(0, DFF, 1024):
            cw = min(1024, DFF - c0)
            wl.load(wg[:, k, c0:c0 + cw], [wg.b], wg_ap[k * 128:(k + 1) * 128, c0:c0 + cw], 128, cw,
                    scale_ap=gt[:, k:k + 1], scale_reads=[gt.b])
            wl.load(wu[:, k, c0:c0 + cw], [wu.b], wu_ap[k * 128:(k + 1) * 128, c0:c0 + cw], 128, cw,
                    scale_ap=gt[:, k:k + 1], scale_reads=[gt.b])
    for f in range(FC):
        wl.load(wd[:, f, :], [wd.b], wd_ap[f * 128:(f + 1) * 128, :], 128, D)
    wl.join([wg.b, wu.b, wd.b])

    def run_seq(L, src, dst, out_tok):
        nch = L // T
        srcv = None if src_tok else src.rearrange("(k p) l -> p k l", p=128)
        dstv = None if dst is None else dst.rearrange("(k p) l -> p k l", p=128)

        def load_chunk(c):
            xs = xT[c % 2]
            t0 = c * T
            if not src_tok:
                cx.dma("sp", [(xs[:, :, :], srcv[:, :, t0:t0 + T])], xs.b, writes=[xs.b])
            else:
                for j in range(T // 128):
                    xk = xtok[j % 2]
                    cx.dma("sp", [(xk[:, :], src[t0 + j * 128:t0 + (j + 1) * 128, :])], xk.b, writes=[xk.b])
                for kk in range(KC // 2):
                    ps = cx.ps()
                    items = []
                    for k2 in range(2):
                        k = kk * 2 + k2
                        for j in range(T // 128):
                            items.append((ps[:, k2 * T + j * 128:k2 * T + (j + 1) * 128],
                                          xtok[j % 2][:, k * 128:(k + 1) * 128]))
                    cx.transposes(items, ident_ap, [xtok[0].b, xtok[1].b] + cb, [ps.b])
                    cx.copy("act", xs[:, kk * 2:kk * 2 + 2, :], ps[:, 0:2 * T].rearrange("p (a t) -> p a t", a=2),
                            [ps.b], [xs.b])

        def norm(c):
            xs_ = xT[c % 2]
            u_ = uu[c % 2]
            rms_stats(cx, xs_[:, :, :], xs_.b, sq, ones_ap, cb, KC, T, rstd, float(D),
                      lambda k: sq[:, :, :] if k is None else sq[:, k, :])
            cx.tt("dve", u_[:, :, :], xs_[:, :, :], rstd[:, 0:T].unsqueeze(1).to_broadcast([128, KC, T]),
                  ALU.mult, [xs_.b, rstd.b], [u_.b])

        load_chunk(0)
        norm(0)
        for c in range(nch):
            xs = xT[c % 2]
            u = uu[c % 2]
            t0 = c * T
            if c + 1 < nch:
                load_chunk(c + 1)
            for f in range(FC):
                pg = cx.ps()
                cx.mm(pg[:, 0:T], [(wg[:, k, f * 128:(f + 1) * 128], u[:, k, :]) for k in range(KC)],
                      [wg.b, u.b], [pg.b])
                pu = cx.ps()
                cx.mm(pu[:, 0:T], [(wu[:, k, f * 128:(f + 1) * 128], u[:, k, :]) for k in range(KC)],
                      [wu.b, u.b], [pu.b])
                s = sil[f % 2]
                cx.act(s[:, :], pg[:, 0:T], AF.Silu, [pg.b], [s.b])
                cx.tt("dve", hT[:, f, :], s[:, :], pu[:, 0:T], ALU.mult, [s.b, pu.b], [hT.b])
                if f == FC // 2 and c + 1 < nch:
                    norm(c + 1)
            for d in range(KC):
                pd = cx.ps()
                cx.mm(pd[:, 0:T], [(wd[:, f, d * 128:(d + 1) * 128], hT[:, f, :]) for f in range(FC)],
                      [wd.b, hT.b], [pd.b])
                cx.stt("dve", xs[:, d, :], pd[:, 0:T], 0.5, xs[:, d, :], ALU.mult, ALU.add, [pd.b, xs.b], [xs.b])
            if final is None:
                cx.dma("sp", [(dstv[:, :, t0:t0 + T], xs[:, :, :])], xs.b, reads=[xs.b])
            else:
                rms_stats(cx, xs[:, :, :], xs.b, sq, ones_ap, cb, KC, T, rstd, float(D),
                          lambda k: sq[:, :, :] if k is None else sq[:, k, :])
                for k in range(KC):
                    cx.stt("dve", xn[:, k, :], xs[:, k, :], gf[:, k:k + 1], rstd[:, 0:T], ALU.mult, ALU.mult,
                           [xs.b, gf.b, rstd.b], [xn.b])
                for j in range(T // 128):
                    xk = xtok[j % 2]
                    for hb in range(2):
                        ps = cx.ps()
                        items = [(ps[:, q * 128:(q + 1) * 128], xn[:, hb * 4 + q, j * 128:(j + 1) * 128])
                                 for q in range(4)]
                        cx.transposes(items, ident_ap, [xn.b] + cb, [ps.b])
                        cx.copy("act", xk[:, hb * 512:(hb + 1) * 512], ps[:, :], [ps.b], [xk.b])
                    cx.dma("sp", [(out_tok[t0 + j * 128:t0 + (j + 1) * 128, :], xk[:, :])], xk.b, reads=[xk.b])

    for (L_, src_, dst_, out_) in seqs:
        run_seq(L_, src_, dst_, out_)
    cx.barrier()


def setup_consts(cx, ident_dram):
    c = {}
    ones = cx.alloc([128, 128], BF16, "ones")
    ident = cx.alloc([128, 128], F32, "ident")
    blk = cx.alloc([128, 128], BF16, "blk64")
    bar = cx.alloc([128, 8], F32, "bar")
    mhalf = cx.alloc([128, 8], F32, "mhalf")
    c["mhalf"] = mhalf
    epst = cx.alloc([128, 8], F32, "eps")
    c["eps"] = epst
    cx.consts = c
    cx.bar_tile = bar
    cbuf = Buf("consts")
    cx.memset("pool", ones[:, :], 1.0, [cbuf])
    cx.memset("pool", blk[:, :], 0.0, [cbuf])
    cx.memset("pool", mhalf[:, :], -0.5, [cbuf])
    cx.memset("pool", epst[:, :], EPS, [cbuf])
    cx.memset("pool", blk[0:64, 0:64], 1.0, [cbuf])
    cx.memset("pool", blk[64:128, 64:128], 1.0, [cbuf])
    cx.dma("sp", [(ident[:, :], ident_dram)], ident.b, writes=[cbuf])
    c["ones"] = ones
    c["ident"] = ident
    c["blk64"] = blk
    c["cbuf"] = cbuf
    cx.persist()
    return c


def small_vec_load(cx, dst_tile, src_ap, writes):
    cx.dma("sp", [(dst_tile, src_ap)], writes[0], writes=writes, slow=True)


def inproj_pass(cx, seqs, l, W, consts):
    T = TT
    TH = T + 2
    cx.phase_reset()
    NA = 2720
    wA = cx.alloc([128, KC, NA], BF16, "wA")
    wR = cx.alloc([128, KC, 672], BF16, "wR")
    wuq = cx.alloc([128, 2, 768], BF16, "wuq")
    wuqr = cx.alloc([128, 2, 768], BF16, "wuqr")
    wukv = cx.alloc([128, 1024], BF16, "wukv")
    gt = cx.alloc([128, KC], F32, "gmix")
    gsm = cx.alloc([128, 8], F32, "gsm")
    hw = cx.alloc([128, 12, 4], F32, "hw")
    wl = WLoader(cx)
    cb = [consts["cbuf"]]
    ones_ap = consts["ones"][:, :]
    blk_ap = consts["blk64"][:, :]
    wb = Buf("p2w")

    w_in = W["w_in"][l]
    cx.dma("sp", [(gt[:, :], W["g_mix"][l].rearrange("(k p) -> p k", p=128))], gt.b, writes=[gt.b], slow=True)
    gq = W["g_qnorm"][l]
    gk = W["g_knorm"][l]
    pairs = []
    for half in range(2):
        p0 = half * 64
        pairs.append((gsm[p0:p0 + 64, 0:1], gq.rearrange("(p o) -> p o", o=1)))
        pairs.append((gsm[p0:p0 + 64, 2:3], gk.rearrange("(p o) -> p o", o=1)))
        gq2 = gq.rearrange("(i two) -> i two", two=2)
        gk2 = gk.rearrange("(i two) -> i two", two=2)
        pairs.append((gsm[p0:p0 + 64:2, 1:2], gq2[:, 1:2]))
        pairs.append((gsm[p0 + 1:p0 + 64:2, 1:2], gq2[:, 0:1]))
        pairs.append((gsm[p0:p0 + 64:2, 3:4], gk2[:, 1:2]))
        pairs.append((gsm[p0 + 1:p0 + 64:2, 3:4], gk2[:, 0:1]))
    pairs.append((gsm[:, 4:6], W["g_mla_q"][l].rearrange("(k p) -> p k", p=128)))
    pairs.append((gsm[:, 6:7], W["g_mla_kv"][l].rearrange("(p o) -> p o", o=1)))
    cx.dma("sp", pairs, gsm.b, writes=[gsm.b], slow=True)
    hp = [(hw[:, :, j], W["w_hy_short"][l][j].rearrange("(c p) -> p c", p=128)) for j in range(3)]
    hp.append((hw[:, :, 3], W["b_hy_short"][l].rearrange("(c p) -> p c", p=128)))
    cx.dma("sp", hp, hw.b, writes=[hw.b], slow=True)

    for k in range(KC):
        for c0 in range(0, NA, 1024):
            cw = min(1024, NA - c0)
            wl.load(wA[:, k, c0:c0 + cw], [wb], w_in[k * 128:(k + 1) * 128, c0:c0 + cw], 128, cw,
                    scale_ap=gt[:, k:k + 1], scale_reads=[gt.b])
    wl.join([wb])
    for k in range(KC):
        for (s0, n, d0) in ((0, 640, 0), (2688, 32, 640)):
            src = wA[:, k, s0:s0 + n].rearrange("p (i two) -> p i two", two=2)
            dst = wR[:, k, d0:d0 + n].rearrange("p (i two) -> p i two", two=2)
            cx.ts("dve", dst[:, :, 0], src[:, :, 1], -1.0, None, ALU.mult, None, [wb], [wb])
            cx.copy("dve", dst[:, :, 1], src[:, :, 0], [wb], [wb])
    for i in range(2):
        wl.load(wuq[:, i, :], [wb], W["w_mla_uq"][l][i * 128:(i + 1) * 128, :], 128, 768,
                scale_ap=gsm[:, 4 + i:5 + i], scale_reads=[gsm.b])
        wl.join([wb])
        cx.copy("dve", wuqr[:, i, :], wuq[:, i, :], [wb], [wb])
        for h in range(8):
            src = wuq[:, i, h * 96 + 64:h * 96 + 96].rearrange("p (i two) -> p i two", two=2)
            dst = wuqr[:, i, h * 96 + 64:h * 96 + 96].rearrange("p (i two) -> p i two", two=2)
            cx.ts("dve", dst[:, :, 0], src[:, :, 1], -1.0, None, ALU.mult, None, [wb], [wb])
            cx.copy("dve", dst[:, :, 1], src[:, :, 0], [wb], [wb])
    wl.load(wukv[:, :], [wb], W["w_mla_ukv"][l][:, :], 128, 1024, scale_ap=gsm[:, 6:7], scale_reads=[gsm.b])
    wl.join([wb])
    wukv_v = wukv[:, :].rearrange("p (h x) -> p h x", x=128)[:, :, 64:128]

    xh = [cx.alloc([128, KC, TH], F32, "xh") for _ in range(2)]
    us = [cx.alloc([128, KC, TH], BF16, "u") for _ in range(2)]
    sqs = [cx.alloc([128, KC, TH], BF16, "sq") for _ in range(2)]
    rstds = [cx.alloc([128, TH], F32, "rstd") for _ in range(2)]
    sq = cx.alloc([128, 2, T], BF16, "sqc")
    rstd = cx.alloc([128, T], F32, "rstdc")
    sqk = cx.alloc([128, T], BF16, "sqk")
    rstdk = cx.alloc([128, T], F32, "rstdk")
    tabA = [cx.alloc([128, 2, T], F32, "tabA") for _ in range(2)]
    tabM = [cx.alloc([96, 2, T], F32, "tabM") for _ in range(2)]
    tabK = [cx.alloc([32, 2, T], F32, "tabK") for _ in range(2)]
    qs = [cx.alloc([128, T], F32, "qs") for _ in range(4)]
    sq2 = [cx.alloc([128, T], BF16, "sq2") for _ in range(4)]
    r2 = [cx.alloc([128, T], F32, "r2") for _ in range(4)]
    ta = [cx.alloc([128, T], F32, "ta") for _ in range(4)]
    tb = [cx.alloc([128, T], F32, "tb") for _ in range(4)]
    qst = [cx.alloc([128, T], BF16, "qst") for _ in range(4)]
    vst = [cx.alloc([128, 2, 128], BF16, "vst") for _ in range(2)]
    hst = [cx.alloc([128, T], F32, "hst") for _ in range(3)]
    x1k = cx.alloc([128, 4, T], F32, "x1k")
    sst = [cx.alloc([128, T], BF16, "sst") for _ in range(2)]
    cqs = cx.alloc([128, 2, T], F32, "cqs")
    cqn = cx.alloc([128, 2, T], BF16, "cqn")
    ckvs = cx.alloc([128, T], F32, "ckvs")
    ckvn = cx.alloc([128, T], BF16, "ckvn")
    kro = [cx.alloc([32, T], BF16, "kro") for _ in range(2)]
    kst = [cx.alloc([64, T], BF16, "kst") for _ in range(3)]
    vmst = [cx.alloc([128, 8, 128], BF16, "vmst") for _ in range(2)]
    for t_ in vst:
        cx.memset("pool", t_[:, :, 64:128], 1.0, [t_.b])
    for t_ in vmst:
        cx.memset("pool", t_[:, :, 64:128], 1.0, [t_.b])

    def run_seq(L, sq_):
        xav = sq_["xa"].rearrange("(k p) l -> p k l", p=128)
        nch = L // T
        cnt = {"q": 0, "h": 0, "s": 0, "k": 0}

        def load_chunk(c):
            xs = xh[c % 2]
            t0 = c * T
            lo = max(t0 - 1, 0)
            hi = min(t0 + T + 1, L)
            d0 = lo - (t0 - 1)
            if c == 0:
                cx.memset("pool", xs[:, :, 0:1], 0.0, [xs.b])
            if c == nch - 1:
                cx.memset("pool", xs[:, :, TH - 1:TH], 0.0, [xs.b])
            cx.dma("sp", [(xs[:, :, d0:d0 + (hi - lo)], xav[:, :, lo:hi])], xs.b, writes=[xs.b])
            ta_ = tabA[c % 2]
            cx.dma("sp", [(ta_[:, 0, :], sq_["ropeA_cos"][:, t0:t0 + T]), (ta_[:, 1, :], sq_["ropeA_sin"][:, t0:t0 + T])],
                   ta_.b, writes=[ta_.b])
            tm_ = tabM[c % 2]
            cx.dma("sp", [(tm_[:, 0, :], sq_["ropeM_cos"][:, t0:t0 + T]), (tm_[:, 1, :], sq_["ropeM_sin"][:, t0:t0 + T])],
                   tm_.b, writes=[tm_.b])
            tk_ = tabK[c % 2]
            cx.dma("sp", [(tk_[:, 0, :], sq_["ropeM_cos"][64:96, t0:t0 + T]), (tk_[:, 1, :], sq_["ropeM_sin"][64:96, t0:t0 + T])],
                   tk_.b, writes=[tk_.b])

        def norm(c):
            xs_ = xh[c % 2]
            sq_n = sqs[c % 2]
            rs_n = rstds[c % 2]
            u_n = us[c % 2]
            rms_stats(cx, xs_[:, :, :], xs_.b, sq_n, ones_ap, cb, KC, TH, rs_n, float(D),
                      lambda k: sq_n[:, :, :] if k is None else sq_n[:, k, :])
            cx.tt("dve", u_n[:, :, :], xs_[:, :, :], rs_n[:, 0:TH].unsqueeze(1).to_broadcast([128, KC, TH]),
                  ALU.mult, [xs_.b, rs_n.b], [u_n.b])

        load_chunk(0)
        norm(0)
        for c in range(nch):
            xs = xh[c % 2]
            u = us[c % 2]
            t0 = c * T
            if c + 1 < nch:
                load_chunk(c + 1)
            tA = tabA[c % 2]
            tM = tabM[c % 2]
            tK = tabK[c % 2]

            def uc(k):
                return u[:, k, 1:T + 1]

            st_q = {}

            def qk_a(ci):
                col0 = ci * 128
                i2 = (cnt["q"] + ci) % 4
                pq = cx.ps()
                cx.mm(pq[:, 0:T], [(wA[:, k, col0:col0 + 128], uc(k)) for k in range(KC)], [wb, u.b], [pq.b])
                pr = cx.ps()
                cx.mm(pr[:, 0:T], [(wR[:, k, col0:col0 + 128], uc(k)) for k in range(KC)], [wb, u.b], [pr.b])
                q_ = qs[i2]
                cx.copy("act", q_[:, :], pq[:, 0:T], [pq.b], [q_.b])
                s2 = sq2[i2]
                cx.tt("pool", s2[:, :], q_[:, :], q_[:, :], ALU.mult, [q_.b], [s2.b])
                st_q[ci] = (pr, q_, s2, i2)

            def qk_b(ci):
                col0 = ci * 128
                gcol = 0 if ci < 4 else 2
                pr, q_, s2, i2 = st_q[ci]
                pss = cx.ps()
                cx.mm(pss[:, 0:T], [(blk_ap, s2[:, :])], [s2.b] + cb, [pss.b])
                r_ = r2[i2]
                rsqrt_from_psum(cx, r_[:, :], r_.b, pss[:, 0:T], pss.b, 64.0)
                a_ = ta[i2]
                b_ = tb[i2]
                cx.stt("dve", a_[:, :], q_[:, :], gsm[:, gcol:gcol + 1], r_[:, :], ALU.mult, ALU.mult,
                       [q_.b, gsm.b, r_.b], [a_.b])
                cx.stt("dve", b_[:, :], pr[:, 0:T], gsm[:, gcol + 1:gcol + 2], r_[:, :], ALU.mult, ALU.mult,
                       [pr.b, gsm.b, r_.b], [b_.b])
                cx.tt("pool", a_[:, :], a_[:, :], tA[:, 0, :], ALU.mult, [a_.b, tA.b], [a_.b])
                cx.tt("pool", b_[:, :], b_[:, :], tA[:, 1, :], ALU.mult, [b_.b, tA.b], [b_.b])
                o_ = qst[i2]
                cx.tt("dve", o_[:, :], a_[:, :], b_[:, :], ALU.add, [a_.b, b_.b], [o_.b])
                dst = sq_["qT"][col0:col0 + 128, t0:t0 + T] if ci < 4 else sq_["kT"][:, t0:t0 + T]
                cx.dma("sp", [(dst, o_[:, :])], o_.b, reads=[o_.b])

            qk_a(0)
            for ci in range(1, 5):
                qk_a(ci)
                qk_b(ci - 1)

            for i in range(2):
                pc = cx.ps()
                cx.mm(pc[:, 0:T], [(wA[:, k, 2304 + i * 128:2304 + (i + 1) * 128], uc(k)) for k in range(KC)],
                      [wb, u.b], [pc.b])
                cx.copy("act", cqs[:, i, :], pc[:, 0:T], [pc.b], [cqs.b])
            sqc = sq
            cx.tt("pool", sqc[:, 0:2, 0:T], cqs[:, :, :], cqs[:, :, :], ALU.mult, [cqs.b], [sq.b])
            pk = cx.ps()
            cx.mm(pk[:, 0:T], [(wA[:, k, 2560:2688], uc(k)) for k in range(KC)], [wb, u.b], [pk.b])
            cx.copy("act", ckvs[:, :], pk[:, 0:T], [pk.b], [ckvs.b])
            cx.tt("pool", sqk[:, 0:T], ckvs[:, :], ckvs[:, :], ALU.mult, [ckvs.b], [sqk.b])
            pkr = cx.ps()
            cx.mm(pkr[0:32, 0:T], [(wA[:, k, 2688:2720], uc(k)) for k in range(KC)], [wb, u.b], [pkr.b])
            pkrr = cx.ps()
            cx.mm(pkrr[0:32, 0:T], [(wR[:, k, 640:672], uc(k)) for k in range(KC)], [wb, u.b], [pkrr.b])
            qk_b(4)
            cnt["q"] += 5
            i2 = cnt["q"] % 4
            a_ = ta[i2]
            b_ = tb[i2]
            cx.tt("dve", a_[0:32, :], pkr[0:32, 0:T], tK[:, 0, :], ALU.mult, [pkr.b, tK.b], [a_.b])
            cx.tt("dve", b_[0:32, :], pkrr[0:32, 0:T], tK[:, 1, :], ALU.mult, [pkrr.b, tK.b], [b_.b])
            kr_ = kro[c % 2]
            cx.tt("pool", kr_[:, :], a_[0:32, :], b_[0:32, :], ALU.add, [a_.b, b_.b], [kr_.b])
            cnt["q"] += 1
            cx.dma("sp", [(sq_["kmT"][h, 64:96, t0:t0 + T], kr_[:, :]) for h in range(8)], kr_.b, reads=[kr_.b])

            for j in range(T // 128):
                pv = cx.ps()
                cx.mm(pv[:, 0:128], [(u[:, k, 1 + j * 128:1 + (j + 1) * 128], wA[:, k, 640:768]) for k in range(KC)],
                      [wb, u.b], [pv.b])
                v_ = vst[j % 2]
                cx.copy("act", v_[:, :, 0:64], pv[:, 0:128].rearrange("p (h x) -> p h x", h=2), [pv.b], [v_.b])
                cx.dma("sp", [(sq_["vA"][t0 + j * 128:t0 + (j + 1) * 128, :, :], v_[:, :, :])], v_.b, reads=[v_.b])

            pss = cx.ps()
            cx.mm(pss[:, 0:T], [(ones_ap, sqc[:, i, 0:T]) for i in range(2)], [sq.b] + cb, [pss.b])
            rsqrt_from_psum(cx, rstd[:, 0:T], rstd.b, pss[:, 0:T], pss.b, 256.0)
            cx.tt("dve", cqn[:, :, :], cqs[:, :, :], rstd[:, 0:T].unsqueeze(1).to_broadcast([128, 2, T]), ALU.mult,
                  [cqs.b, rstd.b], [cqn.b])
            pss2 = cx.ps()
            cx.mm(pss2[:, 0:T], [(ones_ap, sqk[:, 0:T])], [sqk.b] + cb, [pss2.b])
            rsqrt_from_psum(cx, rstdk[:, 0:T], rstdk.b, pss2[:, 0:T], pss2.b, 128.0)
            cx.tt("dve", ckvn[:, :], ckvs[:, :], rstdk[:, 0:T], ALU.mult, [ckvs.b, rstdk.b], [ckvn.b])

            for hc in range(12):
                if hc == 4 and c + 1 < nch:
                    norm(c + 1)
                ph = cx.ps()
                cx.mm(ph[:, 0:TH], [(wA[:, k, 768 + hc * 128:768 + (hc + 1) * 128], u[:, k, :]) for k in range(KC)],
                      [wb, u.b], [ph.b])
                if 4 <= hc < 8:
                    outt, outap, outb = x1k, x1k[:, hc - 4, :], x1k.b
                else:
                    outt = hst[cnt["h"] % 3]
                    cnt["h"] += 1
                    outap, outb = outt[:, :], outt.b
                cx.act(outap, ph[:, 1:T + 1], AF.Identity, [ph.b, hw.b], [outb],
                       bias=hw[:, hc, 3:4], scale=hw[:, hc, 1:2])
                cx.stt("dve", outap, ph[:, 0:T], hw[:, hc, 0:1], outap, ALU.mult, ALU.add, [ph.b, hw.b, outb], [outb])
                cx.stt("dve", outap, ph[:, 2:T + 2], hw[:, hc, 2:3], outap, ALU.mult, ALU.add, [ph.b, hw.b, outb], [outb])
                if hc < 4:
                    cx.dma("sp", [(sq_["x0T"][hc * 128:(hc + 1) * 128, t0:t0 + T], outap)], outb, reads=[outb])
                elif hc >= 8:
                    s_ = sst[cnt["s"] % 2]
                    cnt["s"] += 1
                    cx.tt("pool", s_[:, :], outap, x1k[:, hc - 8, :], ALU.mult, [outb, x1k.b], [s_.b])
                    cx.dma("sp", [(sq_["sT"][(hc - 8) * 128:(hc - 7) * 128, t0:t0 + T], s_[:, :])], s_.b, reads=[s_.b])

            for h in range(8):
                i2 = cnt["q"] % 4
                pq = cx.ps()
                cx.mm(pq[0:96, 0:T], [(wuq[:, i, h * 96:(h + 1) * 96], cqn[:, i, :]) for i in range(2)], [wb, cqn.b], [pq.b])
                pr = cx.ps()
                cx.mm(pr[0:96, 0:T], [(wuqr[:, i, h * 96:(h + 1) * 96], cqn[:, i, :]) for i in range(2)], [wb, cqn.b], [pr.b])
                a_ = ta[i2]
                b_ = tb[i2]
                cx.tt("dve", a_[0:96, :], pq[0:96, 0:T], tM[:, 0, :], ALU.mult, [pq.b, tM.b], [a_.b])
                cx.tt("dve", b_[0:96, :], pr[0:96, 0:T], tM[:, 1, :], ALU.mult, [pr.b, tM.b], [b_.b])
                o_ = qst[i2]
                cx.tt("pool", o_[0:96, :], a_[0:96, :], b_[0:96, :], ALU.add, [a_.b, b_.b], [o_.b])
                cx.dma("sp", [(sq_["qmT"][h, :, t0:t0 + T], o_[0:96, :])], o_.b, reads=[o_.b])
                cnt["q"] += 1
            for h in range(8):
                pkn = cx.ps()
                cx.mm(pkn[0:64, 0:T], [(wukv[:, h * 128:h * 128 + 64], ckvn[:, :])], [wb, ckvn.b], [pkn.b])
                k_ = kst[cnt["k"] % 3]
                cnt["k"] += 1
                cx.copy("act", k_[:, :], pkn[0:64, 0:T], [pkn.b], [k_.b])
                cx.dma("sp", [(sq_["kmT"][h, 0:64, t0:t0 + T], k_[:, :])], k_.b, reads=[k_.b])
            for j in range(T // 128):
                pv = cx.ps()
                cx.mm(pv[:, 0:512].rearrange("p (h x) -> p h x", h=8), [(ckvn[:, j * 128:(j + 1) * 128], wukv_v)],
                      [wb, ckvn.b], [pv.b])
                v_ = vmst[j % 2]
                cx.copy("act", v_[:, :, 0:64], pv[:, 0:512].rearrange("p (h x) -> p h x", h=8), [pv.b], [v_.b])
                cx.dma("sp", [(sq_["vM"][t0 + j * 128:t0 + (j + 1) * 128, :, :], v_[:, :, :])], v_.b, reads=[v_.b])

    for (L_, sq__) in seqs:
        run_seq(L_, sq__)
    cx.barrier()


def rope_tables(L):
    rows = L // GRID_W
    row = np.repeat(np.arange(rows, dtype=np.float32), GRID_W)
    col = np.tile(np.arange(GRID_W, dtype=np.float32), rows)

    def tab(d_rot):
        n_freq = d_rot // 4
        inv = (np.float32(10000.0) ** (-np.arange(n_freq, dtype=np.float32) / np.float32(n_freq))).astype(np.float32)
        ang = np.concatenate([row[:, None] * inv, col[:, None] * inv], axis=-1).astype(np.float32)
        return np.cos(ang).astype(np.float32), np.sin(ang).astype(np.float32)

    cA, sA = tab(64)
    cM, sM = tab(32)
    idxA = (np.arange(128) % 64) // 2
    out = {}
    out["ropeA_cos"] = np.ascontiguousarray(cA[:, idxA].T)
    out["ropeA_sin"] = np.ascontiguousarray(sA[:, idxA].T)
    mc = np.ones((96, L), np.float32)
    ms = np.zeros((96, L), np.float32)
    idxM = np.arange(32) // 2
    mc[64:96] = cM[:, idxM].T
    ms[64:96] = sM[:, idxM].T
    out["ropeM_cos"] = mc
    out["ropeM_sin"] = ms
    return out


def attn_pass(cx, L, jobs, co=None, co_units=0):
    cx.phase_reset()
    QC = 512
    nq = L // QC
    nk = L // 128
    qt = [cx.alloc([96, L], BF16, "aq") for _ in range(2)]
    kt = [cx.alloc([96, L], BF16, "ak") for _ in range(2)]
    vt = [cx.alloc([128, nk, 128], BF16, "av") for _ in range(2)]
    NSP = 2 if co is not None else 3
    pT = [cx.alloc([128, 2, QC], BF16, "pT") for _ in range(NSP)]
    rec = [cx.alloc([128, QC], F32, "rec") for _ in range(2)]
    rec2 = [cx.alloc([64, QC], F32, "rec2") for _ in range(2)]
    ost = [cx.alloc([64, QC], BF16, "ost") for _ in range(2)]
    ps_s = [(cx.psum[2 * i], cx.psum[2 * i + 1]) for i in range(NSP)]
    ps_o = [cx.psum[2 * NSP], cx.psum[2 * NSP + 1]]
    old_base = cx.base
    ystep = max(2, (nk // 2) // 4)
    if co is not None:
        cx.base = cx.cur
        cx.ps_allowed = list(range(2 * NSP + 2, 8))
        n_y = len(jobs) * nq * ((nk // 2) // ystep)
        rate = 1.15 * co_units / max(1, n_y)
    credit = [0.0]

    def co_step():
        if co is None:
            return
        credit[0] += rate
        while credit[0] >= 1.0:
            credit[0] -= 1.0
            try:
                next(co)
            except StopIteration:
                credit[0] = -1e18
                return

    kv_slot = {}
    state = {"kv": 0, "q": 0, "blk": 0, "oc": 0}

    def load_kv(job):
        if job["kvid"] in kv_slot:
            return
        s = state["kv"] % 2
        state["kv"] += 1
        d = job["d"]
        cx.dma("sp", [(kt[s][0:d, :], job["k"])], kt[s].b, writes=[kt[s].b])
        cx.dma("sp", [(vt[s][:, :, :], job["v"].rearrange("(c p) x -> p c x", p=128))], vt[s].b, writes=[vt[s].b])
        kv_slot[job["kvid"]] = s

    def load_q(ji):
        job = jobs[ji]
        s = ji % 2
        cx.dma("sp", [(qt[s][0:job["d"], :], job["q"])], qt[s].b, writes=[qt[s].b])

    load_kv(jobs[0])
    load_q(0)
    for ji, job in enumerate(jobs):
        d = job["d"]
        sc = job["scale"]
        if ji + 1 < len(jobs):
            load_kv(jobs[ji + 1])
            load_q(ji + 1)
        ks = kv_slot[job["kvid"]]
        K = kt[ks]
        V = vt[ks]
        Q = qt[ji % 2]
        for qc in range(nq):
            q0 = qc * QC
            po = ps_o[state["oc"] % 2]
            oslot = state["oc"] % 2
            state["oc"] += 1
            npair = nk // 2

            def s_step(pi):
                pa, pb = ps_s[(state["blk"] + pi) % NSP]
                for half, pp in ((0, pa), (1, pb)):
                    kc = pi * 2 + half
                    cx.mm(pp[:, 0:QC], [(K[0:d, kc * 128:(kc + 1) * 128], Q[0:d, q0:q0 + QC])], [K.b, Q.b], [pp.b])

            def e_step(pi):
                pa, pb = ps_s[(state["blk"] + pi) % NSP]
                p_ = pT[(state["blk"] + pi) % NSP]
                cx.act(p_[:, :, :].rearrange("p a q -> p (a q)"), cx.ps_pair_ap(pa), AF.Exp, [pa.b, pb.b], [p_.b],
                       scale=sc)

            def v_step(pi):
                p_ = pT[(state["blk"] + pi) % NSP]
                for half in range(2):
                    kc = pi * 2 + half
                    cx.mm_acc(po[:, 0:QC], V[:, kc, :], p_[:, half, :], kc == 0, kc == nk - 1, [V.b, p_.b], [po.b])

            LA = NSP - 1
            for pi in range(min(LA, npair)):
                s_step(pi)
            for pi in range(npair):
                if pi + LA < npair:
                    s_step(pi + LA)
                e_step(pi)
                v_step(pi)
                if pi % ystep == ystep - 1:
                    co_step()
            state["blk"] += npair
            r_ = rec[oslot]
            r2_ = rec2[oslot]
            o_ = ost[oslot]
            cx.recip("dve", r_[64:128, :], po[64:128, 0:QC], [po.b], [r_.b])
            cx.dma("sp", [(r2_[:, :], r_[64:128, :])], r2_.b, reads=[r_.b], writes=[r2_.b])
            cx.tt("dve", o_[:, :], po[0:64, 0:QC], r2_[:, :], ALU.mult, [po.b, r2_.b], [o_.b])
            cx.dma("sp", [(job["out"][:, q0:q0 + QC], o_[:, :])], o_.b, reads=[o_.b])
    cx.barrier()
    if co is not None:
        for _ in co:
            pass
        cx.base = old_base
        cx.ps_allowed = None


def merge_pass(cx, seqs, l, W, consts):
    T = TT
    cx.phase_reset()
    wgt = cx.alloc([128, KC, 3072], BF16, "wgt")
    wbr = cx.alloc([128, 12, D], BF16, "wbr")
    wout = cx.alloc([128, KC, D], BF16, "wout")
    gt = cx.alloc([128, KC], F32, "gmix")
    wl = WLoader(cx)
    cb = [consts["cbuf"]]
    ones_ap = consts["ones"][:, :]
    wb = Buf("p4w")
    w_in = W["w_in"][l]
    cx.dma("sp", [(gt[:, :], W["g_mix"][l].rearrange("(k p) -> p k", p=128))], gt.b, writes=[gt.b], slow=True)
    for k in range(KC):
        for c0 in range(0, 3072, 1024):
            wl.load(wgt[:, k, c0:c0 + 1024], [wb], w_in[k * 128:(k + 1) * 128, 2720 + c0:2720 + c0 + 1024], 128, 1024,
                    scale_ap=gt[:, k:k + 1], scale_reads=[gt.b])
    for i in range(3):
        for k in range(4):
            wl.load(wbr[:, i * 4 + k, :], [wb], W["w_branch"][l][i][k * 128:(k + 1) * 128, :], 128, D)
    for k in range(KC):
        wl.load(wout[:, k, :], [wb], W["w_out"][l][k * 128:(k + 1) * 128, :], 128, D)
    wl.join([wb])

    xT = [cx.alloc([128, KC, T], F32, "xT") for _ in range(2)]
    yb = [[cx.alloc([128, 4, T], BF16, "y%d" % i) for _ in range(2)] for i in range(3)]
    uu = [cx.alloc([128, KC, T], BF16, "u") for _ in range(2)]
    sq = cx.alloc([128, KC, T], BF16, "sq")
    rstd = cx.alloc([128, T], F32, "rstd")
    sig = [cx.alloc([128, T], F32, "sig") for _ in range(6)]
    tm = [cx.alloc([128, T], F32, "tm") for _ in range(6)]
    mg = cx.alloc([128, KC, T], BF16, "mg")
    def run_seq(L, sq_):
        xav = sq_["xa"].rearrange("(k p) l -> p k l", p=128)
        xbv = sq_["xb"].rearrange("(k p) l -> p k l", p=128)
        ysrc = [sq_[n].rearrange("(k p) l -> p k l", p=128) for n in ("yaT", "ybT", "ycT")]
        nch = L // T

        def load_chunk(c):
            t0 = c * T
            xs = xT[c % 2]
            cx.dma("sp", [(xs[:, :, :], xav[:, :, t0:t0 + T])], xs.b, writes=[xs.b])
            for i in range(3):
                y_ = yb[i][c % 2]
                cx.dma("sp", [(y_[:, :, :], ysrc[i][:, :, t0:t0 + T])], y_.b, writes=[y_.b])

        def norm(c):
            xs_ = xT[c % 2]
            u_ = uu[c % 2]
            rms_stats(cx, xs_[:, :, :], xs_.b, sq, ones_ap, cb, KC, T, rstd, float(D),
                      lambda k: sq[:, :, :] if k is None else sq[:, k, :])
            cx.tt("dve", u_[:, :, :], xs_[:, :, :], rstd[:, 0:T].unsqueeze(1).to_broadcast([128, KC, T]),
                  ALU.mult, [xs_.b, rstd.b], [u_.b])

        load_chunk(0)
        norm(0)
        for c in range(nch):
            t0 = c * T
            xs = xT[c % 2]
            u = uu[c % 2]
            if c + 1 < nch:
                load_chunk(c + 1)
            for d in range(KC):
                if d == 4 and c + 1 < nch:
                    norm(c + 1)
                pbs = []
                for i in range(3):
                    y_ = yb[i][c % 2]
                    pb = cx.ps()
                    cx.mm(pb[:, 0:T], [(wbr[:, i * 4 + k, d * 128:(d + 1) * 128], y_[:, k, :]) for k in range(4)],
                          [wb, y_.b], [pb.b])
                    pg = cx.ps()
                    cx.mm(pg[:, 0:T], [(wgt[:, k, i * 1024 + d * 128:i * 1024 + (d + 1) * 128], u[:, k, :]) for k in range(KC)],
                          [wb, u.b], [pg.b])
                    sg = sig[(d % 2) * 3 + i]
                    tq = tm[(d % 2) * 3 + i]
                    cx.act(sg[:, :], pg[:, 0:T], AF.Sigmoid, [pg.b], [sg.b])
                    cx.tt("dve", tq[:, :], sg[:, :], pb[:, 0:T], ALU.mult, [sg.b, pb.b], [tq.b])
                t0_, t1_, t2_ = tm[(d % 2) * 3], tm[(d % 2) * 3 + 1], tm[(d % 2) * 3 + 2]
                cx.tt("pool", t0_[:, :], t0_[:, :], t1_[:, :], ALU.add, [t0_.b, t1_.b], [t0_.b])
                cx.tt("pool", mg[:, d, :], t0_[:, :], t2_[:, :], ALU.add, [t0_.b, t2_.b], [mg.b])
            for d2 in range(KC):
                po = cx.ps()
                cx.mm(po[:, 0:T], [(wout[:, d, d2 * 128:(d2 + 1) * 128], mg[:, d, :]) for d in range(KC)], [wb, mg.b], [po.b])
                cx.tt("dve", xs[:, d2, :], xs[:, d2, :], po[:, 0:T], ALU.add, [xs.b, po.b], [xs.b])
            cx.dma("sp", [(xbv[:, :, t0:t0 + T], xs[:, :, :])], xs.b, reads=[xs.b])

    for (L_, sq__) in seqs:
        run_seq(L_, sq__)
    cx.barrier()


HYW = 512


def hyena_tables(L):
    N = 2 * L
    N1 = N // 128
    out = {}
    t = np.linspace(0.0, 1.0, L, dtype=np.float32)
    w = (np.float32(2.0 * math.pi) * np.arange(L, dtype=np.float32) / np.float32(L)).astype(np.float32)
    f = np.linspace(1e-4, 15, 16, dtype=np.float32)[None, :]
    z = np.concatenate([t[:, None], np.cos(f * w[:, None]), -np.sin(f * w[:, None])], axis=-1).astype(np.float32)
    idx2 = (L - np.arange(L)) % L
    zfull = np.concatenate([z, z[idx2]], axis=0)
    tfull = np.concatenate([t, t[idx2]], axis=0).astype(np.float32)
    tfull[L] = 1.0e4
    out["zposT"] = np.ascontiguousarray(zfull.T)
    out["tvec"] = tfull
    max_decay = math.log(1e-2) / 0.3
    min_decay = math.log(1e-2) / 1.5
    deltas = np.abs(np.linspace(min_decay, max_decay, HYW, dtype=np.float32))
    out["negdelta"] = (-deltas).astype(np.float32)
    n1 = np.arange(N1, dtype=np.float64)
    a1 = 2.0 * np.pi * np.outer(n1, n1) / N1
    K1 = N1 // 2 + 1
    out["F1"] = np.concatenate([np.cos(a1)[:, :K1], -np.sin(a1)[:, :K1]], axis=1).astype(np.float32)
    n2 = np.arange(128, dtype=np.float64)
    at = 2.0 * np.pi * np.outer(n2, n1[:K1]) / N
    out["Tc"] = np.cos(at).astype(np.float32)
    out["Ts"] = np.sin(at).astype(np.float32)
    out["T2c"] = np.ascontiguousarray(np.cos(at).T).astype(np.float32)
    out["T2s"] = np.ascontiguousarray(np.sin(at).T).astype(np.float32)
    a2 = 2.0 * np.pi * np.outer(n2, n2) / 128.0
    C = np.cos(a2)
    S = np.sin(a2)
    out["F2"] = np.concatenate([C, S, -S], axis=1).astype(np.float32)
    out["F2i"] = np.concatenate([C, S, -S, C], axis=1).astype(np.float32)
    wk = np.full((K1, 1), 2.0)
    wk[0, 0] = 1.0
    wk[K1 - 1, 0] = 1.0
    out["G1"] = np.concatenate([wk * np.cos(a1)[:K1, :N1 // 2], -wk * np.sin(a1)[:K1, :N1 // 2]], axis=1).astype(np.float32)
    return out


def cmul(cx, Ar, Ai, a_reads, Tc, Ts, t_reads, sgn, outR, outI, o_writes, tmp):
    p1, p2, p3, p4 = tmp
    shp = list(Ar.shape)

    def v(t):
        ap = t[0:shp[0], 0:int(np.prod(shp[1:]))]
        if len(shp) == 3:
            ap = ap.rearrange("p (a b) -> p a b", a=shp[1])
        return ap
    cx.tt("dve", v(p1), Ar, Tc, ALU.mult, a_reads + t_reads, [p1.b])
    cx.tt("dve", v(p2), Ai, Ts, ALU.mult, a_reads + t_reads, [p2.b])
    cx.tt("pool", outR, v(p1), v(p2), ALU.subtract if sgn > 0 else ALU.add, [p1.b, p2.b], o_writes)
    cx.tt("dve", v(p3), Ai, Tc, ALU.mult, a_reads + t_reads, [p3.b])
    cx.tt("dve", v(p4), Ar, Ts, ALU.mult, a_reads + t_reads, [p4.b])
    cx.tt("pool", outI, v(p3), v(p4), ALU.add if sgn > 0 else ALU.subtract, [p3.b, p4.b], o_writes)


class HyFFT:
    def __init__(self, cx, L, tabs):
        self.cx = cx
        self.L = L
        N1 = 2 * L // 128
        self.N1 = N1
        K1 = N1 // 2 + 1
        self.K1 = K1
        self.CG = min(4, max(1, 512 // N1))
        self.cb = Buf("hytab")
        cb = self.cb
        wl = WLoader(cx, ncols=512, nst=2)
        self.F1 = cx.alloc([N1, 2 * K1], BF16, "F1")
        self.F2 = cx.alloc([128, 384], BF16, "F2")
        self.F2i = cx.alloc([128, 512], BF16, "F2i")
        self.G1 = cx.alloc([K1, N1], BF16, "G1")
        self.Tc = cx.alloc([128, K1], F32, "Tc")
        self.Ts = cx.alloc([128, K1], F32, "Ts")
        self.T2c = cx.alloc([K1, 128], F32, "T2c")
        self.T2s = cx.alloc([K1, 128], F32, "T2s")
        wl.load(self.F1[:, :], [cb], tabs["F1"], N1, 2 * K1, eng="dve")
        wl.load(self.F2[:, :], [cb], tabs["F2"], 128, 384, eng="dve")
        wl.load(self.F2i[:, :], [cb], tabs["F2i"], 128, 512, eng="dve")
        wl.load(self.G1[:, :], [cb], tabs["G1"], K1, N1, eng="dve")
        wl.join([cb])
        cx.dma("sp", [(self.Tc[:, :], tabs["Tc"]), (self.Ts[:, :], tabs["Ts"]),
                      (self.T2c[:, :], tabs["T2c"]), (self.T2s[:, :], tabs["T2s"])], self.Tc.b, writes=[cb])
        self.NS = 4
        self.NT = 4
        self.tmp = [[cx.alloc([128, 512], F32, "ctmp") for _ in range(4)] for _ in range(self.NT)]
        self.ti = 0
        self.Br = [cx.alloc([128, self.CG, K1], BF16, "Br") for _ in range(self.NS)]
        self.Bi = [cx.alloc([128, self.CG, K1], BF16, "Bi") for _ in range(self.NS)]

    def tmps(self):
        self.ti += 1
        return self.tmp[self.ti % self.NT]

    def stage1(self, xin, xb, nrows, c0, gi):
        cx = self.cx
        N1 = self.N1
        CG = self.CG
        K1 = self.K1
        cpb = max(1, min(CG, 512 // (2 * K1)))
        Br = self.Br[gi % self.NS]
        Bi = self.Bi[gi % self.NS]
        for b0 in range(0, CG, cpb):
            nb = min(cpb, CG - b0)
            pa = cx.ps()
            for j in range(nb):
                cx.mm(pa[:, j * 2 * K1:(j + 1) * 2 * K1], [(xin[0:nrows, c0 + b0 + j, :], self.F1[0:nrows, :])],
                      [xb, self.cb], [pa.b])
            A = pa[:, 0:nb * 2 * K1].rearrange("p (c r k) -> p c r k", c=nb, r=2)
            Tcb = self.Tc[:, :].unsqueeze(1).to_broadcast([128, nb, K1])
            Tsb = self.Ts[:, :].unsqueeze(1).to_broadcast([128, nb, K1])
            cmul(cx, A[:, :, 0, :], A[:, :, 1, :], [pa.b], Tcb, Tsb, [self.cb], -1,
                 Br[:, b0:b0 + nb, :], Bi[:, b0:b0 + nb, :], [Br.b], self.tmps())

    def stage3(self, gi):
        cx = self.cx
        N1 = self.K1
        CG = self.CG
        Br = self.Br[gi % self.NS]
        Bi = self.Bi[gi % self.NS]
        C = self.F2[:, 0:128]
        S = self.F2[:, 128:256]
        nS = self.F2[:, 256:384]
        brf = Br[:, :, :].rearrange("p c k -> p (c k)")
        bif = Bi[:, :, :].rearrange("p c k -> p (c k)")
        W = CG * N1
        pxr = cx.ps()
        cx.mm(pxr[:, 0:W], [(C, brf), (S, bif)], [Br.b, self.cb], [pxr.b])
        pxi = cx.ps()
        cx.mm(pxi[:, 0:W], [(C, bif), (nS, brf)], [Br.b, self.cb], [pxi.b])
        return pxr, pxi


def hyena_filter_pass(cx, L, l, sq_, W, consts, tabs):
    cx.phase_reset()
    N = 2 * L
    PC = 512
    npc = N // PC
    w1 = cx.alloc([33, 64], F32, "w1")
    w2 = cx.alloc([64, 64], F32, "w2")
    w3 = cx.alloc([64, 1024], F32, "w3")
    sm = cx.alloc([64, 8], F32, "hsm")
    nd = cx.alloc([128, 4], F32, "negd")
    bia = cx.alloc([128, 4], F32, "hbias")
    cb = Buf("hyw")
    cx.dma("sp", [(w1[:, :], W["w_hy_f1"][l]), (w2[:, :], W["w_hy_f2"][l]), (w3[:, :], W["w_hy_f3"][l])], w1.b, writes=[cb])
    cx.dma("sp", [(sm[:, 0:1], W["b_hy_f1"][l].rearrange("(p o) -> p o", o=1)),
                  (sm[:, 1:2], W["b_hy_f2"][l].rearrange("(p o) -> p o", o=1)),
                  (sm[:, 2:3], W["hy_sin_freq"][l].rearrange("(p o) -> p o", o=1)),
                  (nd[:, :], tabs["negdelta"].rearrange("(c p) -> p c", p=128)),
                  (bia[:, :], W["hy_bias"][l].rearrange("(c p) -> p c", p=128))], sm.b, writes=[cb], slow=True)
    cx.tt("dve", sm[:, 3:4], sm[:, 0:1], sm[:, 2:3], ALU.mult, [cb], [cb])
    cx.tt("dve", sm[:, 4:5], sm[:, 1:2], sm[:, 2:3], ALU.mult, [cb], [cb])
    h2all = cx.alloc([64, L], F32, "h2all")
    zt = [cx.alloc([33, PC], F32, "zt") for _ in range(2)]
    arg = [cx.alloc([64, PC], F32, "arg") for _ in range(2)]
    msk = [cx.alloc([64, PC], F32, "msk") for _ in range(2)]
    h1 = [cx.alloc([64, PC], F32, "h1") for _ in range(2)]
    tv = [cx.alloc([128, PC], F32, "tv") for _ in range(2)]
    win = [cx.alloc([128, PC], F32, "win") for _ in range(2)]
    kf = [cx.alloc([128, PC], F32, "kf") for _ in range(2)]
    kb = [cx.alloc([128, PC], BF16, "kb") for _ in range(2)]
    PI = math.pi

    def sin_layer(ps, bcol, out_ap, out_b, i):
        a = arg[i % 2]
        m = msk[i % 2]
        cx.ts("dve", a[:, :], ps[0:64, 0:PC], sm[:, 2:3], sm[:, bcol:bcol + 1], ALU.mult, ALU.add, [ps.b, cb], [a.b])
        for _ in range(2):
            cx.ts("dve", m[:, :], a[:, :], PI, -2.0 * PI, ALU.is_gt, ALU.mult, [a.b], [m.b])
            cx.tt("dve", a[:, :], a[:, :], m[:, :], ALU.add, [a.b, m.b], [a.b])
            cx.ts("dve", m[:, :], a[:, :], -PI, 2.0 * PI, ALU.is_lt, ALU.mult, [a.b], [m.b])
            cx.tt("dve", a[:, :], a[:, :], m[:, :], ALU.add, [a.b, m.b], [a.b])
        cx.act(out_ap, a[:, :], AF.Sin, [a.b], [out_b])

    it = 0
    nph = npc // 2
    for half in range(2):
        for pl in range(nph):
            pc = half * nph + pl
            j0 = pc * PC
            z_ = zt[pc % 2]
            cx.dma("sp", [(z_[:, :], tabs["zposT"][:, j0:j0 + PC])], z_.b, writes=[z_.b])
            p1 = cx.ps()
            cx.mm(p1[0:64, 0:PC], [(w1[:, :], z_[:, :])], [cb, z_.b], [p1.b])
            h_ = h1[pc % 2]
            sin_layer(p1, 3, h_[:, :], h_.b, pc)
            p2 = cx.ps()
            cx.mm(p2[0:64, 0:PC], [(w2[:, :], h_[:, :])], [cb, h_.b], [p2.b])
            sin_layer(p2, 4, h2all[:, pl * PC:(pl + 1) * PC], h2all.b, pc + 1)
            yield
        for pl in range(nph):
            pc = half * nph + pl
            j0 = pc * PC
            t_ = tv[pc % 2]
            cx.dma("sp", [(t_[:, :], tabs["tvec"][j0:j0 + PC].partition_broadcast(128))], t_.b, writes=[t_.b])
            for fc in range(4):
                p3 = cx.ps()
                cx.mm(p3[:, 0:PC], [(w3[:, half * 512 + fc * 128:half * 512 + (fc + 1) * 128], h2all[:, pl * PC:(pl + 1) * PC])],
                      [cb, h2all.b], [p3.b])
                w_ = win[it % 2]
                cx.act(w_[:, :], t_[:, :], AF.Exp, [t_.b, cb], [w_.b], scale=nd[:, fc:fc + 1])
                k_ = kf[it % 2]
                cx.tt("dve", k_[:, :], p3[:, 0:PC], w_[:, :], ALU.mult, [p3.b, w_.b], [k_.b])
                if pc == 0:
                    cx.tt("dve", k_[:, 0:1], k_[:, 0:1], bia[:, fc:fc + 1], ALU.add, [k_.b, cb], [k_.b])
                kb_ = kb[it % 2]
                cx.copy("pool", kb_[:, :], k_[:, :], [k_.b], [kb_.b])
                cx.dma("sp", [(sq_["kcT"][fc * 128:(fc + 1) * 128, j0:j0 + PC], kb_[:, :])], kb_.b, reads=[kb_.b])
                it += 1
                yield
    cx.barrier()


def hyena_spectrum_pass(cx, L, sq_, tabs):
    cx.phase_reset()
    ff = HyFFT(cx, L, tabs)
    N1 = ff.N1
    CG = ff.CG
    CB = 16
    xin = [cx.alloc([N1, CB, 128], BF16, "kin") for _ in range(2)]
    hst = [cx.alloc([128, 2, 512], BF16, "hst") for _ in range(2)]
    src = sq_["kcT"].rearrange("c (a b) -> a c b", b=128)
    inv_n = 1.0 / (2 * L)
    groups = []
    for cb0 in range(0, HYW, CB):
        for c0 in range(0, CB, CG):
            groups.append((cb0, c0))
    G = len(groups)
    K1 = ff.K1
    Wd = CG * K1
    xcur = {}

    def st1(g):
        cb0, c0 = groups[g]
        if c0 == 0:
            x_ = xin[(cb0 // CB) % 2]
            cx.dma("sp", [(x_[:, :, :], src[:, cb0:cb0 + CB, :])], x_.b, writes=[x_.b])
            xcur[cb0] = x_
        x_ = xcur[cb0]
        ff.stage1(x_, x_.b, N1, c0, g)

    def st2(g):
        cb0, c0 = groups[g]
        pxr, pxi = ff.stage3(g)
        h_ = hst[g % 2]
        cx.act(h_[:, 0, 0:Wd], pxr[:, 0:Wd], AF.Copy, [pxr.b], [h_.b], scale=inv_n)
        cx.act(h_[:, 1, 0:Wd], pxi[:, 0:Wd], AF.Copy, [pxi.b], [h_.b], scale=inv_n)
        ch = cb0 + c0
        cx.dma("sp", [(sq_["Hr"][:, ch * K1:ch * K1 + Wd], h_[:, 0, 0:Wd]),
                      (sq_["Hi"][:, ch * K1:ch * K1 + Wd], h_[:, 1, 0:Wd])], h_.b, reads=[h_.b])

    for t in range(G + 1):
        if t < G:
            st1(t)
            yield
        if t >= 1:
            st2(t - 1)
            yield
    cx.barrier()


def hyena_conv_pass(cx, L, sq_, tabs):
    cx.phase_reset()
    ff = HyFFT(cx, L, tabs)
    N1 = ff.N1
    NH = N1 // 2
    K1 = ff.K1
    CG = ff.CG
    CB = 16
    Wd = CG * K1
    xin = [cx.alloc([NH, CB, 128], BF16, "sin") for _ in range(2)]
    x0t = [cx.alloc([NH, CB, 128], F32, "x0in") for _ in range(2)]
    ht = [cx.alloc([128, 2, 512], BF16, "ht") for _ in range(4)]
    Yr = [cx.alloc([128, CG, K1], BF16, "Yr") for _ in range(4)]
    Yi = [cx.alloc([128, CG, K1], BF16, "Yi") for _ in range(4)]
    Dr = [cx.alloc([K1, CG, 128], BF16, "Dr") for _ in range(4)]
    Di = [cx.alloc([K1, CG, 128], BF16, "Di") for _ in range(4)]
    yst = [cx.alloc([NH, CG, 128], BF16, "yst") for _ in range(4)]
    ssrc = sq_["sT"].rearrange("c (a b) -> a c b", b=128)
    x0src = sq_["x0T"].rearrange("c (a b) -> a c b", b=128)
    ydst = sq_["ybT"].rearrange("c (a b) -> a c b", b=128)
    Ci = ff.F2i[:, 0:256]
    Si = ff.F2i[:, 256:512]
    groups = []
    for cb0 in range(0, HYW, CB):
        for c0 in range(0, CB, CG):
            groups.append((cb0, c0))
    G = len(groups)
    xcur = {}
    NS = ff.NS

    def st1(g):
        cb0, c0 = groups[g]
        if c0 == 0:
            x_ = xin[(cb0 // CB) % 2]
            x0_ = x0t[(cb0 // CB) % 2]
            cx.dma("sp", [(x_[:, :, :], ssrc[:, cb0:cb0 + CB, :])], x_.b, writes=[x_.b])
            cx.dma("sp", [(x0_[:, :, :], x0src[:, cb0:cb0 + CB, :])], x0_.b, writes=[x0_.b])
            xcur[cb0] = (x_, x0_)
        x_, x0_ = xcur[cb0]
        ch = cb0 + c0
        h_ = ht[g % NS]
        cx.dma("sp", [(h_[:, 0, 0:Wd], sq_["Hr"][:, ch * K1:ch * K1 + Wd]),
                      (h_[:, 1, 0:Wd], sq_["Hi"][:, ch * K1:ch * K1 + Wd])], h_.b, writes=[h_.b])
        ff.stage1(x_, x_.b, NH, c0, g)

    def st2(g):
        h_ = ht[g % NS]
        pxr, pxi = ff.stage3(g)
        yr = Yr[g % NS]
        yi = Yi[g % NS]
        cmul(cx, pxr[:, 0:Wd], pxi[:, 0:Wd], [pxr.b, pxi.b], h_[:, 0, 0:Wd], h_[:, 1, 0:Wd], [h_.b], +1,
             yr[:, :, :].rearrange("p c k -> p (c k)"), yi[:, :, :].rearrange("p c k -> p (c k)"), [yr.b], ff.tmps())

    def st3(g):
        yr = Yr[g % NS]
        yi = Yi[g % NS]
        dr = Dr[g % NS]
        di = Di[g % NS]
        for b0 in range(0, CG, 2):
            nb = min(2, CG - b0)
            pc_ = cx.ps()
            for j in range(nb):
                cx.mm(pc_[0:K1, j * 256:(j + 1) * 256], [(yr[:, b0 + j, :], Ci), (yi[:, b0 + j, :], Si)],
                      [yr.b, ff.cb], [pc_.b])
            Cv = pc_[0:K1, 0:nb * 256].rearrange("p (c r k) -> p c r k", c=nb, r=2)
            T2cb = ff.T2c[:, :].unsqueeze(1).to_broadcast([K1, nb, 128])
            T2sb = ff.T2s[:, :].unsqueeze(1).to_broadcast([K1, nb, 128])
            cmul(cx, Cv[:, :, 0, :], Cv[:, :, 1, :], [pc_.b], T2cb, T2sb, [ff.cb], +1,
                 dr[:, b0:b0 + nb, :], di[:, b0:b0 + nb, :], [dr.b], ff.tmps())

    def st4(g):
        cb0, c0 = groups[g]
        x_, x0_ = xcur[cb0]
        ch = cb0 + c0
        dr = Dr[g % NS]
        di = Di[g % NS]
        ys = yst[g % NS]
        for q0 in range(0, CG, 4):
            nq_ = min(4, CG - q0)
            py = cx.ps()
            cx.mm(py[0:NH, 0:nq_ * 128],
                  [(ff.G1[:, 0:NH], dr[:, q0:q0 + nq_, :].rearrange("p c k -> p (c k)")),
                   (ff.G1[:, NH:N1], di[:, q0:q0 + nq_, :].rearrange("p c k -> p (c k)"))], [dr.b, ff.cb], [py.b])
            cx.tt("dve", ys[:, q0:q0 + nq_, :], py[0:NH, 0:nq_ * 128].rearrange("p (c k) -> p c k", c=nq_),
                  x0_[:, c0 + q0:c0 + q0 + nq_, :], ALU.mult, [py.b, x0_.b], [ys.b])
        cx.dma("sp", [(ydst[:, ch:ch + CG, :], ys[:, :, :])], ys.b, reads=[ys.b])

    for t in range(G + 3):
        if t < G:
            st1(t)
            yield
        if 0 <= t - 1 < G:
            st2(t - 1)
            yield
        if 0 <= t - 2 < G:
            st3(t - 2)
            yield
        if 0 <= t - 3 < G:
            st4(t - 3)
            yield
    cx.barrier()


def hyena_gen(cx, L, l, s, W, consts):
    yield from hyena_filter_pass(cx, L, l, s, W, consts, s["tabs"])
    yield from hyena_spectrum_pass(cx, L, s, s["tabs"])
    yield from hyena_conv_pass(cx, L, s, s["tabs"])


def hyena_units(L):
    N1 = 2 * L // 128
    CG = min(4, max(1, 512 // N1))
    G = HYW // CG
    npc = 2 * L // 512
    return npc * 5 + 2 * G + 4 * G


W_SHAPES = {
    "g_ffn1": [2, D], "w_ffn1_gate": [2, D, DFF], "w_ffn1_up": [2, D, DFF], "w_ffn1_down": [2, DFF, D],
    "g_mix": [2, D], "w_in": [2, D, IN_W], "g_qnorm": [2, 64], "g_knorm": [2, 64],
    "w_hy_short": [2, 3, 1536], "b_hy_short": [2, 1536], "w_hy_f1": [2, 33, 64], "b_hy_f1": [2, 64],
    "w_hy_f2": [2, 64, 64], "b_hy_f2": [2, 64], "w_hy_f3": [2, 64, 1024], "hy_sin_freq": [2, 64],
    "hy_bias": [2, 512], "g_mla_q": [2, 256], "w_mla_uq": [2, 256, 768], "g_mla_kv": [2, 128],
    "w_mla_ukv": [2, 128, 1024], "w_branch": [2, 3, 512, D], "w_out": [2, D, D],
    "g_ffn2": [2, D], "w_ffn2_gate": [2, D, DFF], "w_ffn2_up": [2, D, DFF], "w_ffn2_down": [2, DFF, D],
    "g_final": [D],
}
DEPTH = 2


def host_tables(L, tag):
    t = {}
    for k, v in rope_tables(L).items():
        t["%s_%s" % (tag, k)] = v
    for k, v in hyena_tables(L).items():
        t["%s_hy_%s" % (tag, k)] = v
    return t


def build_program(lens):
    from contextlib import ExitStack
    nc = bass.Bass("TRN2", target_bir_lowering=False)
    W = {n: nc.dram_tensor(n, shp, F32, kind="ExternalInput").ap() for n, shp in W_SHAPES.items()}
    ident = nc.dram_tensor("ident", [128, 128], F32, kind="ExternalInput").ap()
    seqs = []
    for tag, L in lens.items():
        N1 = 2 * L // 128
        sq = {"L": L, "tag": tag}
        sq["x"] = nc.dram_tensor("x_" + tag, [L, D], F32, kind="ExternalInput").ap()
        sq["y"] = nc.dram_tensor("y_" + tag, [L, D], F32, kind="ExternalOutput").ap()
        tabs = {}
        for k, v in host_tables(L, tag).items():
            ap = nc.dram_tensor(k, list(v.shape), F32, kind="ExternalInput").ap()
            short = k[len(tag) + 1:]
            if short.startswith("hy_"):
                tabs[short[3:]] = ap
            else:
                sq[short] = ap
        sq["tabs"] = tabs

        def scr(n, shape, dt):
            sq[n] = nc.dram_tensor("%s_%s" % (tag, n), shape, dt, kind="Internal").ap()
        for n in ("xa", "xb", "xc"):
            scr(n, [D, L], F32)
        scr("qT", [512, L], BF16)
        scr("kT", [128, L], BF16)
        scr("vA", [L, 2, 128], BF16)
        scr("x0T", [512, L], F32)
        scr("sT", [512, L], BF16)
        scr("qmT", [8, 96, L], BF16)
        scr("kmT", [8, 96, L], BF16)
        scr("vM", [L, 8, 128], BF16)
        for n in ("yaT", "ybT", "ycT"):
            scr(n, [512, L], BF16)
        scr("kcT", [512, 2 * L], BF16)
        scr("Hr", [128, 512 * (N1 // 2 + 1)], BF16)
        scr("Hi", [128, 512 * (N1 // 2 + 1)], BF16)
        seqs.append(sq)

    cx = Ctx(nc)
    consts = setup_consts(cx, ident)
    for l in range(DEPTH):
        last = (l == DEPTH - 1)
        if l == 0:
            ffn_pass(cx, [(s["L"], s["x"], s["xa"], None) for s in seqs], W["g_ffn1"][l], W["w_ffn1_gate"][l],
                     W["w_ffn1_up"][l], W["w_ffn1_down"][l], consts, src_tok=True)
        else:
            ffn_pass(cx, [(s["L"], s["xc"], s["xa"], None) for s in seqs], W["g_ffn1"][l], W["w_ffn1_gate"][l],
                     W["w_ffn1_up"][l], W["w_ffn1_down"][l], consts)
        inproj_pass(cx, [(s["L"], s) for s in seqs], l, W, consts)
        for s in seqs:
            L = s["L"]
            jobs = []
            for h in range(8):
                g = h // 4
                jobs.append(dict(q=s["qT"][h * 64:(h + 1) * 64, :], k=s["kT"][g * 64:(g + 1) * 64, :], v=s["vA"][:, g, :],
                                 d=64, scale=64 ** -0.5, out=s["yaT"][h * 64:(h + 1) * 64, :], kvid="a%d" % g))
            for h in range(8):
                jobs.append(dict(q=s["qmT"][h], k=s["kmT"][h], v=s["vM"][:, h, :], d=96, scale=96 ** -0.5,
                                 out=s["ycT"][h * 64:(h + 1) * 64, :], kvid="m%d" % h))
            attn_pass(cx, L, jobs, co=hyena_gen(cx, L, l, s, W, consts), co_units=hyena_units(L))
        merge_pass(cx, [(s["L"], s) for s in seqs], l, W, consts)
        if not last:
            ffn_pass(cx, [(s["L"], s["xb"], s["xc"], None) for s in seqs], W["g_ffn2"][l], W["w_ffn2_gate"][l],
                     W["w_ffn2_up"][l], W["w_ffn2_down"][l], consts)
        else:
            ffn_pass(cx, [(s["L"], s["xb"], None, s["y"]) for s in seqs], W["g_ffn2"][l], W["w_ffn2_gate"][l],
                     W["w_ffn2_up"][l], W["w_ffn2_down"][l], consts, final=W["g_final"])
    stack = ExitStack()
    cx.S.finalize(stack)
    with nc.Block() as block:
        cx.S.emit(block)
    stack.close()
    return nc, cx


_TABLE_CACHE = {}


def run_step(inputs, lens, n_cores, prompt_idx, sample_idx):
    nc, cx = build_program(lens)
    tabs = {}
    for tag, L in lens.items():
        key = (tag, L)
        if key not in _TABLE_CACHE:
            _TABLE_CACHE[key] = host_tables(L, tag)
        tabs.update(_TABLE_CACHE[key])
    ident = np.eye(128, dtype=np.float32)
    in_maps = []
    for c in range(n_cores):
        m = {n: np.ascontiguousarray(inputs[n], dtype=np.float32) for n in W_SHAPES}
        m["ident"] = ident
        m.update(tabs)
        m["x_p"] = np.ascontiguousarray(inputs["x_prompt"][prompt_idx[c]], dtype=np.float32)
        m["x_s"] = np.ascontiguousarray(inputs["x_sample"][sample_idx[c]], dtype=np.float32)
        in_maps.append(m)
    res = run_bass_kernel_spmd(nc, in_maps, core_ids=list(range(n_cores)))
    return res


def kernel(**inputs):
    xp = np.asarray(inputs["x_prompt"])
    xs = np.asarray(inputs["x_sample"])
    B, Lp, _ = xp.shape
    Bs, Ls, _ = xs.shape
    n = 8
    prompt_idx = [c % B for c in range(n)]
    sample_idx = [c % Bs for c in range(n)]
    res = run_step(inputs, {"p": Lp, "s": Ls}, n, prompt_idx, sample_idx)
    yp = np.stack([np.asarray(res.results[c]["y_p"], dtype=np.float32) for c in range(B)], axis=0)
    ys = np.stack([np.asarray(res.results[c]["y_s"], dtype=np.float32) for c in range(Bs)], axis=0)
    return (yp, ys)
```

```python
import math
import numpy as np
import concourse.bass as bass
import concourse.mybir as mybir
from concourse.bass_utils import run_bass_kernel_spmd

F32 = mybir.dt.float32
BF16 = mybir.dt.bfloat16
ALU = mybir.AluOpType
AF = mybir.ActivationFunctionType
AX = mybir.AxisListType

ENGINES = ("pe", "dve", "act", "pool", "sp")


class Buf:
    __slots__ = ("name", "writers", "readers", "dsem", "depoch", "dlast")

    def __init__(self, name):
        self.name = name
        self.writers = []
        self.readers = []
        self.dsem = None
        self.depoch = -1
        self.dlast = None


class Op:
    __slots__ = ("eng", "fn", "deps", "is_dma", "semkey", "val", "signal", "waits", "snap")

    def __init__(self, eng, fn, is_dma=False):
        self.eng = eng
        self.fn = fn
        self.deps = set()
        self.is_dma = is_dma
        self.semkey = None
        self.val = 0
        self.signal = False
        self.waits = []
        self.snap = None


class Sched:
    def __init__(self, nc):
        self.nc = nc
        self.ops = []
        self.last = {e: None for e in ENGINES}
        self.slot_count = []
        self.slot_free = []
        self.slot_used = []
        self.epoch = 0
        self.dma_since_barrier = []
        self.barbuf = Buf("BAR")

    def _track(self, op, reads, writes):
        for b in reads:
            op.deps.update(b.writers)
        for b in writes:
            op.deps.update(b.writers)
            op.deps.update(b.readers)
        for b in reads:
            b.readers.append(op)
        for b in writes:
            b.writers = [op]
            b.readers = []
        op.deps.discard(op)

    def op(self, eng, fn, reads=(), writes=()):
        o = Op(eng, fn)
        self._track(o, reads, writes)
        self.ops.append(o)
        if fn is not None:
            self.last[eng] = o
        return o

    def dma(self, eng, fn, sembuf, n, reads=(), writes=()):
        o = Op(eng, fn, is_dma=True)
        self._track(o, reads, writes)
        if sembuf.depoch != self.epoch:
            if self.slot_free:
                sembuf.dsem = self.slot_free.pop()
            else:
                sembuf.dsem = len(self.slot_count)
                self.slot_count.append(0)
            self.slot_used.append(sembuf.dsem)
            sembuf.depoch = self.epoch
            sembuf.dlast = None
        if sembuf.dlast is not None:
            o.deps.add(sembuf.dlast)
        self.slot_count[sembuf.dsem] += 16 * n
        sembuf.dlast = o
        o.semkey = ("D", sembuf.dsem)
        o.val = self.slot_count[sembuf.dsem]
        o.signal = True
        self.ops.append(o)
        self.dma_since_barrier.append(o)
        return o

    def barrier(self, memset_fn):
        x = Op("pool", memset_fn)
        for e in ENGINES:
            if self.last[e] is not None:
                x.deps.add(self.last[e])
        x.deps.update(self.dma_since_barrier)
        self.dma_since_barrier = []
        self.slot_free.extend(self.slot_used)
        self.slot_used = []
        self.epoch += 1
        self.ops.append(x)
        self.last["pool"] = x
        for e in ENGINES:
            if e == "pool":
                continue
            o = Op(e, None)
            o.deps.add(x)
            self.ops.append(o)
        return x

    def finalize(self, stack):
        nc = self.nc
        for o in self.ops:
            for d in o.deps:
                d.signal = True
        cnt = {e: 0 for e in ENGINES}
        known = {e: {} for e in ENGINES}
        snaps = {}
        for o in self.ops:
            E = o.eng
            k = known[E]
            waits = {}
            for d in o.deps:
                if d.is_dma:
                    key, val = d.semkey, d.val
                else:
                    if d.eng == "pe" and E == "pe":
                        continue
                    key, val = ("E", d.eng), d.val
                if k.get(key, 0) >= val:
                    continue
                if waits.get(key, 0) < val:
                    waits[key] = val
            for key, val in waits.items():
                if k.get(key, 0) < val:
                    k[key] = val
                sn = snaps.get((key, val))
                if sn is not None:
                    for k2, v2 in sn.items():
                        if k.get(k2, 0) < v2:
                            k[k2] = v2
            o.waits = list(waits.items())
            if o.is_dma:
                snaps[(o.semkey, o.val)] = dict(k)
            elif o.signal and o.fn is not None:
                cnt[E] += 1
                o.val = cnt[E]
                snaps[(("E", E), o.val)] = dict(k)
            elif o.signal and o.fn is None:
                raise RuntimeError("wait-only op cannot signal")
        self.esem = {e: stack.enter_context(nc.semaphore("s_" + e)) for e in ENGINES}
        self.dsem = [stack.enter_context(nc.semaphore("d%d" % i)) for i in range(len(self.slot_count))]
        self.stats = dict(cnt)

    def _sem(self, key):
        return self.esem[key[1]] if key[0] == "E" else self.dsem[key[1]]

    def emit(self, block):
        per = {e: [o for o in self.ops if o.eng == e] for e in ENGINES}

        def run(engobj, E):
            for o in per[E]:
                for key, val in o.waits:
                    engobj.wait_ge(self._sem(key), val)
                if o.fn is None:
                    continue
                if o.is_dma:
                    o.fn(engobj, self._sem(o.semkey))
                else:
                    ins = o.fn(engobj)
                    if o.signal:
                        ins.then_inc(self.esem[E], 1)

        @block.tensor
        def _(t):
            run(t, "pe")

        @block.vector
        def _(v):
            run(v, "dve")

        @block.scalar
        def _(s):
            run(s, "act")

        @block.gpsimd
        def _(g):
            run(g, "pool")

        @block.sync
        def _(s):
            run(s, "sp")


class Tile:
    __slots__ = ("t", "b", "shape")

    def __init__(self, t, b, shape):
        self.t = t
        self.b = b
        self.shape = shape

    def __getitem__(self, k):
        return self.t[k]


def dsize(dt):
    return mybir.dt.size(dt)


class Ctx:
    def __init__(self, nc):
        self.nc = nc
        self.S = Sched(nc)
        self.uid = 0
        self.base = 16640
        self.cur = 16640
        self.cap = 229376
        self.psum = []
        self.ps_pairs = {}
        for i in range(4):
            t2 = nc.alloc_psum_tensor("psd%d" % i, [128, 1024], F32)
            for h in range(2):
                nm = "ps%d" % (2 * i + h)
                self.psum.append(Tile(t2[:, h * 512:(h + 1) * 512], Buf(nm), [128, 512]))
            self.ps_pairs["ps%d" % (2 * i)] = t2[:, :]
        self.ps_rr = 0

    def alloc(self, shape, dtype, name="t"):
        per = dsize(dtype)
        for s in shape[1:]:
            per *= s
        per = (per + 31) // 32 * 32
        assert self.cur + per <= self.cap, ("SBUF overflow", name, self.cur, per)
        self.uid += 1
        nm = "%s_%d" % (name, self.uid)
        t = self.nc.alloc_sbuf_tensor_at(nm, list(shape), dtype, offset=self.cur)
        self.cur += per
        return Tile(t, Buf(nm), list(shape))

    def persist(self):
        self.base = self.cur

    def phase_reset(self):
        self.cur = self.base

    def ps(self):
        p = self.psum[self.ps_rr % 8]
        self.ps_rr += 1
        return p

    def mm(self, out_ap, pairs, reads, writes):
        pairs = list(pairs)

        def fn(e):
            n = len(pairs)
            ins = None
            for i, (l, r) in enumerate(pairs):
                ins = e.matmul(out_ap, l, r, start=(i == 0), stop=(i == n - 1))
            return ins
        return self.S.op("pe", fn, reads, writes)

    def mm_acc(self, out_ap, lhsT, rhs, start, stop, reads, writes):
        def fn(e):
            return e.matmul(out_ap, lhsT, rhs, start=start, stop=stop)
        return self.S.op("pe", fn, reads, writes)

    def ps_pair_ap(self, pa):
        return self.ps_pairs[pa.b.name]

    def transposes(self, items, ident_ap, reads, writes):
        items = list(items)

        def fn(e):
            ins = None
            for o, i in items:
                ins = e.transpose(o, i, ident_ap)
            return ins
        return self.S.op("pe", fn, reads, writes)

    def act(self, out_ap, in_ap, func, reads, writes, bias=0.0, scale=1.0, eng="act"):
        def fn(e):
            return e.activation(out=out_ap, in_=in_ap, func=func, bias=bias, scale=scale)
        return self.S.op("act", fn, reads, writes)

    def tt(self, eng, out_ap, a_ap, b_ap, op, reads, writes):
        def fn(e):
            return e.tensor_tensor(out=out_ap, in0=a_ap, in1=b_ap, op=op)
        return self.S.op(eng, fn, reads, writes)

    def ts(self, eng, out_ap, a_ap, s1, s2, op0, op1, reads, writes):
        def fn(e):
            if op1 is None:
                return e.tensor_scalar(out=out_ap, in0=a_ap, scalar1=s1, scalar2=None, op0=op0)
            return e.tensor_scalar(out=out_ap, in0=a_ap, scalar1=s1, scalar2=s2, op0=op0, op1=op1)
        return self.S.op(eng, fn, reads, writes)

    def stt(self, eng, out_ap, a_ap, s, b_ap, op0, op1, reads, writes):
        def fn(e):
            return e.scalar_tensor_tensor(out=out_ap, in0=a_ap, scalar=s, in1=b_ap, op0=op0, op1=op1)
        return self.S.op(eng, fn, reads, writes)

    def copy(self, eng, out_ap, in_ap, reads, writes):
        if eng == "act":
            def fn(e):
                return e.activation(out=out_ap, in_=in_ap, func=AF.Copy)
        else:
            def fn(e):
                return e.tensor_copy(out=out_ap, in_=in_ap)
        return self.S.op(eng, fn, reads, writes)

    def recip(self, eng, out_ap, in_ap, reads, writes):
        def fn(e):
            return e.reciprocal(out=out_ap, in_=in_ap)
        return self.S.op(eng, fn, reads, writes)

    def memset(self, eng, ap, val, writes):
        def fn(e):
            return e.memset(ap, val)
        return self.S.op(eng, fn, (), writes)

    def dma(self, q, pairs, sembuf, reads=(), writes=(), slow=False):
        pairs = list(pairs)

        def fn(e, sem):
            for o, i in pairs:
                if slow:
                    e.dma_start(out=o, in_=i, allow_slow_non_contiguous=True).then_inc(sem, 16)
                else:
                    e.dma_start(out=o, in_=i).then_inc(sem, 16)
        return self.S.dma(q, fn, sembuf, len(pairs), reads, writes)

    def barrier(self):
        bt = self.bar_tile

        def fn(e):
            return e.memset(bt[:, :], 0.0)
        self.S.barrier(fn)


D = 1024
KC = 8
DFF = 2816
FC = 22
EPS = 1e-6
TT = 256
IN_W = 5792
GRID_W = 64


class WLoader:
    def __init__(self, cx, ncols=1024, nst=3):
        self.cx = cx
        self.ncols = ncols
        self.st = [cx.alloc([128, ncols], F32, "wst") for _ in range(nst)]
        self.i = 0
        self.engs = ("dve", "pool", "act")
        self.parts = []
        self.jt = cx.alloc([128, 8], F32, "wjoin")

    def join(self, targets):
        jt = self.jt

        def fn(e):
            return e.memset(jt[:, :], 0.0)
        self.cx.S.op("dve", fn, list(self.parts), list(targets) + [jt.b])
        self.parts = []

    def load(self, dst_ap, dst_buf_writes, src_ap, rows, cols, scale_ap=None, scale_reads=(), eng=None, negate=False):
        cx = self.cx
        st = self.st[self.i % len(self.st)]
        if eng is None:
            eng = self.engs[self.i % 3]
        self.i += 1
        cx.dma("sp", [(st[0:rows, 0:cols], src_ap)], st.b, writes=[st.b])
        src = st[0:rows, 0:cols]
        pb = Buf("wpart")
        self.parts.append(pb)
        dst_buf_writes = [pb]
        if scale_ap is not None:
            if eng == "act":
                def fn(e):
                    return e.activation(out=dst_ap, in_=src, func=AF.Copy, scale=scale_ap)
            elif eng == "pool":
                shp = list(dst_ap.shape)

                def fn(e):
                    return e.tensor_tensor(out=dst_ap, in0=src, in1=scale_ap.to_broadcast(shp), op=ALU.mult)
            else:
                def fn(e):
                    return e.tensor_scalar(out=dst_ap, in0=src, scalar1=scale_ap, scalar2=None, op0=ALU.mult)
            cx.S.op(eng, fn, [st.b] + list(scale_reads), dst_buf_writes)
        else:
            cx.copy(eng, dst_ap, src, [st.b], dst_buf_writes)
        return st


def rms_stats(cx, x_ap_flat, x_buf, sq, ones_ap, const_reads, nk, T, rstd, dim, sq_view):
    cx.tt("pool", sq_view(None), x_ap_flat, x_ap_flat, ALU.mult, [x_buf], [sq.b])
    ps = cx.ps()
    cx.mm(ps[:, 0:T], [(ones_ap, sq_view(k)) for k in range(nk)], [sq.b] + list(const_reads), [ps.b])
    rsqrt_from_psum(cx, rstd[:, 0:T], rstd.b, ps[:, 0:T], ps.b, float(dim))


def rsqrt_from_psum(cx, out_ap, out_b, ps_ap, ps_b, dim):
    npart = out_ap.shape[0]
    cx.act(out_ap, ps_ap, AF.Sqrt, [ps_b, cx.consts["cbuf"]], [out_b], bias=cx.consts["eps"][0:npart, 0:1], scale=1.0 / dim)
    cx.recip("dve", out_ap, out_ap, [out_b], [out_b])


def ffn_pass(cx, seqs, g_ap, wg_ap, wu_ap, wd_ap, consts, src_tok=False, final=None):
    T = TT
    cx.phase_reset()
    wg = cx.alloc([128, KC, DFF], BF16, "wg")
    wu = cx.alloc([128, KC, DFF], BF16, "wu")
    wd = cx.alloc([128, FC, D], BF16, "wd")
    gt = cx.alloc([128, KC], F32, "g")
    xT = [cx.alloc([128, KC, T], F32, "xT") for _ in range(2)]
    uu = [cx.alloc([128, KC, T], BF16, "u") for _ in range(2)]
    sq = cx.alloc([128, KC, T], BF16, "sq")
    rstd = cx.alloc([128, T], F32, "rstd")
    hT = cx.alloc([128, FC, T], BF16, "hT")
    sil = [cx.alloc([128, T], F32, "sil") for _ in range(2)]
    need_tok = src_tok or (final is not None)
    src = dst = None
    if need_tok:
        xtok = [cx.alloc([128, D], F32, "xtok") for _ in range(2)]
    if final is not None:
        gf = cx.alloc([128, KC], F32, "gf")
        xn = cx.alloc([128, KC, T], F32, "xn")
    wl = WLoader(cx)
    ones_ap = consts["ones"][:, :]
    ident_ap = consts["ident"][:, :]
    cb = [consts["cbuf"]]

    cx.dma("sp", [(gt[:, :], g_ap.rearrange("(k p) -> p k", p=128))], gt.b, writes=[gt.b], slow=True)
    if final is not None:
        cx.dma("sp", [(gf[:, :], final.rearrange("(k p) -> p k", p=128))], gf.b, writes=[gf.b], slow=True)
    for k in range(KC):
        for c0 in range(0, DFF, 1024):
            cw = min(1024, DFF - c0)
            wl.load(wg[:, k, c0:c0 + cw], [wg.b], wg_ap[k * 128:(k + 1) * 128, c0:c0 + cw], 128, cw,
                    scale_ap=gt[:, k:k + 1], scale_reads=[gt.b])
            wl.load(wu[:, k, c0:c0 + cw], [wu.b], wu_ap[k * 128:(k + 1) * 128, c0:c0 + cw], 128, cw,
                    scale_ap=gt[:, k:k + 1], scale_reads=[gt.b])
    for f in range(FC):
        wl.load(wd[:, f, :], [wd.b], wd_ap[f * 128:(f + 1) * 128, :], 128, D)
    wl.join([wg.b, wu.b, wd.b])

    def run_seq(L, src, dst, out_tok):
        nch = L // T
        srcv = None if src_tok else src.rearrange("(k p) l -> p k l", p=128)
        dstv = None if dst is None else dst.rearrange("(k p) l -> p k l", p=128)

        def load_chunk(c):
            xs = xT[c % 2]
            t0 = c * T
            if not src_tok:
                cx.dma("sp", [(xs[:, :, :], srcv[:, :, t0:t0 + T])], xs.b, writes=[xs.b])
            else:
                for j in range(T // 128):
                    xk = xtok[j % 2]
                    cx.dma("sp", [(xk[:, :], src[t0 + j * 128:t0 + (j + 1) * 128, :])], xk.b, writes=[xk.b])
                for kk in range(KC // 2):
                    ps = cx.ps()
                    items = []
                    for k2 in range(2):
                        k = kk * 2 + k2
                        for j in range(T // 128):
                            items.append((ps[:, k2 * T + j * 128:k2 * T + (j + 1) * 128],
                                          xtok[j % 2][:, k * 128:(k + 1) * 128]))
                    cx.transposes(items, ident_ap, [xtok[0].b, xtok[1].b] + cb, [ps.b])
                    cx.copy("act", xs[:, kk * 2:kk * 2 + 2, :], ps[:, 0:2 * T].rearrange("p (a t) -> p a t", a=2),
                            [ps.b], [xs.b])

        def norm(c):
            xs_ = xT[c % 2]
            u_ = uu[c % 2]
            rms_stats(cx, xs_[:, :, :], xs_.b, sq, ones_ap, cb, KC, T, rstd, float(D),
                      lambda k: sq[:, :, :] if k is None else sq[:, k, :])
            cx.tt("dve", u_[:, :, :], xs_[:, :, :], rstd[:, 0:T].unsqueeze(1).to_broadcast([128, KC, T]),
                  ALU.mult, [xs_.b, rstd.b], [u_.b])

        load_chunk(0)
        norm(0)
        for c in range(nch):
            xs = xT[c % 2]
            u = uu[c % 2]
            t0 = c * T
            if c + 1 < nch:
                load_chunk(c + 1)
            for f in range(FC):
                pg = cx.ps()
                cx.mm(pg[:, 0:T], [(wg[:, k, f * 128:(f + 1) * 128], u[:, k, :]) for k in range(KC)],
                      [wg.b, u.b], [pg.b])
                pu = cx.ps()
                cx.mm(pu[:, 0:T], [(wu[:, k, f * 128:(f + 1) * 128], u[:, k, :]) for k in range(KC)],
                      [wu.b, u.b], [pu.b])
                s = sil[f % 2]
                cx.act(s[:, :], pg[:, 0:T], AF.Silu, [pg.b], [s.b])
                cx.tt("dve", hT[:, f, :], s[:, :], pu[:, 0:T], ALU.mult, [s.b, pu.b], [hT.b])
                if f == FC // 2 and c + 1 < nch:
                    norm(c + 1)
            for d in range(KC):
                pd = cx.ps()
                cx.mm(pd[:, 0:T], [(wd[:, f, d * 128:(d + 1) * 128], hT[:, f, :]) for f in range(FC)],
                      [wd.b, hT.b], [pd.b])
                cx.stt("dve", xs[:, d, :], pd[:, 0:T], 0.5, xs[:, d, :], ALU.mult, ALU.add, [pd.b, xs.b], [xs.b])
            if final is None:
                cx.dma("sp", [(dstv[:, :, t0:t0 + T], xs[:, :, :])], xs.b, reads=[xs.b])
            else:
                rms_stats(cx, xs[:, :, :], xs.b, sq, ones_ap, cb, KC, T, rstd, float(D),
                          lambda k: sq[:, :, :] if k is None else sq[:, k, :])
                for k in range(KC):
                    cx.stt("dve", xn[:, k, :], xs[:, k, :], gf[:, k:k + 1], rstd[:, 0:T], ALU.mult, ALU.mult,
                           [xs.b, gf.b, rstd.b], [xn.b])
                for j in range(T // 128):
                    xk = xtok[j % 2]
                    for hb in range(2):
                        ps = cx.ps()
                        items = [(ps[:, q * 128:(q + 1) * 128], xn[:, hb * 4 + q, j * 128:(j + 1) * 128])
                                 for q in range(4)]
                        cx.transposes(items, ident_ap, [xn.b] + cb, [ps.b])
                        cx.copy("act", xk[:, hb * 512:(hb + 1) * 512], ps[:, :], [ps.b], [xk.b])
                    cx.dma("sp", [(out_tok[t0 + j * 128:t0 + (j + 1) * 128, :], xk[:, :])], xk.b, reads=[xk.b])

    for (L_, src_, dst_, out_) in seqs:
        run_seq(L_, src_, dst_, out_)
    cx.barrier()


def setup_consts(cx, ident_dram):
    c = {}
    ones = cx.alloc([128, 128], BF16, "ones")
    ident = cx.alloc([128, 128], F32, "ident")
    blk = cx.alloc([128, 128], BF16, "blk64")
    bar = cx.alloc([128, 8], F32, "bar")
    mhalf = cx.alloc([128, 8], F32, "mhalf")
    c["mhalf"] = mhalf
    epst = cx.alloc([128, 8], F32, "eps")
    c["eps"] = epst
    cx.consts = c
    cx.bar_tile = bar
    cbuf = Buf("consts")
    cx.memset("pool", ones[:, :], 1.0, [cbuf])
    cx.memset("pool", blk[:, :], 0.0, [cbuf])
    cx.memset("pool", mhalf[:, :], -0.5, [cbuf])
    cx.memset("pool", epst[:, :], EPS, [cbuf])
    cx.memset("pool", blk[0:64, 0:64], 1.0, [cbuf])
    cx.memset("pool", blk[64:128, 64:128], 1.0, [cbuf])
    cx.dma("sp", [(ident[:, :], ident_dram)], ident.b, writes=[cbuf])
    c["ones"] = ones
    c["ident"] = ident
    c["blk64"] = blk
    c["cbuf"] = cbuf
    cx.persist()
    return c


def small_vec_load(cx, dst_tile, src_ap, writes):
    cx.dma("sp", [(dst_tile, src_ap)], writes[0], writes=writes, slow=True)


def inproj_pass(cx, seqs, l, W, consts):
    T = TT
    TH = T + 2
    cx.phase_reset()
    NA = 2720
    wA = cx.alloc([128, KC, NA], BF16, "wA")
    wR = cx.alloc([128, KC, 672], BF16, "wR")
    wuq = cx.alloc([128, 2, 768], BF16, "wuq")
    wuqr = cx.alloc([128, 2, 768], BF16, "wuqr")
    wukv = cx.alloc([128, 1024], BF16, "wukv")
    gt = cx.alloc([128, KC], F32, "gmix")
    gsm = cx.alloc([128, 8], F32, "gsm")
    hw = cx.alloc([128, 12, 4], F32, "hw")
    wl = WLoader(cx)
    cb = [consts["cbuf"]]
    ones_ap = consts["ones"][:, :]
    blk_ap = consts["blk64"][:, :]
    wb = Buf("p2w")

    w_in = W["w_in"][l]
    cx.dma("sp", [(gt[:, :], W["g_mix"][l].rearrange("(k p) -> p k", p=128))], gt.b, writes=[gt.b], slow=True)
    gq = W["g_qnorm"][l]
    gk = W["g_knorm"][l]
    pairs = []
    for half in range(2):
        p0 = half * 64
        pairs.append((gsm[p0:p0 + 64, 0:1], gq.rearrange("(p o) -> p o", o=1)))
        pairs.append((gsm[p0:p0 + 64, 2:3], gk.rearrange("(p o) -> p o", o=1)))
        gq2 = gq.rearrange("(i two) -> i two", two=2)
        gk2 = gk.rearrange("(i two) -> i two", two=2)
        pairs.append((gsm[p0:p0 + 64:2, 1:2], gq2[:, 1:2]))
        pairs.append((gsm[p0 + 1:p0 + 64:2, 1:2], gq2[:, 0:1]))
        pairs.append((gsm[p0:p0 + 64:2, 3:4], gk2[:, 1:2]))
        pairs.append((gsm[p0 + 1:p0 + 64:2, 3:4], gk2[:, 0:1]))
    pairs.append((gsm[:, 4:6], W["g_mla_q"][l].rearrange("(k p) -> p k", p=128)))
    pairs.append((gsm[:, 6:7], W["g_mla_kv"][l].rearrange("(p o) -> p o", o=1)))
    cx.dma("sp", pairs, gsm.b, writes=[gsm.b], slow=True)
    hp = [(hw[:, :, j], W["w_hy_short"][l][j].rearrange("(c p) -> p c", p=128)) for j in range(3)]
    hp.append((hw[:, :, 3], W["b_hy_short"][l].rearrange("(c p) -> p c", p=128)))
    cx.dma("sp", hp, hw.b, writes=[hw.b], slow=True)

    for k in range(KC):
        for c0 in range(0, NA, 1024):
            cw = min(1024, NA - c0)
            wl.load(wA[:, k, c0:c0 + cw], [wb], w_in[k * 128:(k + 1) * 128, c0:c0 + cw], 128, cw,
                    scale_ap=gt[:, k:k + 1], scale_reads=[gt.b])
    wl.join([wb])
    for k in range(KC):
        for (s0, n, d0) in ((0, 640, 0), (2688, 32, 640)):
            src = wA[:, k, s0:s0 + n].rearrange("p (i two) -> p i two", two=2)
            dst = wR[:, k, d0:d0 + n].rearrange("p (i two) -> p i two", two=2)
            cx.ts("dve", dst[:, :, 0], src[:, :, 1], -1.0, None, ALU.mult, None, [wb], [wb])
            cx.copy("dve", dst[:, :, 1], src[:, :, 0], [wb], [wb])
    for i in range(2):
        wl.load(wuq[:, i, :], [wb], W["w_mla_uq"][l][i * 128:(i + 1) * 128, :], 128, 768,
                scale_ap=gsm[:, 4 + i:5 + i], scale_reads=[gsm.b])
        wl.join([wb])
        cx.copy("dve", wuqr[:, i, :], wuq[:, i, :], [wb], [wb])
        for h in range(8):
            src = wuq[:, i, h * 96 + 64:h * 96 + 96].rearrange("p (i two) -> p i two", two=2)
            dst = wuqr[:, i, h * 96 + 64:h * 96 + 96].rearrange("p (i two) -> p i two", two=2)
            cx.ts("dve", dst[:, :, 0], src[:, :, 1], -1.0, None, ALU.mult, None, [wb], [wb])
            cx.copy("dve", dst[:, :, 1], src[:, :, 0], [wb], [wb])
    wl.load(wukv[:, :], [wb], W["w_mla_ukv"][l][:, :], 128, 1024, scale_ap=gsm[:, 6:7], scale_reads=[gsm.b])
    wl.join([wb])
    wukv_v = wukv[:, :].rearrange("p (h x) -> p h x", x=128)[:, :, 64:128]

    xh = [cx.alloc([128, KC, TH], F32, "xh") for _ in range(2)]
    us = [cx.alloc([128, KC, TH], BF16, "u") for _ in range(2)]
    sqs = [cx.alloc([128, KC, TH], BF16, "sq") for _ in range(2)]
    rstds = [cx.alloc([128, TH], F32, "rstd") for _ in range(2)]
    sq = cx.alloc([128, 2, T], BF16, "sqc")
    rstd = cx.alloc([128, T], F32, "rstdc")
    sqk = cx.alloc([128, T], BF16, "sqk")
    rstdk = cx.alloc([128, T], F32, "rstdk")
    tabA = [cx.alloc([128, 2, T], F32, "tabA") for _ in range(2)]
    tabM = [cx.alloc([96, 2, T], F32, "tabM") for _ in range(2)]
    tabK = [cx.alloc([32, 2, T], F32, "tabK") for _ in range(2)]
    qs = [cx.alloc([128, T], F32, "qs") for _ in range(4)]
    sq2 = [cx.alloc([128, T], BF16, "sq2") for _ in range(4)]
    r2 = [cx.alloc([128, T], F32, "r2") for _ in range(4)]
    ta = [cx.alloc([128, T], F32, "ta") for _ in range(4)]
    tb = [cx.alloc([128, T], F32, "tb") for _ in range(4)]
    qst = [cx.alloc([128, T], BF16, "qst") for _ in range(4)]
    vst = [cx.alloc([128, 2, 128], BF16, "vst") for _ in range(2)]
    hst = [cx.alloc([128, T], F32, "hst") for _ in range(3)]
    x1k = cx.alloc([128, 4, T], F32, "x1k")
    sst = [cx.alloc([128, T], BF16, "sst") for _ in range(2)]
    cqs = cx.alloc([128, 2, T], F32, "cqs")
    cqn = cx.alloc([128, 2, T], BF16, "cqn")
    ckvs = cx.alloc([128, T], F32, "ckvs")
    ckvn = cx.alloc([128, T], BF16, "ckvn")
    kro = [cx.alloc([32, T], BF16, "kro") for _ in range(2)]
    kst = [cx.alloc([64, T], BF16, "kst") for _ in range(3)]
    vmst = [cx.alloc([128, 8, 128], BF16, "vmst") for _ in range(2)]
    for t_ in vst:
        cx.memset("pool", t_[:, :, 64:128], 1.0, [t_.b])
    for t_ in vmst:
        cx.memset("pool", t_[:, :, 64:128], 1.0, [t_.b])

    def run_seq(L, sq_):
        xav = sq_["xa"].rearrange("(k p) l -> p k l", p=128)
        nch = L // T
        cnt = {"q": 0, "h": 0, "s": 0, "k": 0}

        def load_chunk(c):
            xs = xh[c % 2]
            t0 = c * T
            lo = max(t0 - 1, 0)
            hi = min(t0 + T + 1, L)
            d0 = lo - (t0 - 1)
            if c == 0:
                cx.memset("pool", xs[:, :, 0:1], 0.0, [xs.b])
            if c == nch - 1:
                cx.memset("pool", xs[:, :, TH - 1:TH], 0.0, [xs.b])
            cx.dma("sp", [(xs[:, :, d0:d0 + (hi - lo)], xav[:, :, lo:hi])], xs.b, writes=[xs.b])
            ta_ = tabA[c % 2]
            cx.dma("sp", [(ta_[:, 0, :], sq_["ropeA_cos"][:, t0:t0 + T]), (ta_[:, 1, :], sq_["ropeA_sin"][:, t0:t0 + T])],
                   ta_.b, writes=[ta_.b])
            tm_ = tabM[c % 2]
            cx.dma("sp", [(tm_[:, 0, :], sq_["ropeM_cos"][:, t0:t0 + T]), (tm_[:, 1, :], sq_["ropeM_sin"][:, t0:t0 + T])],
                   tm_.b, writes=[tm_.b])
            tk_ = tabK[c % 2]
            cx.dma("sp", [(tk_[:, 0, :], sq_["ropeM_cos"][64:96, t0:t0 + T]), (tk_[:, 1, :], sq_["ropeM_sin"][64:96, t0:t0 + T])],
                   tk_.b, writes=[tk_.b])

        def norm(c):
            xs_ = xh[c % 2]
            sq_n = sqs[c % 2]
            rs_n = rstds[c % 2]
            u_n = us[c % 2]
            rms_stats(cx, xs_[:, :, :], xs_.b, sq_n, ones_ap, cb, KC, TH, rs_n, float(D),
                      lambda k: sq_n[:, :, :] if k is None else sq_n[:, k, :])
            cx.tt("dve", u_n[:, :, :], xs_[:, :, :], rs_n[:, 0:TH].unsqueeze(1).to_broadcast([128, KC, TH]),
                  ALU.mult, [xs_.b, rs_n.b], [u_n.b])

        load_chunk(0)
        norm(0)
        for c in range(nch):
            xs = xh[c % 2]
            u = us[c % 2]
            t0 = c * T
            if c + 1 < nch:
                load_chunk(c + 1)
            tA = tabA[c % 2]
            tM = tabM[c % 2]
            tK = tabK[c % 2]

            def uc(k):
                return u[:, k, 1:T + 1]

            st_q = {}

            def qk_a(ci):
                col0 = ci * 128
                i2 = (cnt["q"] + ci) % 4
                pq = cx.ps()
                cx.mm(pq[:, 0:T], [(wA[:, k, col0:col0 + 128], uc(k)) for k in range(KC)], [wb, u.b], [pq.b])
                pr = cx.ps()
                cx.mm(pr[:, 0:T], [(wR[:, k, col0:col0 + 128], uc(k)) for k in range(KC)], [wb, u.b], [pr.b])
                q_ = qs[i2]
                cx.copy("act", q_[:, :], pq[:, 0:T], [pq.b], [q_.b])
                s2 = sq2[i2]
                cx.tt("pool", s2[:, :], q_[:, :], q_[:, :], ALU.mult, [q_.b], [s2.b])
                st_q[ci] = (pr, q_, s2, i2)

            def qk_b(ci):
                col0 = ci * 128
                gcol = 0 if ci < 4 else 2
                pr, q_, s2, i2 = st_q[ci]
                pss = cx.ps()
                cx.mm(pss[:, 0:T], [(blk_ap, s2[:, :])], [s2.b] + cb, [pss.b])
                r_ = r2[i2]
                rsqrt_from_psum(cx, r_[:, :], r_.b, pss[:, 0:T], pss.b, 64.0)
                a_ = ta[i2]
                b_ = tb[i2]
                cx.stt("dve", a_[:, :], q_[:, :], gsm[:, gcol:gcol + 1], r_[:, :], ALU.mult, ALU.mult,
                       [q_.b, gsm.b, r_.b], [a_.b])
                cx.stt("dve", b_[:, :], pr[:, 0:T], gsm[:, gcol + 1:gcol + 2], r_[:, :], ALU.mult, ALU.mult,
                       [pr.b, gsm.b, r_.b], [b_.b])
                cx.tt("pool", a_[:, :], a_[:, :], tA[:, 0, :], ALU.mult, [a_.b, tA.b], [a_.b])
                cx.tt("pool", b_[:, :], b_[:, :], tA[:, 1, :], ALU.mult, [b_.b, tA.b], [b_.b])
                o_ = qst[i2]
                cx.tt("dve", o_[:, :], a_[:, :], b_[:, :], ALU.add, [a_.b, b_.b], [o_.b])
                dst = sq_["qT"][col0:col0 + 128, t0:t0 + T] if ci < 4 else sq_["kT"][:, t0:t0 + T]
                cx.dma("sp", [(dst, o_[:, :])], o_.b, reads=[o_.b])

            qk_a(0)
            for ci in range(1, 5):
                qk_a(ci)
                qk_b(ci - 1)

            for i in range(2):
                pc = cx.ps()
                cx.mm(pc[:, 0:T], [(wA[:, k, 2304 + i * 128:2304 + (i + 1) * 128], uc(k)) for k in range(KC)],
                      [wb, u.b], [pc.b])
                cx.copy("act", cqs[:, i, :], pc[:, 0:T], [pc.b], [cqs.b])
            sqc = sq
            cx.tt("pool", sqc[:, 0:2, 0:T], cqs[:, :, :], cqs[:, :, :], ALU.mult, [cqs.b], [sq.b])
            pk = cx.ps()
            cx.mm(pk[:, 0:T], [(wA[:, k, 2560:2688], uc(k)) for k in range(KC)], [wb, u.b], [pk.b])
            cx.copy("act", ckvs[:, :], pk[:, 0:T], [pk.b], [ckvs.b])
            cx.tt("pool", sqk[:, 0:T], ckvs[:, :], ckvs[:, :], ALU.mult, [ckvs.b], [sqk.b])
            pkr = cx.ps()
            cx.mm(pkr[0:32, 0:T], [(wA[:, k, 2688:2720], uc(k)) for k in range(KC)], [wb, u.b], [pkr.b])
            pkrr = cx.ps()
            cx.mm(pkrr[0:32, 0:T], [(wR[:, k, 640:672], uc(k)) for k in range(KC)], [wb, u.b], [pkrr.b])
            qk_b(4)
            cnt["q"] += 5
            i2 = cnt["q"] % 4
            a_ = ta[i2]
            b_ = tb[i2]
            cx.tt("dve", a_[0:32, :], pkr[0:32, 0:T], tK[:, 0, :], ALU.mult, [pkr.b, tK.b], [a_.b])
            cx.tt("dve", b_[0:32, :], pkrr[0:32, 0:T], tK[:, 1, :], ALU.mult, [pkrr.b, tK.b], [b_.b])
            kr_ = kro[c % 2]
            cx.tt("pool", kr_[:, :], a_[0:32, :], b_[0:32, :], ALU.add, [a_.b, b_.b], [kr_.b])
            cnt["q"] += 1
            cx.dma("sp", [(sq_["kmT"][h, 64:96, t0:t0 + T], kr_[:, :]) for h in range(8)], kr_.b, reads=[kr_.b])

            for j in range(T // 128):
                pv = cx.ps()
                cx.mm(pv[:, 0:128], [(u[:, k, 1 + j * 128:1 + (j + 1) * 128], wA[:, k, 640:768]) for k in range(KC)],
                      [wb, u.b], [pv.b])
                v_ = vst[j % 2]
                cx.copy("act", v_[:, :, 0:64], pv[:, 0:128].rearrange("p (h x) -> p h x", h=2), [pv.b], [v_.b])
                cx.dma("sp", [(sq_["vA"][t0 + j * 128:t0 + (j + 1) * 128, :, :], v_[:, :, :])], v_.b, reads=[v_.b])

            pss = cx.ps()
            cx.mm(pss[:, 0:T], [(ones_ap, sqc[:, i, 0:T]) for i in range(2)], [sq.b] + cb, [pss.b])
            rsqrt_from_psum(cx, rstd[:, 0:T], rstd.b, pss[:, 0:T], pss.b, 256.0)
            cx.tt("dve", cqn[:, :, :], cqs[:, :, :], rstd[:, 0:T].unsqueeze(1).to_broadcast([128, 2, T]), ALU.mult,
                  [cqs.b, rstd.b], [cqn.b])
            pss2 = cx.ps()
            cx.mm(pss2[:, 0:T], [(ones_ap, sqk[:, 0:T])], [sqk.b] + cb, [pss2.b])
            rsqrt_from_psum(cx, rstdk[:, 0:T], rstdk.b, pss2[:, 0:T], pss2.b, 128.0)
            cx.tt("dve", ckvn[:, :], ckvs[:, :], rstdk[:, 0:T], ALU.mult, [ckvs.b, rstdk.b], [ckvn.b])

            for hc in range(12):
                if hc == 4 and c + 1 < nch:
                    norm(c + 1)
                ph = cx.ps()
                cx.mm(ph[:, 0:TH], [(wA[:, k, 768 + hc * 128:768 + (hc + 1) * 128], u[:, k, :]) for k in range(KC)],
                      [wb, u.b], [ph.b])
                if 4 <= hc < 8:
                    outt, outap, outb = x1k, x1k[:, hc - 4, :], x1k.b
                else:
                    outt = hst[cnt["h"] % 3]
                    cnt["h"] += 1
                    outap, outb = outt[:, :], outt.b
                cx.act(outap, ph[:, 1:T + 1], AF.Identity, [ph.b, hw.b], [outb],
                       bias=hw[:, hc, 3:4], scale=hw[:, hc, 1:2])
                cx.stt("dve", outap, ph[:, 0:T], hw[:, hc, 0:1], outap, ALU.mult, ALU.add, [ph.b, hw.b, outb], [outb])
                cx.stt("dve", outap, ph[:, 2:T + 2], hw[:, hc, 2:3], outap, ALU.mult, ALU.add, [ph.b, hw.b, outb], [outb])
                if hc < 4:
                    cx.dma("sp", [(sq_["x0T"][hc * 128:(hc + 1) * 128, t0:t0 + T], outap)], outb, reads=[outb])
                elif hc >= 8:
                    s_ = sst[cnt["s"] % 2]
                    cnt["s"] += 1
                    cx.tt("pool", s_[:, :], outap, x1k[:, hc - 8, :], ALU.mult, [outb, x1k.b], [s_.b])
                    cx.dma("sp", [(sq_["sT"][(hc - 8) * 128:(hc - 7) * 128, t0:t0 + T], s_[:, :])], s_.b, reads=[s_.b])

            for h in range(8):
                i2 = cnt["q"] % 4
                pq = cx.ps()
                cx.mm(pq[0:96, 0:T], [(wuq[:, i, h * 96:(h + 1) * 96], cqn[:, i, :]) for i in range(2)], [wb, cqn.b], [pq.b])
                pr = cx.ps()
                cx.mm(pr[0:96, 0:T], [(wuqr[:, i, h * 96:(h + 1) * 96], cqn[:, i, :]) for i in range(2)], [wb, cqn.b], [pr.b])
                a_ = ta[i2]
                b_ = tb[i2]
                cx.tt("dve", a_[0:96, :], pq[0:96, 0:T], tM[:, 0, :], ALU.mult, [pq.b, tM.b], [a_.b])
                cx.tt("dve", b_[0:96, :], pr[0:96, 0:T], tM[:, 1, :], ALU.mult, [pr.b, tM.b], [b_.b])
                o_ = qst[i2]
                cx.tt("pool", o_[0:96, :], a_[0:96, :], b_[0:96, :], ALU.add, [a_.b, b_.b], [o_.b])
                cx.dma("sp", [(sq_["qmT"][h, :, t0:t0 + T], o_[0:96, :])], o_.b, reads=[o_.b])
                cnt["q"] += 1
            for h in range(8):
                pkn = cx.ps()
                cx.mm(pkn[0:64, 0:T], [(wukv[:, h * 128:h * 128 + 64], ckvn[:, :])], [wb, ckvn.b], [pkn.b])
                k_ = kst[cnt["k"] % 3]
                cnt["k"] += 1
                cx.copy("act", k_[:, :], pkn[0:64, 0:T], [pkn.b], [k_.b])
                cx.dma("sp", [(sq_["kmT"][h, 0:64, t0:t0 + T], k_[:, :])], k_.b, reads=[k_.b])
            for j in range(T // 128):
                pv = cx.ps()
                cx.mm(pv[:, 0:512].rearrange("p (h x) -> p h x", h=8), [(ckvn[:, j * 128:(j + 1) * 128], wukv_v)],
                      [wb, ckvn.b], [pv.b])
                v_ = vmst[j % 2]
                cx.copy("act", v_[:, :, 0:64], pv[:, 0:512].rearrange("p (h x) -> p h x", h=8), [pv.b], [v_.b])
                cx.dma("sp", [(sq_["vM"][t0 + j * 128:t0 + (j + 1) * 128, :, :], v_[:, :, :])], v_.b, reads=[v_.b])

    for (L_, sq__) in seqs:
        run_seq(L_, sq__)
    cx.barrier()


def rope_tables(L):
    rows = L // GRID_W
    row = np.repeat(np.arange(rows, dtype=np.float32), GRID_W)
    col = np.tile(np.arange(GRID_W, dtype=np.float32), rows)

    def tab(d_rot):
        n_freq = d_rot // 4
        inv = (np.float32(10000.0) ** (-np.arange(n_freq, dtype=np.float32) / np.float32(n_freq))).astype(np.float32)
        ang = np.concatenate([row[:, None] * inv, col[:, None] * inv], axis=-1).astype(np.float32)
        return np.cos(ang).astype(np.float32), np.sin(ang).astype(np.float32)

    cA, sA = tab(64)
    cM, sM = tab(32)
    idxA = (np.arange(128) % 64) // 2
    out = {}
    out["ropeA_cos"] = np.ascontiguousarray(cA[:, idxA].T)
    out["ropeA_sin"] = np.ascontiguousarray(sA[:, idxA].T)
    mc = np.ones((96, L), np.float32)
    ms = np.zeros((96, L), np.float32)
    idxM = np.arange(32) // 2
    mc[64:96] = cM[:, idxM].T
    ms[64:96] = sM[:, idxM].T
    out["ropeM_cos"] = mc
    out["ropeM_sin"] = ms
    return out


def attn_pass(cx, L, jobs):
    cx.phase_reset()
    QC = 512
    nq = L // QC
    nk = L // 128
    qt = [cx.alloc([128, L], BF16, "aq") for _ in range(2)]
    kt = [cx.alloc([128, L], BF16, "ak") for _ in range(2)]
    vt = [cx.alloc([128, nk, 128], BF16, "av") for _ in range(2)]
    pT = [cx.alloc([128, 2, QC], BF16, "pT") for _ in range(3)]
    rec = [cx.alloc([128, QC], F32, "rec") for _ in range(2)]
    rec2 = [cx.alloc([64, QC], F32, "rec2") for _ in range(2)]
    ost = [cx.alloc([64, QC], BF16, "ost") for _ in range(2)]
    ps_s = [(cx.psum[2 * i], cx.psum[2 * i + 1]) for i in range(3)]
    ps_o = [cx.psum[6], cx.psum[7]]

    kv_slot = {}
    state = {"kv": 0, "q": 0, "blk": 0, "oc": 0}

    def load_kv(job):
        if job["kvid"] in kv_slot:
            return
        s = state["kv"] % 2
        state["kv"] += 1
        d = job["d"]
        kp = [(kt[s][0:d, :], job["k"])]
        if d == 64:
            kp.append((kt[s][64:128, :], job["k"]))
        cx.dma("sp", kp, kt[s].b, writes=[kt[s].b])
        cx.dma("sp", [(vt[s][:, :, :], job["v"].rearrange("(c p) x -> p c x", p=128))], vt[s].b, writes=[vt[s].b])
        kv_slot[job["kvid"]] = s

    def load_q(ji):
        job = jobs[ji]
        s = ji % 2
        qp = [(qt[s][0:job["d"], :], job["q"])]
        if job["d"] == 64:
            qp.append((qt[s][64:128, :], job["q"]))
        cx.dma("sp", qp, qt[s].b, writes=[qt[s].b])

    load_kv(jobs[0])
    load_q(0)
    for ji, job in enumerate(jobs):
        d = job["d"]
        sc = job["scale"]
        if ji + 1 < len(jobs):
            load_kv(jobs[ji + 1])
            load_q(ji + 1)
        ks = kv_slot[job["kvid"]]
        K = kt[ks]
        V = vt[ks]
        Q = qt[ji % 2]
        for qc in range(nq):
            q0 = qc * QC
            po = ps_o[state["oc"] % 2]
            oslot = state["oc"] % 2
            state["oc"] += 1
            npair = nk // 2

            def s_step(pi):
                pa, pb = ps_s[(state["blk"] + pi) % 3]
                for half, pp in ((0, pa), (1, pb)):
                    kc = pi * 2 + half
                    r0 = 64 if (d == 64 and half == 1) else 0
                    cx.mm(pp[:, 0:QC], [(K[r0:r0 + d, kc * 128:(kc + 1) * 128], Q[r0:r0 + d, q0:q0 + QC])],
                          [K.b, Q.b], [pp.b])

            def e_step(pi):
                pa, pb = ps_s[(state["blk"] + pi) % 3]
                p_ = pT[(state["blk"] + pi) % 3]
                cx.act(p_[:, :, :].rearrange("p a q -> p (a q)"), cx.ps_pair_ap(pa), AF.Exp, [pa.b, pb.b], [p_.b],
                       scale=sc)

            def v_step(pi):
                p_ = pT[(state["blk"] + pi) % 3]
                for half in range(2):
                    kc = pi * 2 + half
                    cx.mm_acc(po[:, 0:QC], V[:, kc, :], p_[:, half, :], kc == 0, kc == nk - 1, [V.b, p_.b], [po.b])

            s_step(0)
            if npair > 1:
                s_step(1)
            for pi in range(npair):
                if pi + 2 < npair:
                    s_step(pi + 2)
                e_step(pi)
                v_step(pi)
            state["blk"] += npair
            r_ = rec[oslot]
            r2_ = rec2[oslot]
            o_ = ost[oslot]
            cx.recip("dve", r_[64:128, :], po[64:128, 0:QC], [po.b], [r_.b])
            cx.dma("sp", [(r2_[:, :], r_[64:128, :])], r2_.b, reads=[r_.b], writes=[r2_.b])
            cx.tt("dve", o_[:, :], po[0:64, 0:QC], r2_[:, :], ALU.mult, [po.b, r2_.b], [o_.b])
            cx.dma("sp", [(job["out"][:, q0:q0 + QC], o_[:, :])], o_.b, reads=[o_.b])
    cx.barrier()


def merge_pass(cx, seqs, l, W, consts):
    T = TT
    cx.phase_reset()
    wgt = cx.alloc([128, KC, 3072], BF16, "wgt")
    wbr = cx.alloc([128, 12, D], BF16, "wbr")
    wout = cx.alloc([128, KC, D], BF16, "wout")
    gt = cx.alloc([128, KC], F32, "gmix")
    wl = WLoader(cx)
    cb = [consts["cbuf"]]
    ones_ap = consts["ones"][:, :]
    wb = Buf("p4w")
    w_in = W["w_in"][l]
    cx.dma("sp", [(gt[:, :], W["g_mix"][l].rearrange("(k p) -> p k", p=128))], gt.b, writes=[gt.b], slow=True)
    for k in range(KC):
        for c0 in range(0, 3072, 1024):
            wl.load(wgt[:, k, c0:c0 + 1024], [wb], w_in[k * 128:(k + 1) * 128, 2720 + c0:2720 + c0 + 1024], 128, 1024,
                    scale_ap=gt[:, k:k + 1], scale_reads=[gt.b])
    for i in range(3):
        for k in range(4):
            wl.load(wbr[:, i * 4 + k, :], [wb], W["w_branch"][l][i][k * 128:(k + 1) * 128, :], 128, D)
    for k in range(KC):
        wl.load(wout[:, k, :], [wb], W["w_out"][l][k * 128:(k + 1) * 128, :], 128, D)
    wl.join([wb])

    xT = [cx.alloc([128, KC, T], F32, "xT") for _ in range(2)]
    yb = [[cx.alloc([128, 4, T], BF16, "y%d" % i) for _ in range(2)] for i in range(3)]
    uu = [cx.alloc([128, KC, T], BF16, "u") for _ in range(2)]
    sq = cx.alloc([128, KC, T], BF16, "sq")
    rstd = cx.alloc([128, T], F32, "rstd")
    sig = [cx.alloc([128, T], F32, "sig") for _ in range(6)]
    tm = [cx.alloc([128, T], F32, "tm") for _ in range(6)]
    mg = cx.alloc([128, KC, T], BF16, "mg")
    def run_seq(L, sq_):
        xav = sq_["xa"].rearrange("(k p) l -> p k l", p=128)
        xbv = sq_["xb"].rearrange("(k p) l -> p k l", p=128)
        ysrc = [sq_[n].rearrange("(k p) l -> p k l", p=128) for n in ("yaT", "ybT", "ycT")]
        nch = L // T

        def load_chunk(c):
            t0 = c * T
            xs = xT[c % 2]
            cx.dma("sp", [(xs[:, :, :], xav[:, :, t0:t0 + T])], xs.b, writes=[xs.b])
            for i in range(3):
                y_ = yb[i][c % 2]
                cx.dma("sp", [(y_[:, :, :], ysrc[i][:, :, t0:t0 + T])], y_.b, writes=[y_.b])

        def norm(c):
            xs_ = xT[c % 2]
            u_ = uu[c % 2]
            rms_stats(cx, xs_[:, :, :], xs_.b, sq, ones_ap, cb, KC, T, rstd, float(D),
                      lambda k: sq[:, :, :] if k is None else sq[:, k, :])
            cx.tt("dve", u_[:, :, :], xs_[:, :, :], rstd[:, 0:T].unsqueeze(1).to_broadcast([128, KC, T]),
                  ALU.mult, [xs_.b, rstd.b], [u_.b])

        load_chunk(0)
        norm(0)
        for c in range(nch):
            t0 = c * T
            xs = xT[c % 2]
            u = uu[c % 2]
            if c + 1 < nch:
                load_chunk(c + 1)
            for d in range(KC):
                if d == 4 and c + 1 < nch:
                    norm(c + 1)
                pbs = []
                for i in range(3):
                    y_ = yb[i][c % 2]
                    pb = cx.ps()
                    cx.mm(pb[:, 0:T], [(wbr[:, i * 4 + k, d * 128:(d + 1) * 128], y_[:, k, :]) for k in range(4)],
                          [wb, y_.b], [pb.b])
                    pg = cx.ps()
                    cx.mm(pg[:, 0:T], [(wgt[:, k, i * 1024 + d * 128:i * 1024 + (d + 1) * 128], u[:, k, :]) for k in range(KC)],
                          [wb, u.b], [pg.b])
                    sg = sig[(d % 2) * 3 + i]
                    tq = tm[(d % 2) * 3 + i]
                    cx.act(sg[:, :], pg[:, 0:T], AF.Sigmoid, [pg.b], [sg.b])
                    cx.tt("dve", tq[:, :], sg[:, :], pb[:, 0:T], ALU.mult, [sg.b, pb.b], [tq.b])
                t0_, t1_, t2_ = tm[(d % 2) * 3], tm[(d % 2) * 3 + 1], tm[(d % 2) * 3 + 2]
                cx.tt("pool", t0_[:, :], t0_[:, :], t1_[:, :], ALU.add, [t0_.b, t1_.b], [t0_.b])
                cx.tt("pool", mg[:, d, :], t0_[:, :], t2_[:, :], ALU.add, [t0_.b, t2_.b], [mg.b])
            for d2 in range(KC):
                po = cx.ps()
                cx.mm(po[:, 0:T], [(wout[:, d, d2 * 128:(d2 + 1) * 128], mg[:, d, :]) for d in range(KC)], [wb, mg.b], [po.b])
                cx.tt("dve", xs[:, d2, :], xs[:, d2, :], po[:, 0:T], ALU.add, [xs.b, po.b], [xs.b])
            cx.dma("sp", [(xbv[:, :, t0:t0 + T], xs[:, :, :])], xs.b, reads=[xs.b])

    for (L_, sq__) in seqs:
        run_seq(L_, sq__)
    cx.barrier()


HYW = 512


def hyena_tables(L):
    N = 2 * L
    N1 = N // 128
    out = {}
    t = np.linspace(0.0, 1.0, L, dtype=np.float32)
    w = (np.float32(2.0 * math.pi) * np.arange(L, dtype=np.float32) / np.float32(L)).astype(np.float32)
    f = np.linspace(1e-4, 15, 16, dtype=np.float32)[None, :]
    z = np.concatenate([t[:, None], np.cos(f * w[:, None]), -np.sin(f * w[:, None])], axis=-1).astype(np.float32)
    idx2 = (L - np.arange(L)) % L
    zfull = np.concatenate([z, z[idx2]], axis=0)
    tfull = np.concatenate([t, t[idx2]], axis=0).astype(np.float32)
    tfull[L] = 1.0e4
    out["zposT"] = np.ascontiguousarray(zfull.T)
    out["tvec"] = tfull
    max_decay = math.log(1e-2) / 0.3
    min_decay = math.log(1e-2) / 1.5
    deltas = np.abs(np.linspace(min_decay, max_decay, HYW, dtype=np.float32))
    out["negdelta"] = (-deltas).astype(np.float32)
    n1 = np.arange(N1, dtype=np.float64)
    a1 = 2.0 * np.pi * np.outer(n1, n1) / N1
    out["F1"] = np.concatenate([np.cos(a1), -np.sin(a1)], axis=1).astype(np.float32)
    n2 = np.arange(128, dtype=np.float64)
    at = 2.0 * np.pi * np.outer(n2, n1) / N
    out["Tc"] = np.cos(at).astype(np.float32)
    out["Ts"] = np.sin(at).astype(np.float32)
    out["T2c"] = np.ascontiguousarray(np.cos(at).T).astype(np.float32)
    out["T2s"] = np.ascontiguousarray(np.sin(at).T).astype(np.float32)
    a2 = 2.0 * np.pi * np.outer(n2, n2) / 128.0
    C = np.cos(a2)
    S = np.sin(a2)
    out["F2"] = np.concatenate([C, S, -S], axis=1).astype(np.float32)
    out["F2i"] = np.concatenate([C, S, -S, C], axis=1).astype(np.float32)
    out["G1"] = np.concatenate([np.cos(a1)[:, :N1 // 2], -np.sin(a1)[:, :N1 // 2]], axis=1).astype(np.float32)
    return out


def cmul(cx, Ar, Ai, a_reads, Tc, Ts, t_reads, sgn, outR, outI, o_writes, tmp):
    p1, p2, p3, p4 = tmp
    shp = list(Ar.shape)

    def v(t):
        ap = t[0:shp[0], 0:int(np.prod(shp[1:]))]
        if len(shp) == 3:
            ap = ap.rearrange("p (a b) -> p a b", a=shp[1])
        return ap
    cx.tt("dve", v(p1), Ar, Tc, ALU.mult, a_reads + t_reads, [p1.b])
    cx.tt("dve", v(p2), Ai, Ts, ALU.mult, a_reads + t_reads, [p2.b])
    cx.tt("pool", outR, v(p1), v(p2), ALU.subtract if sgn > 0 else ALU.add, [p1.b, p2.b], o_writes)
    cx.tt("dve", v(p3), Ai, Tc, ALU.mult, a_reads + t_reads, [p3.b])
    cx.tt("dve", v(p4), Ar, Ts, ALU.mult, a_reads + t_reads, [p4.b])
    cx.tt("pool", outI, v(p3), v(p4), ALU.add if sgn > 0 else ALU.subtract, [p3.b, p4.b], o_writes)


class HyFFT:
    def __init__(self, cx, L, tabs):
        self.cx = cx
        self.L = L
        N1 = 2 * L // 128
        self.N1 = N1
        self.CG = min(8, max(1, 512 // N1))
        self.cb = Buf("hytab")
        cb = self.cb
        wl = WLoader(cx, ncols=512, nst=2)
        self.F1 = cx.alloc([N1, 2 * N1], BF16, "F1")
        self.F2 = cx.alloc([128, 384], BF16, "F2")
        self.F2i = cx.alloc([128, 512], BF16, "F2i")
        self.G1 = cx.alloc([N1, N1], BF16, "G1")
        self.Tc = cx.alloc([128, N1], F32, "Tc")
        self.Ts = cx.alloc([128, N1], F32, "Ts")
        self.T2c = cx.alloc([N1, 128], F32, "T2c")
        self.T2s = cx.alloc([N1, 128], F32, "T2s")
        wl.load(self.F1[:, :], [cb], tabs["F1"], N1, 2 * N1, eng="dve")
        wl.load(self.F2[:, :], [cb], tabs["F2"], 128, 384, eng="dve")
        wl.load(self.F2i[:, :], [cb], tabs["F2i"], 128, 512, eng="dve")
        wl.load(self.G1[:, :], [cb], tabs["G1"], N1, N1, eng="dve")
        wl.join([cb])
        cx.dma("sp", [(self.Tc[:, :], tabs["Tc"]), (self.Ts[:, :], tabs["Ts"]),
                      (self.T2c[:, :], tabs["T2c"]), (self.T2s[:, :], tabs["T2s"])], self.Tc.b, writes=[cb])
        self.NS = 4
        self.NT = 8
        self.tmp = [[cx.alloc([128, 512], F32, "ctmp") for _ in range(4)] for _ in range(self.NT)]
        self.ti = 0
        self.Br = [cx.alloc([128, self.CG, N1], BF16, "Br") for _ in range(self.NS)]
        self.Bi = [cx.alloc([128, self.CG, N1], BF16, "Bi") for _ in range(self.NS)]

    def tmps(self):
        self.ti += 1
        return self.tmp[self.ti % self.NT]

    def stage1(self, xin, xb, nrows, c0, gi):
        cx = self.cx
        N1 = self.N1
        CG = self.CG
        cpb = max(1, min(CG, 512 // (2 * N1)))
        Br = self.Br[gi % self.NS]
        Bi = self.Bi[gi % self.NS]
        for b0 in range(0, CG, cpb):
            pa = cx.ps()
            for j in range(cpb):
                cx.mm(pa[:, j * 2 * N1:(j + 1) * 2 * N1], [(xin[0:nrows, c0 + b0 + j, :], self.F1[0:nrows, :])],
                      [xb, self.cb], [pa.b])
            A = pa[:, 0:cpb * 2 * N1].rearrange("p (c r k) -> p c r k", c=cpb, r=2)
            Tcb = self.Tc[:, :].unsqueeze(1).to_broadcast([128, cpb, N1])
            Tsb = self.Ts[:, :].unsqueeze(1).to_broadcast([128, cpb, N1])
            cmul(cx, A[:, :, 0, :], A[:, :, 1, :], [pa.b], Tcb, Tsb, [self.cb], -1,
                 Br[:, b0:b0 + cpb, :], Bi[:, b0:b0 + cpb, :], [Br.b], self.tmps())

    def stage3(self, gi):
        cx = self.cx
        N1 = self.N1
        CG = self.CG
        Br = self.Br[gi % self.NS]
        Bi = self.Bi[gi % self.NS]
        C = self.F2[:, 0:128]
        S = self.F2[:, 128:256]
        nS = self.F2[:, 256:384]
        brf = Br[:, :, :].rearrange("p c k -> p (c k)")
        bif = Bi[:, :, :].rearrange("p c k -> p (c k)")
        W = CG * N1
        pxr = cx.ps()
        cx.mm(pxr[:, 0:W], [(C, brf), (S, bif)], [Br.b, self.cb], [pxr.b])
        pxi = cx.ps()
        cx.mm(pxi[:, 0:W], [(C, bif), (nS, brf)], [Br.b, self.cb], [pxi.b])
        return pxr, pxi


def hyena_filter_pass(cx, L, l, sq_, W, consts, tabs):
    cx.phase_reset()
    N = 2 * L
    PC = 512
    npc = N // PC
    w1 = cx.alloc([33, 64], F32, "w1")
    w2 = cx.alloc([64, 64], F32, "w2")
    w3 = cx.alloc([64, 1024], F32, "w3")
    sm = cx.alloc([64, 8], F32, "hsm")
    nd = cx.alloc([128, 4], F32, "negd")
    bia = cx.alloc([128, 4], F32, "hbias")
    cb = Buf("hyw")
    cx.dma("sp", [(w1[:, :], W["w_hy_f1"][l]), (w2[:, :], W["w_hy_f2"][l]), (w3[:, :], W["w_hy_f3"][l])], w1.b, writes=[cb])
    cx.dma("sp", [(sm[:, 0:1], W["b_hy_f1"][l].rearrange("(p o) -> p o", o=1)),
                  (sm[:, 1:2], W["b_hy_f2"][l].rearrange("(p o) -> p o", o=1)),
                  (sm[:, 2:3], W["hy_sin_freq"][l].rearrange("(p o) -> p o", o=1)),
                  (nd[:, :], tabs["negdelta"].rearrange("(c p) -> p c", p=128)),
                  (bia[:, :], W["hy_bias"][l].rearrange("(c p) -> p c", p=128))], sm.b, writes=[cb], slow=True)
    cx.tt("dve", sm[:, 3:4], sm[:, 0:1], sm[:, 2:3], ALU.mult, [cb], [cb])
    cx.tt("dve", sm[:, 4:5], sm[:, 1:2], sm[:, 2:3], ALU.mult, [cb], [cb])
    h2all = cx.alloc([64, N], F32, "h2all")
    zt = [cx.alloc([33, PC], F32, "zt") for _ in range(2)]
    arg = [cx.alloc([64, PC], F32, "arg") for _ in range(2)]
    msk = [cx.alloc([64, PC], F32, "msk") for _ in range(2)]
    h1 = [cx.alloc([64, PC], F32, "h1") for _ in range(2)]
    PI = math.pi

    def sin_layer(ps, bcol, out_ap, out_b, i):
        a = arg[i % 2]
        m = msk[i % 2]
        cx.ts("dve", a[:, :], ps[0:64, 0:PC], sm[:, 2:3], sm[:, bcol:bcol + 1], ALU.mult, ALU.add, [ps.b, cb], [a.b])
        for _ in range(2):
            cx.ts("dve", m[:, :], a[:, :], PI, -2.0 * PI, ALU.is_gt, ALU.mult, [a.b], [m.b])
            cx.tt("dve", a[:, :], a[:, :], m[:, :], ALU.add, [a.b, m.b], [a.b])
            cx.ts("dve", m[:, :], a[:, :], -PI, 2.0 * PI, ALU.is_lt, ALU.mult, [a.b], [m.b])
            cx.tt("dve", a[:, :], a[:, :], m[:, :], ALU.add, [a.b, m.b], [a.b])
        cx.act(out_ap, a[:, :], AF.Sin, [a.b], [out_b])

    for pc in range(npc):
        j0 = pc * PC
        z_ = zt[pc % 2]
        cx.dma("sp", [(z_[:, :], tabs["zposT"][:, j0:j0 + PC])], z_.b, writes=[z_.b])
        p1 = cx.ps()
        cx.mm(p1[0:64, 0:PC], [(w1[:, :], z_[:, :])], [cb, z_.b], [p1.b])
        h_ = h1[pc % 2]
        sin_layer(p1, 3, h_[:, :], h_.b, pc)
        p2 = cx.ps()
        cx.mm(p2[0:64, 0:PC], [(w2[:, :], h_[:, :])], [cb, h_.b], [p2.b])
        sin_layer(p2, 4, h2all[:, j0:j0 + PC], h2all.b, pc + 1)

    tv = [cx.alloc([128, PC], F32, "tv") for _ in range(2)]
    win = [cx.alloc([128, PC], F32, "win") for _ in range(2)]
    kf = [cx.alloc([128, PC], F32, "kf") for _ in range(2)]
    kb = [cx.alloc([128, PC], BF16, "kb") for _ in range(2)]
    it = 0
    for pc in range(npc):
        j0 = pc * PC
        t_ = tv[pc % 2]
        cx.dma("sp", [(t_[:, :], tabs["tvec"][j0:j0 + PC].partition_broadcast(128))], t_.b, writes=[t_.b])
        half = 0 if j0 < L else 1
        for fc in range(4):
            p3 = cx.ps()
            cx.mm(p3[:, 0:PC], [(w3[:, half * 512 + fc * 128:half * 512 + (fc + 1) * 128], h2all[:, j0:j0 + PC])],
                  [cb, h2all.b], [p3.b])
            w_ = win[it % 2]
            cx.act(w_[:, :], t_[:, :], AF.Exp, [t_.b, cb], [w_.b], scale=nd[:, fc:fc + 1])
            k_ = kf[it % 2]
            cx.tt("dve", k_[:, :], p3[:, 0:PC], w_[:, :], ALU.mult, [p3.b, w_.b], [k_.b])
            if pc == 0:
                cx.tt("dve", k_[:, 0:1], k_[:, 0:1], bia[:, fc:fc + 1], ALU.add, [k_.b, cb], [k_.b])
            kb_ = kb[it % 2]
            cx.copy("pool", kb_[:, :], k_[:, :], [k_.b], [kb_.b])
            cx.dma("sp", [(sq_["kcT"][fc * 128:(fc + 1) * 128, j0:j0 + PC], kb_[:, :])], kb_.b, reads=[kb_.b])
            it += 1
    cx.barrier()


def hyena_spectrum_pass(cx, L, sq_, tabs):
    cx.phase_reset()
    ff = HyFFT(cx, L, tabs)
    N1 = ff.N1
    CG = ff.CG
    CB = 32
    xin = [cx.alloc([N1, CB, 128], BF16, "kin") for _ in range(2)]
    hst = [cx.alloc([128, 2, 512], BF16, "hst") for _ in range(2)]
    src = sq_["kcT"].rearrange("c (a b) -> a c b", b=128)
    inv_n = 1.0 / (2 * L)
    groups = []
    for cb0 in range(0, HYW, CB):
        for c0 in range(0, CB, CG):
            groups.append((cb0, c0))
    G = len(groups)
    Wd = CG * N1
    xcur = {}

    def st1(g):
        cb0, c0 = groups[g]
        if c0 == 0:
            x_ = xin[(cb0 // CB) % 2]
            cx.dma("sp", [(x_[:, :, :], src[:, cb0:cb0 + CB, :])], x_.b, writes=[x_.b])
            xcur[cb0] = x_
        x_ = xcur[cb0]
        ff.stage1(x_, x_.b, N1, c0, g)

    def st2(g):
        cb0, c0 = groups[g]
        pxr, pxi = ff.stage3(g)
        h_ = hst[g % 2]
        cx.act(h_[:, 0, 0:Wd], pxr[:, 0:Wd], AF.Copy, [pxr.b], [h_.b], scale=inv_n)
        cx.act(h_[:, 1, 0:Wd], pxi[:, 0:Wd], AF.Copy, [pxi.b], [h_.b], scale=inv_n)
        ch = cb0 + c0
        cx.dma("sp", [(sq_["Hr"][:, ch * N1:ch * N1 + Wd], h_[:, 0, 0:Wd]),
                      (sq_["Hi"][:, ch * N1:ch * N1 + Wd], h_[:, 1, 0:Wd])], h_.b, reads=[h_.b])

    for t in range(G + 1):
        if t < G:
            st1(t)
        if t >= 1:
            st2(t - 1)
    cx.barrier()


def hyena_conv_pass(cx, L, sq_, tabs):
    cx.phase_reset()
    ff = HyFFT(cx, L, tabs)
    N1 = ff.N1
    NH = N1 // 2
    CG = ff.CG
    CB = 32
    Wd = CG * N1
    xin = [cx.alloc([NH, CB, 128], BF16, "sin") for _ in range(2)]
    x0t = [cx.alloc([NH, CB, 128], F32, "x0in") for _ in range(2)]
    ht = [cx.alloc([128, 2, 512], BF16, "ht") for _ in range(4)]
    Yr = [cx.alloc([128, CG, N1], BF16, "Yr") for _ in range(4)]
    Yi = [cx.alloc([128, CG, N1], BF16, "Yi") for _ in range(4)]
    Dr = [cx.alloc([N1, CG, 128], BF16, "Dr") for _ in range(4)]
    Di = [cx.alloc([N1, CG, 128], BF16, "Di") for _ in range(4)]
    yst = [cx.alloc([NH, CG, 128], BF16, "yst") for _ in range(4)]
    ssrc = sq_["sT"].rearrange("c (a b) -> a c b", b=128)
    x0src = sq_["x0T"].rearrange("c (a b) -> a c b", b=128)
    ydst = sq_["ybT"].rearrange("c (a b) -> a c b", b=128)
    Ci = ff.F2i[:, 0:256]
    Si = ff.F2i[:, 256:512]
    groups = []
    for cb0 in range(0, HYW, CB):
        for c0 in range(0, CB, CG):
            groups.append((cb0, c0))
    G = len(groups)
    xcur = {}
    NS = ff.NS

    def st1(g):
        cb0, c0 = groups[g]
        if c0 == 0:
            x_ = xin[(cb0 // CB) % 2]
            x0_ = x0t[(cb0 // CB) % 2]
            cx.dma("sp", [(x_[:, :, :], ssrc[:, cb0:cb0 + CB, :])], x_.b, writes=[x_.b])
            cx.dma("sp", [(x0_[:, :, :], x0src[:, cb0:cb0 + CB, :])], x0_.b, writes=[x0_.b])
            xcur[cb0] = (x_, x0_)
        x_, x0_ = xcur[cb0]
        ch = cb0 + c0
        h_ = ht[g % NS]
        cx.dma("sp", [(h_[:, 0, 0:Wd], sq_["Hr"][:, ch * N1:ch * N1 + Wd]),
                      (h_[:, 1, 0:Wd], sq_["Hi"][:, ch * N1:ch * N1 + Wd])], h_.b, writes=[h_.b])
        ff.stage1(x_, x_.b, NH, c0, g)

    def st2(g):
        h_ = ht[g % NS]
        pxr, pxi = ff.stage3(g)
        yr = Yr[g % NS]
        yi = Yi[g % NS]
        cmul(cx, pxr[:, 0:Wd], pxi[:, 0:Wd], [pxr.b, pxi.b], h_[:, 0, 0:Wd], h_[:, 1, 0:Wd], [h_.b], +1,
             yr[:, :, :].rearrange("p c k -> p (c k)"), yi[:, :, :].rearrange("p c k -> p (c k)"), [yr.b], ff.tmps())

    def st3(g):
        yr = Yr[g % NS]
        yi = Yi[g % NS]
        dr = Dr[g % NS]
        di = Di[g % NS]
        for b0 in range(0, CG, 2):
            nb = min(2, CG - b0)
            pc_ = cx.ps()
            for j in range(nb):
                cx.mm(pc_[0:N1, j * 256:(j + 1) * 256], [(yr[:, b0 + j, :], Ci), (yi[:, b0 + j, :], Si)],
                      [yr.b, ff.cb], [pc_.b])
            Cv = pc_[0:N1, 0:nb * 256].rearrange("p (c r k) -> p c r k", c=nb, r=2)
            T2cb = ff.T2c[:, :].unsqueeze(1).to_broadcast([N1, nb, 128])
            T2sb = ff.T2s[:, :].unsqueeze(1).to_broadcast([N1, nb, 128])
            cmul(cx, Cv[:, :, 0, :], Cv[:, :, 1, :], [pc_.b], T2cb, T2sb, [ff.cb], +1,
                 dr[:, b0:b0 + nb, :], di[:, b0:b0 + nb, :], [dr.b], ff.tmps())

    def st4(g):
        cb0, c0 = groups[g]
        x_, x0_ = xcur[cb0]
        ch = cb0 + c0
        dr = Dr[g % NS]
        di = Di[g % NS]
        ys = yst[g % NS]
        for q0 in range(0, CG, 4):
            nq_ = min(4, CG - q0)
            py = cx.ps()
            cx.mm(py[0:NH, 0:nq_ * 128],
                  [(ff.G1[:, 0:NH], dr[:, q0:q0 + nq_, :].rearrange("p c k -> p (c k)")),
                   (ff.G1[:, NH:N1], di[:, q0:q0 + nq_, :].rearrange("p c k -> p (c k)"))], [dr.b, ff.cb], [py.b])
            cx.tt("dve", ys[:, q0:q0 + nq_, :], py[0:NH, 0:nq_ * 128].rearrange("p (c k) -> p c k", c=nq_),
                  x0_[:, c0 + q0:c0 + q0 + nq_, :], ALU.mult, [py.b, x0_.b], [ys.b])
        cx.dma("sp", [(ydst[:, ch:ch + CG, :], ys[:, :, :])], ys.b, reads=[ys.b])

    for t in range(G + 3):
        if t < G:
            st1(t)
        if 0 <= t - 1 < G:
            st2(t - 1)
        if 0 <= t - 2 < G:
            st3(t - 2)
        if 0 <= t - 3 < G:
            st4(t - 3)
    cx.barrier()


W_SHAPES = {
    "g_ffn1": [2, D], "w_ffn1_gate": [2, D, DFF], "w_ffn1_up": [2, D, DFF], "w_ffn1_down": [2, DFF, D],
    "g_mix": [2, D], "w_in": [2, D, IN_W], "g_qnorm": [2, 64], "g_knorm": [2, 64],
    "w_hy_short": [2, 3, 1536], "b_hy_short": [2, 1536], "w_hy_f1": [2, 33, 64], "b_hy_f1": [2, 64],
    "w_hy_f2": [2, 64, 64], "b_hy_f2": [2, 64], "w_hy_f3": [2, 64, 1024], "hy_sin_freq": [2, 64],
    "hy_bias": [2, 512], "g_mla_q": [2, 256], "w_mla_uq": [2, 256, 768], "g_mla_kv": [2, 128],
    "w_mla_ukv": [2, 128, 1024], "w_branch": [2, 3, 512, D], "w_out": [2, D, D],
    "g_ffn2": [2, D], "w_ffn2_gate": [2, D, DFF], "w_ffn2_up": [2, D, DFF], "w_ffn2_down": [2, DFF, D],
    "g_final": [D],
}
DEPTH = 2


def host_tables(L, tag):
    t = {}
    for k, v in rope_tables(L).items():
        t["%s_%s" % (tag, k)] = v
    for k, v in hyena_tables(L).items():
        t["%s_hy_%s" % (tag, k)] = v
    return t


def build_program(lens):
    from contextlib import ExitStack
    nc = bass.Bass("TRN2", target_bir_lowering=False)
    W = {n: nc.dram_tensor(n, shp, F32, kind="ExternalInput").ap() for n, shp in W_SHAPES.items()}
    ident = nc.dram_tensor("ident", [128, 128], F32, kind="ExternalInput").ap()
    seqs = []
    for tag, L in lens.items():
        N1 = 2 * L // 128
        sq = {"L": L, "tag": tag}
        sq["x"] = nc.dram_tensor("x_" + tag, [L, D], F32, kind="ExternalInput").ap()
        sq["y"] = nc.dram_tensor("y_" + tag, [L, D], F32, kind="ExternalOutput").ap()
        tabs = {}
        for k, v in host_tables(L, tag).items():
            ap = nc.dram_tensor(k, list(v.shape), F32, kind="ExternalInput").ap()
            short = k[len(tag) + 1:]
            if short.startswith("hy_"):
                tabs[short[3:]] = ap
            else:
                sq[short] = ap
        sq["tabs"] = tabs

        def scr(n, shape, dt):
            sq[n] = nc.dram_tensor("%s_%s" % (tag, n), shape, dt, kind="Internal").ap()
        for n in ("xa", "xb", "xc"):
            scr(n, [D, L], F32)
        scr("qT", [512, L], BF16)
        scr("kT", [128, L], BF16)
        scr("vA", [L, 2, 128], BF16)
        scr("x0T", [512, L], F32)
        scr("sT", [512, L], BF16)
        scr("qmT", [8, 96, L], BF16)
        scr("kmT", [8, 96, L], BF16)
        scr("vM", [L, 8, 128], BF16)
        for n in ("yaT", "ybT", "ycT"):
            scr(n, [512, L], BF16)
        scr("kcT", [512, 2 * L], BF16)
        scr("Hr", [128, 512 * N1], BF16)
        scr("Hi", [128, 512 * N1], BF16)
        seqs.append(sq)

    cx = Ctx(nc)
    consts = setup_consts(cx, ident)
    for l in range(DEPTH):
        last = (l == DEPTH - 1)
        if l == 0:
            ffn_pass(cx, [(s["L"], s["x"], s["xa"], None) for s in seqs], W["g_ffn1"][l], W["w_ffn1_gate"][l],
                     W["w_ffn1_up"][l], W["w_ffn1_down"][l], consts, src_tok=True)
        else:
            ffn_pass(cx, [(s["L"], s["xc"], s["xa"], None) for s in seqs], W["g_ffn1"][l], W["w_ffn1_gate"][l],
                     W["w_ffn1_up"][l], W["w_ffn1_down"][l], consts)
        inproj_pass(cx, [(s["L"], s) for s in seqs], l, W, consts)
        for s in seqs:
            L = s["L"]
            jobs = []
            for h in range(8):
                g = h // 4
                jobs.append(dict(q=s["qT"][h * 64:(h + 1) * 64, :], k=s["kT"][g * 64:(g + 1) * 64, :], v=s["vA"][:, g, :],
                                 d=64, scale=64 ** -0.5, out=s["yaT"][h * 64:(h + 1) * 64, :], kvid="a%d" % g))
            for h in range(8):
                jobs.append(dict(q=s["qmT"][h], k=s["kmT"][h], v=s["vM"][:, h, :], d=96, scale=96 ** -0.5,
                                 out=s["ycT"][h * 64:(h + 1) * 64, :], kvid="m%d" % h))
            attn_pass(cx, L, jobs)
            hyena_filter_pass(cx, L, l, s, W, consts, s["tabs"])
            hyena_spectrum_pass(cx, L, s, s["tabs"])
            hyena_conv_pass(cx, L, s, s["tabs"])
        merge_pass(cx, [(s["L"], s) for s in seqs], l, W, consts)
        if not last:
            ffn_pass(cx, [(s["L"], s["xb"], s["xc"], None) for s in seqs], W["g_ffn2"][l], W["w_ffn2_gate"][l],
                     W["w_ffn2_up"][l], W["w_ffn2_down"][l], consts)
        else:
            ffn_pass(cx, [(s["L"], s["xb"], None, s["y"]) for s in seqs], W["g_ffn2"][l], W["w_ffn2_gate"][l],
                     W["w_ffn2_up"][l], W["w_ffn2_down"][l], consts, final=W["g_final"])
    stack = ExitStack()
    cx.S.finalize(stack)
    with nc.Block() as block:
        cx.S.emit(block)
    stack.close()
    return nc, cx


_TABLE_CACHE = {}


def run_step(inputs, lens, n_cores, prompt_idx, sample_idx):
    nc, cx = build_program(lens)
    tabs = {}
    for tag, L in lens.items():
        key = (tag, L)
        if key not in _TABLE_CACHE:
            _TABLE_CACHE[key] = host_tables(L, tag)
        tabs.update(_TABLE_CACHE[key])
    ident = np.eye(128, dtype=np.float32)
    in_maps = []
    for c in range(n_cores):
        m = {n: np.ascontiguousarray(inputs[n], dtype=np.float32) for n in W_SHAPES}
        m["ident"] = ident
        m.update(tabs)
        m["x_p"] = np.ascontiguousarray(inputs["x_prompt"][prompt_idx[c]], dtype=np.float32)
        m["x_s"] = np.ascontiguousarray(inputs["x_sample"][sample_idx[c]], dtype=np.float32)
        in_maps.append(m)
    res = run_bass_kernel_spmd(nc, in_maps, core_ids=list(range(n_cores)))
    return res


def kernel(**inputs):
    xp = np.asarray(inputs["x_prompt"])
    xs = np.asarray(inputs["x_sample"])
    B, Lp, _ = xp.shape
    Bs, Ls, _ = xs.shape
    n = 8
    prompt_idx = [c % B for c in range(n)]
    sample_idx = [c % Bs for c in range(n)]
    res = run_step(inputs, {"p": Lp, "s": Ls}, n, prompt_idx, sample_idx)
    yp = np.stack([np.asarray(res.results[c]["y_p"], dtype=np.float32) for c in range(B)], axis=0)
    ys = np.stack([np.asarray(res.results[c]["y_s"], dtype=np.float32) for c in range(Bs)], axis=0)
    return (yp, ys)
```

```python
import math
import numpy as np
import concourse.bass as bass
import concourse.mybir as mybir
from concourse.bass_utils import run_bass_kernel_spmd

F32 = mybir.dt.float32
BF16 = mybir.dt.bfloat16
ALU = mybir.AluOpType
AF = mybir.ActivationFunctionType
AX = mybir.AxisListType

ENGINES = ("pe", "dve", "act", "pool", "sp")


class Buf:
    __slots__ = ("name", "writers", "readers", "dsem", "depoch", "dlast")

    def __init__(self, name):
        self.name = name
        self.writers = []
        self.readers = []
        self.dsem = None
        self.depoch = -1
        self.dlast = None


class Op:
    __slots__ = ("eng", "fn", "deps", "is_dma", "semkey", "val", "signal", "waits", "snap")

    def __init__(self, eng, fn, is_dma=False):
        self.eng = eng
        self.fn = fn
        self.deps = set()
        self.is_dma = is_dma
        self.semkey = None
        self.val = 0
        self.signal = False
        self.waits = []
        self.snap = None


class Sched:
    def __init__(self, nc):
        self.nc = nc
        self.ops = []
        self.last = {e: None for e in ENGINES}
        self.slot_count = []
        self.slot_free = []
        self.slot_used = []
        self.epoch = 0
        self.dma_since_barrier = []
        self.barbuf = Buf("BAR")

    def _track(self, op, reads, writes):
        for b in reads:
            op.deps.update(b.writers)
        for b in writes:
            op.deps.update(b.writers)
            op.deps.update(b.readers)
        for b in reads:
            b.readers.append(op)
        for b in writes:
            b.writers = [op]
            b.readers = []
        op.deps.discard(op)

    def op(self, eng, fn, reads=(), writes=()):
        o = Op(eng, fn)
        self._track(o, reads, writes)
        self.ops.append(o)
        if fn is not None:
            self.last[eng] = o
        return o

    def dma(self, eng, fn, sembuf, n, reads=(), writes=()):
        o = Op(eng, fn, is_dma=True)
        self._track(o, reads, writes)
        if sembuf.depoch != self.epoch:
            if self.slot_free:
                sembuf.dsem = self.slot_free.pop()
            else:
                sembuf.dsem = len(self.slot_count)
                self.slot_count.append(0)
            self.slot_used.append(sembuf.dsem)
            sembuf.depoch = self.epoch
            sembuf.dlast = None
        if sembuf.dlast is not None:
            o.deps.add(sembuf.dlast)
        self.slot_count[sembuf.dsem] += 16 * n
        sembuf.dlast = o
        o.semkey = ("D", sembuf.dsem)
        o.val = self.slot_count[sembuf.dsem]
        o.signal = True
        self.ops.append(o)
        self.dma_since_barrier.append(o)
        return o

    def barrier(self, memset_fn):
        x = Op("pool", memset_fn)
        for e in ENGINES:
            if self.last[e] is not None:
                x.deps.add(self.last[e])
        x.deps.update(self.dma_since_barrier)
        self.dma_since_barrier = []
        self.slot_free.extend(self.slot_used)
        self.slot_used = []
        self.epoch += 1
        self.ops.append(x)
        self.last["pool"] = x
        for e in ENGINES:
            if e == "pool":
                continue
            o = Op(e, None)
            o.deps.add(x)
            self.ops.append(o)
        return x

    def finalize(self, stack):
        nc = self.nc
        for o in self.ops:
            for d in o.deps:
                d.signal = True
        cnt = {e: 0 for e in ENGINES}
        known = {e: {} for e in ENGINES}
        snaps = {}
        for o in self.ops:
            E = o.eng
            k = known[E]
            waits = {}
            for d in o.deps:
                if d.is_dma:
                    key, val = d.semkey, d.val
                else:
                    if d.eng == "pe" and E == "pe":
                        continue
                    key, val = ("E", d.eng), d.val
                if k.get(key, 0) >= val:
                    continue
                if waits.get(key, 0) < val:
                    waits[key] = val
            for key, val in waits.items():
                if k.get(key, 0) < val:
                    k[key] = val
                sn = snaps.get((key, val))
                if sn is not None:
                    for k2, v2 in sn.items():
                        if k.get(k2, 0) < v2:
                            k[k2] = v2
            o.waits = list(waits.items())
            if o.is_dma:
                snaps[(o.semkey, o.val)] = dict(k)
            elif o.signal and o.fn is not None:
                cnt[E] += 1
                o.val = cnt[E]
                snaps[(("E", E), o.val)] = dict(k)
            elif o.signal and o.fn is None:
                raise RuntimeError("wait-only op cannot signal")
        self.esem = {e: stack.enter_context(nc.semaphore("s_" + e)) for e in ENGINES}
        self.dsem = [stack.enter_context(nc.semaphore("d%d" % i)) for i in range(len(self.slot_count))]
        self.stats = dict(cnt)

    def _sem(self, key):
        return self.esem[key[1]] if key[0] == "E" else self.dsem[key[1]]

    def emit(self, block):
        per = {e: [o for o in self.ops if o.eng == e] for e in ENGINES}

        def run(engobj, E):
            for o in per[E]:
                for key, val in o.waits:
                    engobj.wait_ge(self._sem(key), val)
                if o.fn is None:
                    continue
                if o.is_dma:
                    o.fn(engobj, self._sem(o.semkey))
                else:
                    ins = o.fn(engobj)
                    if o.signal:
                        ins.then_inc(self.esem[E], 1)

        @block.tensor
        def _(t):
            run(t, "pe")

        @block.vector
        def _(v):
            run(v, "dve")

        @block.scalar
        def _(s):
            run(s, "act")

        @block.gpsimd
        def _(g):
            run(g, "pool")

        @block.sync
        def _(s):
            run(s, "sp")


class Tile:
    __slots__ = ("t", "b", "shape")

    def __init__(self, t, b, shape):
        self.t = t
        self.b = b
        self.shape = shape

    def __getitem__(self, k):
        return self.t[k]


def dsize(dt):
    return mybir.dt.size(dt)


class Ctx:
    def __init__(self, nc):
        self.nc = nc
        self.S = Sched(nc)
        self.uid = 0
        self.base = 16640
        self.cur = 16640
        self.cap = 229376
        self.psum = []
        self.ps_pairs = {}
        for i in range(4):
            t2 = nc.alloc_psum_tensor("psd%d" % i, [128, 1024], F32)
            for h in range(2):
                nm = "ps%d" % (2 * i + h)
                self.psum.append(Tile(t2[:, h * 512:(h + 1) * 512], Buf(nm), [128, 512]))
            self.ps_pairs["ps%d" % (2 * i)] = t2[:, :]
        self.ps_rr = 0

    def alloc(self, shape, dtype, name="t"):
        per = dsize(dtype)
        for s in shape[1:]:
            per *= s
        per = (per + 31) // 32 * 32
        assert self.cur + per <= self.cap, ("SBUF overflow", name, self.cur, per)
        self.uid += 1
        nm = "%s_%d" % (name, self.uid)
        t = self.nc.alloc_sbuf_tensor_at(nm, list(shape), dtype, offset=self.cur)
        self.cur += per
        return Tile(t, Buf(nm), list(shape))

    def persist(self):
        self.base = self.cur

    def phase_reset(self):
        self.cur = self.base

    def ps(self):
        p = self.psum[self.ps_rr % 8]
        self.ps_rr += 1
        return p

    def mm(self, out_ap, pairs, reads, writes):
        pairs = list(pairs)

        def fn(e):
            n = len(pairs)
            ins = None
            for i, (l, r) in enumerate(pairs):
                ins = e.matmul(out_ap, l, r, start=(i == 0), stop=(i == n - 1))
            return ins
        return self.S.op("pe", fn, reads, writes)

    def mm_acc(self, out_ap, lhsT, rhs, start, stop, reads, writes):
        def fn(e):
            return e.matmul(out_ap, lhsT, rhs, start=start, stop=stop)
        return self.S.op("pe", fn, reads, writes)

    def ps_pair_ap(self, pa):
        return self.ps_pairs[pa.b.name]

    def transposes(self, items, ident_ap, reads, writes):
        items = list(items)

        def fn(e):
            ins = None
            for o, i in items:
                ins = e.transpose(o, i, ident_ap)
            return ins
        return self.S.op("pe", fn, reads, writes)

    def act(self, out_ap, in_ap, func, reads, writes, bias=0.0, scale=1.0, eng="act"):
        def fn(e):
            return e.activation(out=out_ap, in_=in_ap, func=func, bias=bias, scale=scale)
        return self.S.op("act", fn, reads, writes)

    def tt(self, eng, out_ap, a_ap, b_ap, op, reads, writes):
        def fn(e):
            return e.tensor_tensor(out=out_ap, in0=a_ap, in1=b_ap, op=op)
        return self.S.op(eng, fn, reads, writes)

    def ts(self, eng, out_ap, a_ap, s1, s2, op0, op1, reads, writes):
        def fn(e):
            if op1 is None:
                return e.tensor_scalar(out=out_ap, in0=a_ap, scalar1=s1, scalar2=None, op0=op0)
            return e.tensor_scalar(out=out_ap, in0=a_ap, scalar1=s1, scalar2=s2, op0=op0, op1=op1)
        return self.S.op(eng, fn, reads, writes)

    def stt(self, eng, out_ap, a_ap, s, b_ap, op0, op1, reads, writes):
        def fn(e):
            return e.scalar_tensor_tensor(out=out_ap, in0=a_ap, scalar=s, in1=b_ap, op0=op0, op1=op1)
        return self.S.op(eng, fn, reads, writes)

    def copy(self, eng, out_ap, in_ap, reads, writes):
        if eng == "act":
            def fn(e):
                return e.activation(out=out_ap, in_=in_ap, func=AF.Copy)
        else:
            def fn(e):
                return e.tensor_copy(out=out_ap, in_=in_ap)
        return self.S.op(eng, fn, reads, writes)

    def recip(self, eng, out_ap, in_ap, reads, writes):
        def fn(e):
            return e.reciprocal(out=out_ap, in_=in_ap)
        return self.S.op(eng, fn, reads, writes)

    def memset(self, eng, ap, val, writes):
        def fn(e):
            return e.memset(ap, val)
        return self.S.op(eng, fn, (), writes)

    def dma(self, q, pairs, sembuf, reads=(), writes=(), slow=False):
        pairs = list(pairs)

        def fn(e, sem):
            for o, i in pairs:
                if slow:
                    e.dma_start(out=o, in_=i, allow_slow_non_contiguous=True).then_inc(sem, 16)
                else:
                    e.dma_start(out=o, in_=i).then_inc(sem, 16)
        return self.S.dma(q, fn, sembuf, len(pairs), reads, writes)

    def barrier(self):
        bt = self.bar_tile

        def fn(e):
            return e.memset(bt[:, :], 0.0)
        self.S.barrier(fn)


D = 1024
KC = 8
DFF = 2816
FC = 22
EPS = 1e-6
TT = 256
IN_W = 5792
GRID_W = 64


class WLoader:
    def __init__(self, cx, ncols=1024, nst=3):
        self.cx = cx
        self.ncols = ncols
        self.st = [cx.alloc([128, ncols], F32, "wst") for _ in range(nst)]
        self.i = 0
        self.engs = ("dve", "pool", "act")
        self.parts = []
        self.jt = cx.alloc([128, 8], F32, "wjoin")

    def join(self, targets):
        jt = self.jt

        def fn(e):
            return e.memset(jt[:, :], 0.0)
        self.cx.S.op("dve", fn, list(self.parts), list(targets) + [jt.b])
        self.parts = []

    def load(self, dst_ap, dst_buf_writes, src_ap, rows, cols, scale_ap=None, scale_reads=(), eng=None, negate=False):
        cx = self.cx
        st = self.st[self.i % len(self.st)]
        if eng is None:
            eng = self.engs[self.i % 3]
        self.i += 1
        cx.dma("sp", [(st[0:rows, 0:cols], src_ap)], st.b, writes=[st.b])
        src = st[0:rows, 0:cols]
        pb = Buf("wpart")
        self.parts.append(pb)
        dst_buf_writes = [pb]
        if scale_ap is not None:
            if eng == "act":
                def fn(e):
                    return e.activation(out=dst_ap, in_=src, func=AF.Copy, scale=scale_ap)
            elif eng == "pool":
                shp = list(dst_ap.shape)

                def fn(e):
                    return e.tensor_tensor(out=dst_ap, in0=src, in1=scale_ap.to_broadcast(shp), op=ALU.mult)
            else:
                def fn(e):
                    return e.tensor_scalar(out=dst_ap, in0=src, scalar1=scale_ap, scalar2=None, op0=ALU.mult)
            cx.S.op(eng, fn, [st.b] + list(scale_reads), dst_buf_writes)
        else:
            cx.copy(eng, dst_ap, src, [st.b], dst_buf_writes)
        return st


def rms_stats(cx, x_ap_flat, x_buf, sq, ones_ap, const_reads, nk, T, rstd, dim, sq_view):
    cx.tt("pool", sq_view(None), x_ap_flat, x_ap_flat, ALU.mult, [x_buf], [sq.b])
    ps = cx.ps()
    cx.mm(ps[:, 0:T], [(ones_ap, sq_view(k)) for k in range(nk)], [sq.b] + list(const_reads), [ps.b])
    rsqrt_from_psum(cx, rstd[:, 0:T], rstd.b, ps[:, 0:T], ps.b, float(dim))


def rsqrt_from_psum(cx, out_ap, out_b, ps_ap, ps_b, dim):
    npart = out_ap.shape[0]
    cx.act(out_ap, ps_ap, AF.Ln, [ps_b, cx.consts["cbuf"]], [out_b], bias=cx.consts["eps"][0:npart, 0:1], scale=1.0 / dim)
    cx.act(out_ap, out_ap, AF.Exp, [out_b], [out_b], scale=-0.5)


def ffn_pass(cx, seqs, g_ap, wg_ap, wu_ap, wd_ap, consts, src_tok=False, final=None):
    T = TT
    cx.phase_reset()
    wg = cx.alloc([128, KC, DFF], BF16, "wg")
    wu = cx.alloc([128, KC, DFF], BF16, "wu")
    wd = cx.alloc([128, FC, D], BF16, "wd")
    gt = cx.alloc([128, KC], F32, "g")
    xT = [cx.alloc([128, KC, T], F32, "xT") for _ in range(2)]
    uu = [cx.alloc([128, KC, T], BF16, "u") for _ in range(2)]
    sq = cx.alloc([128, KC, T], BF16, "sq")
    rstd = cx.alloc([128, T], F32, "rstd")
    hT = cx.alloc([128, FC, T], BF16, "hT")
    sil = [cx.alloc([128, T], F32, "sil") for _ in range(2)]
    need_tok = src_tok or (final is not None)
    src = dst = None
    if need_tok:
        xtok = [cx.alloc([128, D], F32, "xtok") for _ in range(2)]
    if final is not None:
        gf = cx.alloc([128, KC], F32, "gf")
        xn = cx.alloc([128, KC, T], F32, "xn")
    wl = WLoader(cx)
    ones_ap = consts["ones"][:, :]
    ident_ap = consts["ident"][:, :]
    cb = [consts["cbuf"]]

    cx.dma("sp", [(gt[:, :], g_ap.rearrange("(k p) -> p k", p=128))], gt.b, writes=[gt.b], slow=True)
    if final is not None:
        cx.dma("sp", [(gf[:, :], final.rearrange("(k p) -> p k", p=128))], gf.b, writes=[gf.b], slow=True)
    for k in range(KC):
        for c0 in range(0, DFF, 1024):
            cw = min(1024, DFF - c0)
            wl.load(wg[:, k, c0:c0 + cw], [wg.b], wg_ap[k * 128:(k + 1) * 128, c0:c0 + cw], 128, cw,
                    scale_ap=gt[:, k:k + 1], scale_reads=[gt.b])
            wl.load(wu[:, k, c0:c0 + cw], [wu.b], wu_ap[k * 128:(k + 1) * 128, c0:c0 + cw], 128, cw,
                    scale_ap=gt[:, k:k + 1], scale_reads=[gt.b])
    for f in range(FC):
        wl.load(wd[:, f, :], [wd.b], wd_ap[f * 128:(f + 1) * 128, :], 128, D)
    wl.join([wg.b, wu.b, wd.b])

    def run_seq(L, src, dst, out_tok):
        nch = L // T
        srcv = None if src_tok else src.rearrange("(k p) l -> p k l", p=128)
        dstv = None if dst is None else dst.rearrange("(k p) l -> p k l", p=128)

        def load_chunk(c):
            xs = xT[c % 2]
            t0 = c * T
            if not src_tok:
                cx.dma("sp", [(xs[:, :, :], srcv[:, :, t0:t0 + T])], xs.b, writes=[xs.b])
            else:
                for j in range(T // 128):
                    xk = xtok[j % 2]
                    cx.dma("sp", [(xk[:, :], src[t0 + j * 128:t0 + (j + 1) * 128, :])], xk.b, writes=[xk.b])
                for kk in range(KC // 2):
                    ps = cx.ps()
                    items = []
                    for k2 in range(2):
                        k = kk * 2 + k2
                        for j in range(T // 128):
                            items.append((ps[:, k2 * T + j * 128:k2 * T + (j + 1) * 128],
                                          xtok[j % 2][:, k * 128:(k + 1) * 128]))
                    cx.transposes(items, ident_ap, [xtok[0].b, xtok[1].b] + cb, [ps.b])
                    cx.copy("act", xs[:, kk * 2:kk * 2 + 2, :], ps[:, 0:2 * T].rearrange("p (a t) -> p a t", a=2),
                            [ps.b], [xs.b])

        def norm(c):
            xs_ = xT[c % 2]
            u_ = uu[c % 2]
            rms_stats(cx, xs_[:, :, :], xs_.b, sq, ones_ap, cb, KC, T, rstd, float(D),
                      lambda k: sq[:, :, :] if k is None else sq[:, k, :])
            cx.tt("dve", u_[:, :, :], xs_[:, :, :], rstd[:, 0:T].unsqueeze(1).to_broadcast([128, KC, T]),
                  ALU.mult, [xs_.b, rstd.b], [u_.b])

        load_chunk(0)
        norm(0)
        for c in range(nch):
            xs = xT[c % 2]
            u = uu[c % 2]
            t0 = c * T
            if c + 1 < nch:
                load_chunk(c + 1)
            for f in range(FC):
                pg = cx.ps()
                cx.mm(pg[:, 0:T], [(wg[:, k, f * 128:(f + 1) * 128], u[:, k, :]) for k in range(KC)],
                      [wg.b, u.b], [pg.b])
                pu = cx.ps()
                cx.mm(pu[:, 0:T], [(wu[:, k, f * 128:(f + 1) * 128], u[:, k, :]) for k in range(KC)],
                      [wu.b, u.b], [pu.b])
                s = sil[f % 2]
                cx.act(s[:, :], pg[:, 0:T], AF.Silu, [pg.b], [s.b])
                cx.tt("dve", hT[:, f, :], s[:, :], pu[:, 0:T], ALU.mult, [s.b, pu.b], [hT.b])
                if f == FC // 2 and c + 1 < nch:
                    norm(c + 1)
            for d in range(KC):
                pd = cx.ps()
                cx.mm(pd[:, 0:T], [(wd[:, f, d * 128:(d + 1) * 128], hT[:, f, :]) for f in range(FC)],
                      [wd.b, hT.b], [pd.b])
                cx.stt("dve", xs[:, d, :], pd[:, 0:T], 0.5, xs[:, d, :], ALU.mult, ALU.add, [pd.b, xs.b], [xs.b])
            if final is None:
                cx.dma("sp", [(dstv[:, :, t0:t0 + T], xs[:, :, :])], xs.b, reads=[xs.b])
            else:
                rms_stats(cx, xs[:, :, :], xs.b, sq, ones_ap, cb, KC, T, rstd, float(D),
                          lambda k: sq[:, :, :] if k is None else sq[:, k, :])
                for k in range(KC):
                    cx.stt("dve", xn[:, k, :], xs[:, k, :], gf[:, k:k + 1], rstd[:, 0:T], ALU.mult, ALU.mult,
                           [xs.b, gf.b, rstd.b], [xn.b])
                for j in range(T // 128):
                    xk = xtok[j % 2]
                    for hb in range(2):
                        ps = cx.ps()
                        items = [(ps[:, q * 128:(q + 1) * 128], xn[:, hb * 4 + q, j * 128:(j + 1) * 128])
                                 for q in range(4)]
                        cx.transposes(items, ident_ap, [xn.b] + cb, [ps.b])
                        cx.copy("act", xk[:, hb * 512:(hb + 1) * 512], ps[:, :], [ps.b], [xk.b])
                    cx.dma("sp", [(out_tok[t0 + j * 128:t0 + (j + 1) * 128, :], xk[:, :])], xk.b, reads=[xk.b])

    for (L_, src_, dst_, out_) in seqs:
        run_seq(L_, src_, dst_, out_)
    cx.barrier()


def setup_consts(cx, ident_dram):
    c = {}
    ones = cx.alloc([128, 128], BF16, "ones")
    ident = cx.alloc([128, 128], F32, "ident")
    blk = cx.alloc([128, 128], BF16, "blk64")
    bar = cx.alloc([128, 8], F32, "bar")
    mhalf = cx.alloc([128, 8], F32, "mhalf")
    c["mhalf"] = mhalf
    epst = cx.alloc([128, 8], F32, "eps")
    c["eps"] = epst
    cx.consts = c
    cx.bar_tile = bar
    cbuf = Buf("consts")
    cx.memset("pool", ones[:, :], 1.0, [cbuf])
    cx.memset("pool", blk[:, :], 0.0, [cbuf])
    cx.memset("pool", mhalf[:, :], -0.5, [cbuf])
    cx.memset("pool", epst[:, :], EPS, [cbuf])
    cx.memset("pool", blk[0:64, 0:64], 1.0, [cbuf])
    cx.memset("pool", blk[64:128, 64:128], 1.0, [cbuf])
    cx.dma("sp", [(ident[:, :], ident_dram)], ident.b, writes=[cbuf])
    c["ones"] = ones
    c["ident"] = ident
    c["blk64"] = blk
    c["cbuf"] = cbuf
    cx.persist()
    return c


def small_vec_load(cx, dst_tile, src_ap, writes):
    cx.dma("sp", [(dst_tile, src_ap)], writes[0], writes=writes, slow=True)


def inproj_pass(cx, seqs, l, W, consts):
    T = TT
    TH = T + 2
    cx.phase_reset()
    NA = 2720
    wA = cx.alloc([128, KC, NA], BF16, "wA")
    wR = cx.alloc([128, KC, 672], BF16, "wR")
    wuq = cx.alloc([128, 2, 768], BF16, "wuq")
    wuqr = cx.alloc([128, 2, 768], BF16, "wuqr")
    wukv = cx.alloc([128, 1024], BF16, "wukv")
    gt = cx.alloc([128, KC], F32, "gmix")
    gsm = cx.alloc([128, 8], F32, "gsm")
    hw = cx.alloc([128, 12, 4], F32, "hw")
    wl = WLoader(cx)
    cb = [consts["cbuf"]]
    ones_ap = consts["ones"][:, :]
    blk_ap = consts["blk64"][:, :]
    wb = Buf("p2w")

    w_in = W["w_in"][l]
    cx.dma("sp", [(gt[:, :], W["g_mix"][l].rearrange("(k p) -> p k", p=128))], gt.b, writes=[gt.b], slow=True)
    gq = W["g_qnorm"][l]
    gk = W["g_knorm"][l]
    pairs = []
    for half in range(2):
        p0 = half * 64
        pairs.append((gsm[p0:p0 + 64, 0:1], gq.rearrange("(p o) -> p o", o=1)))
        pairs.append((gsm[p0:p0 + 64, 2:3], gk.rearrange("(p o) -> p o", o=1)))
        gq2 = gq.rearrange("(i two) -> i two", two=2)
        gk2 = gk.rearrange("(i two) -> i two", two=2)
        pairs.append((gsm[p0:p0 + 64:2, 1:2], gq2[:, 1:2]))
        pairs.append((gsm[p0 + 1:p0 + 64:2, 1:2], gq2[:, 0:1]))
        pairs.append((gsm[p0:p0 + 64:2, 3:4], gk2[:, 1:2]))
        pairs.append((gsm[p0 + 1:p0 + 64:2, 3:4], gk2[:, 0:1]))
    pairs.append((gsm[:, 4:6], W["g_mla_q"][l].rearrange("(k p) -> p k", p=128)))
    pairs.append((gsm[:, 6:7], W["g_mla_kv"][l].rearrange("(p o) -> p o", o=1)))
    cx.dma("sp", pairs, gsm.b, writes=[gsm.b], slow=True)
    hp = [(hw[:, :, j], W["w_hy_short"][l][j].rearrange("(c p) -> p c", p=128)) for j in range(3)]
    hp.append((hw[:, :, 3], W["b_hy_short"][l].rearrange("(c p) -> p c", p=128)))
    cx.dma("sp", hp, hw.b, writes=[hw.b], slow=True)

    for k in range(KC):
        for c0 in range(0, NA, 1024):
            cw = min(1024, NA - c0)
            wl.load(wA[:, k, c0:c0 + cw], [wb], w_in[k * 128:(k + 1) * 128, c0:c0 + cw], 128, cw,
                    scale_ap=gt[:, k:k + 1], scale_reads=[gt.b])
    wl.join([wb])
    for k in range(KC):
        for (s0, n, d0) in ((0, 640, 0), (2688, 32, 640)):
            src = wA[:, k, s0:s0 + n].rearrange("p (i two) -> p i two", two=2)
            dst = wR[:, k, d0:d0 + n].rearrange("p (i two) -> p i two", two=2)
            cx.ts("dve", dst[:, :, 0], src[:, :, 1], -1.0, None, ALU.mult, None, [wb], [wb])
            cx.copy("dve", dst[:, :, 1], src[:, :, 0], [wb], [wb])
    for i in range(2):
        wl.load(wuq[:, i, :], [wb], W["w_mla_uq"][l][i * 128:(i + 1) * 128, :], 128, 768,
                scale_ap=gsm[:, 4 + i:5 + i], scale_reads=[gsm.b])
        wl.join([wb])
        cx.copy("dve", wuqr[:, i, :], wuq[:, i, :], [wb], [wb])
        for h in range(8):
            src = wuq[:, i, h * 96 + 64:h * 96 + 96].rearrange("p (i two) -> p i two", two=2)
            dst = wuqr[:, i, h * 96 + 64:h * 96 + 96].rearrange("p (i two) -> p i two", two=2)
            cx.ts("dve", dst[:, :, 0], src[:, :, 1], -1.0, None, ALU.mult, None, [wb], [wb])
            cx.copy("dve", dst[:, :, 1], src[:, :, 0], [wb], [wb])
    wl.load(wukv[:, :], [wb], W["w_mla_ukv"][l][:, :], 128, 1024, scale_ap=gsm[:, 6:7], scale_reads=[gsm.b])
    wl.join([wb])
    wukv_v = wukv[:, :].rearrange("p (h x) -> p h x", x=128)[:, :, 64:128]

    xh = [cx.alloc([128, KC, TH], F32, "xh") for _ in range(2)]
    us = [cx.alloc([128, KC, TH], BF16, "u") for _ in range(2)]
    sqs = [cx.alloc([128, KC, TH], BF16, "sq") for _ in range(2)]
    rstds = [cx.alloc([128, TH], F32, "rstd") for _ in range(2)]
    sq = cx.alloc([128, 2, T], BF16, "sqc")
    rstd = cx.alloc([128, T], F32, "rstdc")
    sqk = cx.alloc([128, T], BF16, "sqk")
    rstdk = cx.alloc([128, T], F32, "rstdk")
    tabA = [cx.alloc([128, 2, T], F32, "tabA") for _ in range(2)]
    tabM = [cx.alloc([96, 2, T], F32, "tabM") for _ in range(2)]
    tabK = [cx.alloc([32, 2, T], F32, "tabK") for _ in range(2)]
    qs = [cx.alloc([128, T], F32, "qs") for _ in range(4)]
    sq2 = [cx.alloc([128, T], BF16, "sq2") for _ in range(4)]
    r2 = [cx.alloc([128, T], F32, "r2") for _ in range(4)]
    ta = [cx.alloc([128, T], F32, "ta") for _ in range(4)]
    tb = [cx.alloc([128, T], F32, "tb") for _ in range(4)]
    qst = [cx.alloc([128, T], BF16, "qst") for _ in range(4)]
    vst = [cx.alloc([128, 2, 128], BF16, "vst") for _ in range(2)]
    hst = [cx.alloc([128, T], F32, "hst") for _ in range(3)]
    x1k = cx.alloc([128, 4, T], F32, "x1k")
    sst = [cx.alloc([128, T], BF16, "sst") for _ in range(2)]
    cqs = cx.alloc([128, 2, T], F32, "cqs")
    cqn = cx.alloc([128, 2, T], BF16, "cqn")
    ckvs = cx.alloc([128, T], F32, "ckvs")
    ckvn = cx.alloc([128, T], BF16, "ckvn")
    kro = [cx.alloc([32, T], BF16, "kro") for _ in range(2)]
    kst = [cx.alloc([64, T], BF16, "kst") for _ in range(3)]
    vmst = [cx.alloc([128, 8, 128], BF16, "vmst") for _ in range(2)]
    for t_ in vst:
        cx.memset("pool", t_[:, :, 64:128], 1.0, [t_.b])
    for t_ in vmst:
        cx.memset("pool", t_[:, :, 64:128], 1.0, [t_.b])

    def run_seq(L, sq_):
        xav = sq_["xa"].rearrange("(k p) l -> p k l", p=128)
        nch = L // T
        cnt = {"q": 0, "h": 0, "s": 0, "k": 0}

        def load_chunk(c):
            xs = xh[c % 2]
            t0 = c * T
            lo = max(t0 - 1, 0)
            hi = min(t0 + T + 1, L)
            d0 = lo - (t0 - 1)
            if c == 0:
                cx.memset("pool", xs[:, :, 0:1], 0.0, [xs.b])
            if c == nch - 1:
                cx.memset("pool", xs[:, :, TH - 1:TH], 0.0, [xs.b])
            cx.dma("sp", [(xs[:, :, d0:d0 + (hi - lo)], xav[:, :, lo:hi])], xs.b, writes=[xs.b])
            ta_ = tabA[c % 2]
            cx.dma("sp", [(ta_[:, 0, :], sq_["ropeA_cos"][:, t0:t0 + T]), (ta_[:, 1, :], sq_["ropeA_sin"][:, t0:t0 + T])],
                   ta_.b, writes=[ta_.b])
            tm_ = tabM[c % 2]
            cx.dma("sp", [(tm_[:, 0, :], sq_["ropeM_cos"][:, t0:t0 + T]), (tm_[:, 1, :], sq_["ropeM_sin"][:, t0:t0 + T])],
                   tm_.b, writes=[tm_.b])
            tk_ = tabK[c % 2]
            cx.dma("sp", [(tk_[:, 0, :], sq_["ropeM_cos"][64:96, t0:t0 + T]), (tk_[:, 1, :], sq_["ropeM_sin"][64:96, t0:t0 + T])],
                   tk_.b, writes=[tk_.b])

        def norm(c):
            xs_ = xh[c % 2]
            sq_n = sqs[c % 2]
            rs_n = rstds[c % 2]
            u_n = us[c % 2]
            rms_stats(cx, xs_[:, :, :], xs_.b, sq_n, ones_ap, cb, KC, TH, rs_n, float(D),
                      lambda k: sq_n[:, :, :] if k is None else sq_n[:, k, :])
            cx.tt("dve", u_n[:, :, :], xs_[:, :, :], rs_n[:, 0:TH].unsqueeze(1).to_broadcast([128, KC, TH]),
                  ALU.mult, [xs_.b, rs_n.b], [u_n.b])

        load_chunk(0)
        norm(0)
        for c in range(nch):
            xs = xh[c % 2]
            u = us[c % 2]
            t0 = c * T
            if c + 1 < nch:
                load_chunk(c + 1)
            tA = tabA[c % 2]
            tM = tabM[c % 2]
            tK = tabK[c % 2]

            def uc(k):
                return u[:, k, 1:T + 1]

            st_q = {}

            def qk_a(ci):
                col0 = ci * 128
                i2 = (cnt["q"] + ci) % 4
                pq = cx.ps()
                cx.mm(pq[:, 0:T], [(wA[:, k, col0:col0 + 128], uc(k)) for k in range(KC)], [wb, u.b], [pq.b])
                pr = cx.ps()
                cx.mm(pr[:, 0:T], [(wR[:, k, col0:col0 + 128], uc(k)) for k in range(KC)], [wb, u.b], [pr.b])
                q_ = qs[i2]
                cx.copy("act", q_[:, :], pq[:, 0:T], [pq.b], [q_.b])
                s2 = sq2[i2]
                cx.tt("pool", s2[:, :], q_[:, :], q_[:, :], ALU.mult, [q_.b], [s2.b])
                st_q[ci] = (pr, q_, s2, i2)

            def qk_b(ci):
                col0 = ci * 128
                gcol = 0 if ci < 4 else 2
                pr, q_, s2, i2 = st_q[ci]
                pss = cx.ps()
                cx.mm(pss[:, 0:T], [(blk_ap, s2[:, :])], [s2.b] + cb, [pss.b])
                r_ = r2[i2]
                rsqrt_from_psum(cx, r_[:, :], r_.b, pss[:, 0:T], pss.b, 64.0)
                a_ = ta[i2]
                b_ = tb[i2]
                cx.stt("dve", a_[:, :], q_[:, :], gsm[:, gcol:gcol + 1], r_[:, :], ALU.mult, ALU.mult,
                       [q_.b, gsm.b, r_.b], [a_.b])
                cx.stt("dve", b_[:, :], pr[:, 0:T], gsm[:, gcol + 1:gcol + 2], r_[:, :], ALU.mult, ALU.mult,
                       [pr.b, gsm.b, r_.b], [b_.b])
                cx.tt("pool", a_[:, :], a_[:, :], tA[:, 0, :], ALU.mult, [a_.b, tA.b], [a_.b])
                cx.tt("pool", b_[:, :], b_[:, :], tA[:, 1, :], ALU.mult, [b_.b, tA.b], [b_.b])
                o_ = qst[i2]
                cx.tt("dve", o_[:, :], a_[:, :], b_[:, :], ALU.add, [a_.b, b_.b], [o_.b])
                dst = sq_["qT"][col0:col0 + 128, t0:t0 + T] if ci < 4 else sq_["kT"][:, t0:t0 + T]
                cx.dma("sp", [(dst, o_[:, :])], o_.b, reads=[o_.b])

            qk_a(0)
            for ci in range(1, 5):
                qk_a(ci)
                qk_b(ci - 1)

            for i in range(2):
                pc = cx.ps()
                cx.mm(pc[:, 0:T], [(wA[:, k, 2304 + i * 128:2304 + (i + 1) * 128], uc(k)) for k in range(KC)],
                      [wb, u.b], [pc.b])
                cx.copy("act", cqs[:, i, :], pc[:, 0:T], [pc.b], [cqs.b])
            sqc = sq
            cx.tt("pool", sqc[:, 0:2, 0:T], cqs[:, :, :], cqs[:, :, :], ALU.mult, [cqs.b], [sq.b])
            pk = cx.ps()
            cx.mm(pk[:, 0:T], [(wA[:, k, 2560:2688], uc(k)) for k in range(KC)], [wb, u.b], [pk.b])
            cx.copy("act", ckvs[:, :], pk[:, 0:T], [pk.b], [ckvs.b])
            cx.tt("pool", sqk[:, 0:T], ckvs[:, :], ckvs[:, :], ALU.mult, [ckvs.b], [sqk.b])
            pkr = cx.ps()
            cx.mm(pkr[0:32, 0:T], [(wA[:, k, 2688:2720], uc(k)) for k in range(KC)], [wb, u.b], [pkr.b])
            pkrr = cx.ps()
            cx.mm(pkrr[0:32, 0:T], [(wR[:, k, 640:672], uc(k)) for k in range(KC)], [wb, u.b], [pkrr.b])
            qk_b(4)
            cnt["q"] += 5
            i2 = cnt["q"] % 4
            a_ = ta[i2]
            b_ = tb[i2]
            cx.tt("dve", a_[0:32, :], pkr[0:32, 0:T], tK[:, 0, :], ALU.mult, [pkr.b, tK.b], [a_.b])
            cx.tt("dve", b_[0:32, :], pkrr[0:32, 0:T], tK[:, 1, :], ALU.mult, [pkrr.b, tK.b], [b_.b])
            kr_ = kro[c % 2]
            cx.tt("pool", kr_[:, :], a_[0:32, :], b_[0:32, :], ALU.add, [a_.b, b_.b], [kr_.b])
            cnt["q"] += 1
            cx.dma("sp", [(sq_["kmT"][h, 64:96, t0:t0 + T], kr_[:, :]) for h in range(8)], kr_.b, reads=[kr_.b])

            for j in range(T // 128):
                pv = cx.ps()
                cx.mm(pv[:, 0:128], [(u[:, k, 1 + j * 128:1 + (j + 1) * 128], wA[:, k, 640:768]) for k in range(KC)],
                      [wb, u.b], [pv.b])
                v_ = vst[j % 2]
                cx.copy("act", v_[:, :, 0:64], pv[:, 0:128].rearrange("p (h x) -> p h x", h=2), [pv.b], [v_.b])
                cx.dma("sp", [(sq_["vA"][t0 + j * 128:t0 + (j + 1) * 128, :, :], v_[:, :, :])], v_.b, reads=[v_.b])

            pss = cx.ps()
            cx.mm(pss[:, 0:T], [(ones_ap, sqc[:, i, 0:T]) for i in range(2)], [sq.b] + cb, [pss.b])
            rsqrt_from_psum(cx, rstd[:, 0:T], rstd.b, pss[:, 0:T], pss.b, 256.0)
            cx.tt("dve", cqn[:, :, :], cqs[:, :, :], rstd[:, 0:T].unsqueeze(1).to_broadcast([128, 2, T]), ALU.mult,
                  [cqs.b, rstd.b], [cqn.b])
            pss2 = cx.ps()
            cx.mm(pss2[:, 0:T], [(ones_ap, sqk[:, 0:T])], [sqk.b] + cb, [pss2.b])
            rsqrt_from_psum(cx, rstdk[:, 0:T], rstdk.b, pss2[:, 0:T], pss2.b, 128.0)
            cx.tt("dve", ckvn[:, :], ckvs[:, :], rstdk[:, 0:T], ALU.mult, [ckvs.b, rstdk.b], [ckvn.b])

            for hc in range(12):
                if hc == 4 and c + 1 < nch:
                    norm(c + 1)
                ph = cx.ps()
                cx.mm(ph[:, 0:TH], [(wA[:, k, 768 + hc * 128:768 + (hc + 1) * 128], u[:, k, :]) for k in range(KC)],
                      [wb, u.b], [ph.b])
                if 4 <= hc < 8:
                    outt, outap, outb = x1k, x1k[:, hc - 4, :], x1k.b
                else:
                    outt = hst[cnt["h"] % 3]
                    cnt["h"] += 1
                    outap, outb = outt[:, :], outt.b
                cx.act(outap, ph[:, 1:T + 1], AF.Identity, [ph.b, hw.b], [outb],
                       bias=hw[:, hc, 3:4], scale=hw[:, hc, 1:2])
                cx.stt("dve", outap, ph[:, 0:T], hw[:, hc, 0:1], outap, ALU.mult, ALU.add, [ph.b, hw.b, outb], [outb])
                cx.stt("dve", outap, ph[:, 2:T + 2], hw[:, hc, 2:3], outap, ALU.mult, ALU.add, [ph.b, hw.b, outb], [outb])
                if hc < 4:
                    cx.dma("sp", [(sq_["x0T"][hc * 128:(hc + 1) * 128, t0:t0 + T], outap)], outb, reads=[outb])
                elif hc >= 8:
                    s_ = sst[cnt["s"] % 2]
                    cnt["s"] += 1
                    cx.tt("pool", s_[:, :], outap, x1k[:, hc - 8, :], ALU.mult, [outb, x1k.b], [s_.b])
                    cx.dma("sp", [(sq_["sT"][(hc - 8) * 128:(hc - 7) * 128, t0:t0 + T], s_[:, :])], s_.b, reads=[s_.b])

            for h in range(8):
                i2 = cnt["q"] % 4
                pq = cx.ps()
                cx.mm(pq[0:96, 0:T], [(wuq[:, i, h * 96:(h + 1) * 96], cqn[:, i, :]) for i in range(2)], [wb, cqn.b], [pq.b])
                pr = cx.ps()
                cx.mm(pr[0:96, 0:T], [(wuqr[:, i, h * 96:(h + 1) * 96], cqn[:, i, :]) for i in range(2)], [wb, cqn.b], [pr.b])
                a_ = ta[i2]
                b_ = tb[i2]
                cx.tt("dve", a_[0:96, :], pq[0:96, 0:T], tM[:, 0, :], ALU.mult, [pq.b, tM.b], [a_.b])
                cx.tt("dve", b_[0:96, :], pr[0:96, 0:T], tM[:, 1, :], ALU.mult, [pr.b, tM.b], [b_.b])
                o_ = qst[i2]
                cx.tt("pool", o_[0:96, :], a_[0:96, :], b_[0:96, :], ALU.add, [a_.b, b_.b], [o_.b])
                cx.dma("sp", [(sq_["qmT"][h, :, t0:t0 + T], o_[0:96, :])], o_.b, reads=[o_.b])
                cnt["q"] += 1
            for h in range(8):
                pkn = cx.ps()
                cx.mm(pkn[0:64, 0:T], [(wukv[:, h * 128:h * 128 + 64], ckvn[:, :])], [wb, ckvn.b], [pkn.b])
                k_ = kst[cnt["k"] % 3]
                cnt["k"] += 1
                cx.copy("act", k_[:, :], pkn[0:64, 0:T], [pkn.b], [k_.b])
                cx.dma("sp", [(sq_["kmT"][h, 0:64, t0:t0 + T], k_[:, :])], k_.b, reads=[k_.b])
            for j in range(T // 128):
                pv = cx.ps()
                cx.mm(pv[:, 0:512].rearrange("p (h x) -> p h x", h=8), [(ckvn[:, j * 128:(j + 1) * 128], wukv_v)],
                      [wb, ckvn.b], [pv.b])
                v_ = vmst[j % 2]
                cx.copy("act", v_[:, :, 0:64], pv[:, 0:512].rearrange("p (h x) -> p h x", h=8), [pv.b], [v_.b])
                cx.dma("sp", [(sq_["vM"][t0 + j * 128:t0 + (j + 1) * 128, :, :], v_[:, :, :])], v_.b, reads=[v_.b])

    for (L_, sq__) in seqs:
        run_seq(L_, sq__)
    cx.barrier()


def rope_tables(L):
    rows = L // GRID_W
    row = np.repeat(np.arange(rows, dtype=np.float32), GRID_W)
    col = np.tile(np.arange(GRID_W, dtype=np.float32), rows)

    def tab(d_rot):
        n_freq = d_rot // 4
        inv = (np.float32(10000.0) ** (-np.arange(n_freq, dtype=np.float32) / np.float32(n_freq))).astype(np.float32)
        ang = np.concatenate([row[:, None] * inv, col[:, None] * inv], axis=-1).astype(np.float32)
        return np.cos(ang).astype(np.float32), np.sin(ang).astype(np.float32)

    cA, sA = tab(64)
    cM, sM = tab(32)
    idxA = (np.arange(128) % 64) // 2
    out = {}
    out["ropeA_cos"] = np.ascontiguousarray(cA[:, idxA].T)
    out["ropeA_sin"] = np.ascontiguousarray(sA[:, idxA].T)
    mc = np.ones((96, L), np.float32)
    ms = np.zeros((96, L), np.float32)
    idxM = np.arange(32) // 2
    mc[64:96] = cM[:, idxM].T
    ms[64:96] = sM[:, idxM].T
    out["ropeM_cos"] = mc
    out["ropeM_sin"] = ms
    return out


def attn_pass(cx, L, jobs):
    cx.phase_reset()
    QC = 512
    nq = L // QC
    nk = L // 128
    qt = [cx.alloc([128, L], BF16, "aq") for _ in range(2)]
    kt = [cx.alloc([128, L], BF16, "ak") for _ in range(2)]
    vt = [cx.alloc([128, nk, 128], BF16, "av") for _ in range(2)]
    pT = [cx.alloc([128, 2, QC], BF16, "pT") for _ in range(3)]
    rec = [cx.alloc([128, QC], F32, "rec") for _ in range(2)]
    rec2 = [cx.alloc([64, QC], F32, "rec2") for _ in range(2)]
    ost = [cx.alloc([64, QC], BF16, "ost") for _ in range(2)]
    ps_s = [(cx.psum[2 * i], cx.psum[2 * i + 1]) for i in range(3)]
    ps_o = [cx.psum[6], cx.psum[7]]

    kv_slot = {}
    state = {"kv": 0, "q": 0, "blk": 0, "oc": 0}

    def load_kv(job):
        if job["kvid"] in kv_slot:
            return
        s = state["kv"] % 2
        state["kv"] += 1
        d = job["d"]
        kp = [(kt[s][0:d, :], job["k"])]
        if d == 64:
            kp.append((kt[s][64:128, :], job["k"]))
        cx.dma("sp", kp, kt[s].b, writes=[kt[s].b])
        cx.dma("sp", [(vt[s][:, :, :], job["v"].rearrange("(c p) x -> p c x", p=128))], vt[s].b, writes=[vt[s].b])
        kv_slot[job["kvid"]] = s

    def load_q(ji):
        job = jobs[ji]
        s = ji % 2
        qp = [(qt[s][0:job["d"], :], job["q"])]
        if job["d"] == 64:
            qp.append((qt[s][64:128, :], job["q"]))
        cx.dma("sp", qp, qt[s].b, writes=[qt[s].b])

    load_kv(jobs[0])
    load_q(0)
    for ji, job in enumerate(jobs):
        d = job["d"]
        sc = job["scale"]
        if ji + 1 < len(jobs):
            load_kv(jobs[ji + 1])
            load_q(ji + 1)
        ks = kv_slot[job["kvid"]]
        K = kt[ks]
        V = vt[ks]
        Q = qt[ji % 2]
        for qc in range(nq):
            q0 = qc * QC
            po = ps_o[state["oc"] % 2]
            oslot = state["oc"] % 2
            state["oc"] += 1
            npair = nk // 2

            def s_step(pi):
                pa, pb = ps_s[(state["blk"] + pi) % 3]
                for half, pp in ((0, pa), (1, pb)):
                    kc = pi * 2 + half
                    r0 = 64 if (d == 64 and half == 1) else 0
                    cx.mm(pp[:, 0:QC], [(K[r0:r0 + d, kc * 128:(kc + 1) * 128], Q[r0:r0 + d, q0:q0 + QC])],
                          [K.b, Q.b], [pp.b])

            def e_step(pi):
                pa, pb = ps_s[(state["blk"] + pi) % 3]
                p_ = pT[(state["blk"] + pi) % 3]
                cx.act(p_[:, :, :].rearrange("p a q -> p (a q)"), cx.ps_pair_ap(pa), AF.Exp, [pa.b, pb.b], [p_.b],
                       scale=sc)

            def v_step(pi):
                p_ = pT[(state["blk"] + pi) % 3]
                for half in range(2):
                    kc = pi * 2 + half
                    cx.mm_acc(po[:, 0:QC], V[:, kc, :], p_[:, half, :], kc == 0, kc == nk - 1, [V.b, p_.b], [po.b])

            s_step(0)
            if npair > 1:
                s_step(1)
            for pi in range(npair):
                if pi + 2 < npair:
                    s_step(pi + 2)
                e_step(pi)
                v_step(pi)
            state["blk"] += npair
            r_ = rec[oslot]
            r2_ = rec2[oslot]
            o_ = ost[oslot]
            cx.recip("dve", r_[64:128, :], po[64:128, 0:QC], [po.b], [r_.b])
            cx.dma("sp", [(r2_[:, :], r_[64:128, :])], r2_.b, reads=[r_.b], writes=[r2_.b])
            cx.tt("dve", o_[:, :], po[0:64, 0:QC], r2_[:, :], ALU.mult, [po.b, r2_.b], [o_.b])
            cx.dma("sp", [(job["out"][:, q0:q0 + QC], o_[:, :])], o_.b, reads=[o_.b])
    cx.barrier()


def merge_pass(cx, seqs, l, W, consts):
    T = TT
    cx.phase_reset()
    wgt = cx.alloc([128, KC, 3072], BF16, "wgt")
    wbr = cx.alloc([128, 12, D], BF16, "wbr")
    wout = cx.alloc([128, KC, D], BF16, "wout")
    gt = cx.alloc([128, KC], F32, "gmix")
    wl = WLoader(cx)
    cb = [consts["cbuf"]]
    ones_ap = consts["ones"][:, :]
    wb = Buf("p4w")
    w_in = W["w_in"][l]
    cx.dma("sp", [(gt[:, :], W["g_mix"][l].rearrange("(k p) -> p k", p=128))], gt.b, writes=[gt.b], slow=True)
    for k in range(KC):
        for c0 in range(0, 3072, 1024):
            wl.load(wgt[:, k, c0:c0 + 1024], [wb], w_in[k * 128:(k + 1) * 128, 2720 + c0:2720 + c0 + 1024], 128, 1024,
                    scale_ap=gt[:, k:k + 1], scale_reads=[gt.b])
    for i in range(3):
        for k in range(4):
            wl.load(wbr[:, i * 4 + k, :], [wb], W["w_branch"][l][i][k * 128:(k + 1) * 128, :], 128, D)
    for k in range(KC):
        wl.load(wout[:, k, :], [wb], W["w_out"][l][k * 128:(k + 1) * 128, :], 128, D)
    wl.join([wb])

    xT = [cx.alloc([128, KC, T], F32, "xT") for _ in range(2)]
    yb = [[cx.alloc([128, 4, T], BF16, "y%d" % i) for _ in range(2)] for i in range(3)]
    uu = [cx.alloc([128, KC, T], BF16, "u") for _ in range(2)]
    sq = cx.alloc([128, KC, T], BF16, "sq")
    rstd = cx.alloc([128, T], F32, "rstd")
    sig = [cx.alloc([128, T], F32, "sig") for _ in range(6)]
    tm = [cx.alloc([128, T], F32, "tm") for _ in range(6)]
    mg = cx.alloc([128, KC, T], BF16, "mg")
    def run_seq(L, sq_):
        xav = sq_["xa"].rearrange("(k p) l -> p k l", p=128)
        xbv = sq_["xb"].rearrange("(k p) l -> p k l", p=128)
        ysrc = [sq_[n].rearrange("(k p) l -> p k l", p=128) for n in ("yaT", "ybT", "ycT")]
        nch = L // T

        def load_chunk(c):
            t0 = c * T
            xs = xT[c % 2]
            cx.dma("sp", [(xs[:, :, :], xav[:, :, t0:t0 + T])], xs.b, writes=[xs.b])
            for i in range(3):
                y_ = yb[i][c % 2]
                cx.dma("sp", [(y_[:, :, :], ysrc[i][:, :, t0:t0 + T])], y_.b, writes=[y_.b])

        def norm(c):
            xs_ = xT[c % 2]
            u_ = uu[c % 2]
            rms_stats(cx, xs_[:, :, :], xs_.b, sq, ones_ap, cb, KC, T, rstd, float(D),
                      lambda k: sq[:, :, :] if k is None else sq[:, k, :])
            cx.tt("dve", u_[:, :, :], xs_[:, :, :], rstd[:, 0:T].unsqueeze(1).to_broadcast([128, KC, T]),
                  ALU.mult, [xs_.b, rstd.b], [u_.b])

        load_chunk(0)
        norm(0)
        for c in range(nch):
            t0 = c * T
            xs = xT[c % 2]
            u = uu[c % 2]
            if c + 1 < nch:
                load_chunk(c + 1)
            for d in range(KC):
                if d == 4 and c + 1 < nch:
                    norm(c + 1)
                pbs = []
                for i in range(3):
                    y_ = yb[i][c % 2]
                    pb = cx.ps()
                    cx.mm(pb[:, 0:T], [(wbr[:, i * 4 + k, d * 128:(d + 1) * 128], y_[:, k, :]) for k in range(4)],
                          [wb, y_.b], [pb.b])
                    pg = cx.ps()
                    cx.mm(pg[:, 0:T], [(wgt[:, k, i * 1024 + d * 128:i * 1024 + (d + 1) * 128], u[:, k, :]) for k in range(KC)],
                          [wb, u.b], [pg.b])
                    sg = sig[(d % 2) * 3 + i]
                    tq = tm[(d % 2) * 3 + i]
                    cx.act(sg[:, :], pg[:, 0:T], AF.Sigmoid, [pg.b], [sg.b])
                    cx.tt("dve", tq[:, :], sg[:, :], pb[:, 0:T], ALU.mult, [sg.b, pb.b], [tq.b])
                t0_, t1_, t2_ = tm[(d % 2) * 3], tm[(d % 2) * 3 + 1], tm[(d % 2) * 3 + 2]
                cx.tt("pool", t0_[:, :], t0_[:, :], t1_[:, :], ALU.add, [t0_.b, t1_.b], [t0_.b])
                cx.tt("pool", mg[:, d, :], t0_[:, :], t2_[:, :], ALU.add, [t0_.b, t2_.b], [mg.b])
            for d2 in range(KC):
                po = cx.ps()
                cx.mm(po[:, 0:T], [(wout[:, d, d2 * 128:(d2 + 1) * 128], mg[:, d, :]) for d in range(KC)], [wb, mg.b], [po.b])
                cx.tt("dve", xs[:, d2, :], xs[:, d2, :], po[:, 0:T], ALU.add, [xs.b, po.b], [xs.b])
            cx.dma("sp", [(xbv[:, :, t0:t0 + T], xs[:, :, :])], xs.b, reads=[xs.b])

    for (L_, sq__) in seqs:
        run_seq(L_, sq__)
    cx.barrier()


HYW = 512


def hyena_tables(L):
    N = 2 * L
    N1 = N // 128
    out = {}
    t = np.linspace(0.0, 1.0, L, dtype=np.float32)
    w = (np.float32(2.0 * math.pi) * np.arange(L, dtype=np.float32) / np.float32(L)).astype(np.float32)
    f = np.linspace(1e-4, 15, 16, dtype=np.float32)[None, :]
    z = np.concatenate([t[:, None], np.cos(f * w[:, None]), -np.sin(f * w[:, None])], axis=-1).astype(np.float32)
    idx2 = (L - np.arange(L)) % L
    zfull = np.concatenate([z, z[idx2]], axis=0)
    tfull = np.concatenate([t, t[idx2]], axis=0).astype(np.float32)
    tfull[L] = 1.0e4
    out["zposT"] = np.ascontiguousarray(zfull.T)
    out["tvec"] = tfull
    max_decay = math.log(1e-2) / 0.3
    min_decay = math.log(1e-2) / 1.5
    deltas = np.abs(np.linspace(min_decay, max_decay, HYW, dtype=np.float32))
    out["negdelta"] = (-deltas).astype(np.float32)
    n1 = np.arange(N1, dtype=np.float64)
    a1 = 2.0 * np.pi * np.outer(n1, n1) / N1
    out["F1"] = np.concatenate([np.cos(a1), -np.sin(a1)], axis=1).astype(np.float32)
    n2 = np.arange(128, dtype=np.float64)
    at = 2.0 * np.pi * np.outer(n2, n1) / N
    out["Tc"] = np.cos(at).astype(np.float32)
    out["Ts"] = np.sin(at).astype(np.float32)
    out["T2c"] = np.ascontiguousarray(np.cos(at).T).astype(np.float32)
    out["T2s"] = np.ascontiguousarray(np.sin(at).T).astype(np.float32)
    a2 = 2.0 * np.pi * np.outer(n2, n2) / 128.0
    C = np.cos(a2)
    S = np.sin(a2)
    out["F2"] = np.concatenate([C, S, -S], axis=1).astype(np.float32)
    out["F2i"] = np.concatenate([C, S, -S, C], axis=1).astype(np.float32)
    out["G1"] = np.concatenate([np.cos(a1)[:, :N1 // 2], -np.sin(a1)[:, :N1 // 2]], axis=1).astype(np.float32)
    return out


def cmul(cx, Ar, Ai, a_reads, Tc, Ts, t_reads, sgn, outR, outI, o_writes, tmp):
    p1, p2, p3, p4 = tmp
    shp = list(Ar.shape)

    def v(t):
        ap = t[0:shp[0], 0:int(np.prod(shp[1:]))]
        if len(shp) == 3:
            ap = ap.rearrange("p (a b) -> p a b", a=shp[1])
        return ap
    cx.tt("dve", v(p1), Ar, Tc, ALU.mult, a_reads + t_reads, [p1.b])
    cx.tt("dve", v(p2), Ai, Ts, ALU.mult, a_reads + t_reads, [p2.b])
    cx.tt("pool", outR, v(p1), v(p2), ALU.subtract if sgn > 0 else ALU.add, [p1.b, p2.b], o_writes)
    cx.tt("dve", v(p3), Ai, Tc, ALU.mult, a_reads + t_reads, [p3.b])
    cx.tt("dve", v(p4), Ar, Ts, ALU.mult, a_reads + t_reads, [p4.b])
    cx.tt("pool", outI, v(p3), v(p4), ALU.add if sgn > 0 else ALU.subtract, [p3.b, p4.b], o_writes)


class HyFFT:
    def __init__(self, cx, L, tabs):
        self.cx = cx
        self.L = L
        N1 = 2 * L // 128
        self.N1 = N1
        self.CG = min(8, max(1, 512 // N1))
        self.cb = Buf("hytab")
        cb = self.cb
        wl = WLoader(cx, ncols=512, nst=2)
        self.F1 = cx.alloc([N1, 2 * N1], BF16, "F1")
        self.F2 = cx.alloc([128, 384], BF16, "F2")
        self.F2i = cx.alloc([128, 512], BF16, "F2i")
        self.G1 = cx.alloc([N1, N1], BF16, "G1")
        self.Tc = cx.alloc([128, N1], F32, "Tc")
        self.Ts = cx.alloc([128, N1], F32, "Ts")
        self.T2c = cx.alloc([N1, 128], F32, "T2c")
        self.T2s = cx.alloc([N1, 128], F32, "T2s")
        wl.load(self.F1[:, :], [cb], tabs["F1"], N1, 2 * N1, eng="dve")
        wl.load(self.F2[:, :], [cb], tabs["F2"], 128, 384, eng="dve")
        wl.load(self.F2i[:, :], [cb], tabs["F2i"], 128, 512, eng="dve")
        wl.load(self.G1[:, :], [cb], tabs["G1"], N1, N1, eng="dve")
        wl.join([cb])
        cx.dma("sp", [(self.Tc[:, :], tabs["Tc"]), (self.Ts[:, :], tabs["Ts"]),
                      (self.T2c[:, :], tabs["T2c"]), (self.T2s[:, :], tabs["T2s"])], self.Tc.b, writes=[cb])
        self.NS = 4
        self.NT = 8
        self.tmp = [[cx.alloc([128, 512], F32, "ctmp") for _ in range(4)] for _ in range(self.NT)]
        self.ti = 0
        self.Br = [cx.alloc([128, self.CG, N1], BF16, "Br") for _ in range(self.NS)]
        self.Bi = [cx.alloc([128, self.CG, N1], BF16, "Bi") for _ in range(self.NS)]

    def tmps(self):
        self.ti += 1
        return self.tmp[self.ti % self.NT]

    def stage1(self, xin, xb, nrows, c0, gi):
        cx = self.cx
        N1 = self.N1
        CG = self.CG
        cpb = max(1, min(CG, 512 // (2 * N1)))
        Br = self.Br[gi % self.NS]
        Bi = self.Bi[gi % self.NS]
        for b0 in range(0, CG, cpb):
            pa = cx.ps()
            for j in range(cpb):
                cx.mm(pa[:, j * 2 * N1:(j + 1) * 2 * N1], [(xin[0:nrows, c0 + b0 + j, :], self.F1[0:nrows, :])],
                      [xb, self.cb], [pa.b])
            A = pa[:, 0:cpb * 2 * N1].rearrange("p (c r k) -> p c r k", c=cpb, r=2)
            Tcb = self.Tc[:, :].unsqueeze(1).to_broadcast([128, cpb, N1])
            Tsb = self.Ts[:, :].unsqueeze(1).to_broadcast([128, cpb, N1])
            cmul(cx, A[:, :, 0, :], A[:, :, 1, :], [pa.b], Tcb, Tsb, [self.cb], -1,
                 Br[:, b0:b0 + cpb, :], Bi[:, b0:b0 + cpb, :], [Br.b], self.tmps())

    def stage3(self, gi):
        cx = self.cx
        N1 = self.N1
        CG = self.CG
        Br = self.Br[gi % self.NS]
        Bi = self.Bi[gi % self.NS]
        C = self.F2[:, 0:128]
        S = self.F2[:, 128:256]
        nS = self.F2[:, 256:384]
        brf = Br[:, :, :].rearrange("p c k -> p (c k)")
        bif = Bi[:, :, :].rearrange("p c k -> p (c k)")
        W = CG * N1
        pxr = cx.ps()
        cx.mm(pxr[:, 0:W], [(C, brf), (S, bif)], [Br.b, self.cb], [pxr.b])
        pxi = cx.ps()
        cx.mm(pxi[:, 0:W], [(C, bif), (nS, brf)], [Br.b, self.cb], [pxi.b])
        return pxr, pxi


def hyena_filter_pass(cx, L, l, sq_, W, consts, tabs):
    cx.phase_reset()
    N = 2 * L
    PC = 512
    npc = N // PC
    w1 = cx.alloc([33, 64], F32, "w1")
    w2 = cx.alloc([64, 64], F32, "w2")
    w3 = cx.alloc([64, 1024], F32, "w3")
    sm = cx.alloc([64, 8], F32, "hsm")
    nd = cx.alloc([128, 4], F32, "negd")
    bia = cx.alloc([128, 4], F32, "hbias")
    cb = Buf("hyw")
    cx.dma("sp", [(w1[:, :], W["w_hy_f1"][l]), (w2[:, :], W["w_hy_f2"][l]), (w3[:, :], W["w_hy_f3"][l])], w1.b, writes=[cb])
    cx.dma("sp", [(sm[:, 0:1], W["b_hy_f1"][l].rearrange("(p o) -> p o", o=1)),
                  (sm[:, 1:2], W["b_hy_f2"][l].rearrange("(p o) -> p o", o=1)),
                  (sm[:, 2:3], W["hy_sin_freq"][l].rearrange("(p o) -> p o", o=1)),
                  (nd[:, :], tabs["negdelta"].rearrange("(c p) -> p c", p=128)),
                  (bia[:, :], W["hy_bias"][l].rearrange("(c p) -> p c", p=128))], sm.b, writes=[cb], slow=True)
    cx.tt("dve", sm[:, 3:4], sm[:, 0:1], sm[:, 2:3], ALU.mult, [cb], [cb])
    cx.tt("dve", sm[:, 4:5], sm[:, 1:2], sm[:, 2:3], ALU.mult, [cb], [cb])
    h2all = cx.alloc([64, N], F32, "h2all")
    zt = [cx.alloc([33, PC], F32, "zt") for _ in range(2)]
    arg = [cx.alloc([64, PC], F32, "arg") for _ in range(2)]
    msk = [cx.alloc([64, PC], F32, "msk") for _ in range(2)]
    h1 = [cx.alloc([64, PC], F32, "h1") for _ in range(2)]
    PI = math.pi

    def sin_layer(ps, bcol, out_ap, out_b, i):
        a = arg[i % 2]
        m = msk[i % 2]
        cx.ts("dve", a[:, :], ps[0:64, 0:PC], sm[:, 2:3], sm[:, bcol:bcol + 1], ALU.mult, ALU.add, [ps.b, cb], [a.b])
        for _ in range(2):
            cx.ts("dve", m[:, :], a[:, :], PI, -2.0 * PI, ALU.is_gt, ALU.mult, [a.b], [m.b])
            cx.tt("dve", a[:, :], a[:, :], m[:, :], ALU.add, [a.b, m.b], [a.b])
            cx.ts("dve", m[:, :], a[:, :], -PI, 2.0 * PI, ALU.is_lt, ALU.mult, [a.b], [m.b])
            cx.tt("dve", a[:, :], a[:, :], m[:, :], ALU.add, [a.b, m.b], [a.b])
        cx.act(out_ap, a[:, :], AF.Sin, [a.b], [out_b])

    for pc in range(npc):
        j0 = pc * PC
        z_ = zt[pc % 2]
        cx.dma("sp", [(z_[:, :], tabs["zposT"][:, j0:j0 + PC])], z_.b, writes=[z_.b])
        p1 = cx.ps()
        cx.mm(p1[0:64, 0:PC], [(w1[:, :], z_[:, :])], [cb, z_.b], [p1.b])
        h_ = h1[pc % 2]
        sin_layer(p1, 3, h_[:, :], h_.b, pc)
        p2 = cx.ps()
        cx.mm(p2[0:64, 0:PC], [(w2[:, :], h_[:, :])], [cb, h_.b], [p2.b])
        sin_layer(p2, 4, h2all[:, j0:j0 + PC], h2all.b, pc + 1)

    tv = [cx.alloc([128, PC], F32, "tv") for _ in range(2)]
    win = [cx.alloc([128, PC], F32, "win") for _ in range(2)]
    kf = [cx.alloc([128, PC], F32, "kf") for _ in range(2)]
    kb = [cx.alloc([128, PC], BF16, "kb") for _ in range(2)]
    it = 0
    for pc in range(npc):
        j0 = pc * PC
        t_ = tv[pc % 2]
        cx.dma("sp", [(t_[:, :], tabs["tvec"][j0:j0 + PC].partition_broadcast(128))], t_.b, writes=[t_.b])
        half = 0 if j0 < L else 1
        for fc in range(4):
            p3 = cx.ps()
            cx.mm(p3[:, 0:PC], [(w3[:, half * 512 + fc * 128:half * 512 + (fc + 1) * 128], h2all[:, j0:j0 + PC])],
                  [cb, h2all.b], [p3.b])
            w_ = win[it % 2]
            cx.act(w_[:, :], t_[:, :], AF.Exp, [t_.b, cb], [w_.b], scale=nd[:, fc:fc + 1])
            k_ = kf[it % 2]
            cx.tt("dve", k_[:, :], p3[:, 0:PC], w_[:, :], ALU.mult, [p3.b, w_.b], [k_.b])
            if pc == 0:
                cx.tt("dve", k_[:, 0:1], k_[:, 0:1], bia[:, fc:fc + 1], ALU.add, [k_.b, cb], [k_.b])
            kb_ = kb[it % 2]
            cx.copy("pool", kb_[:, :], k_[:, :], [k_.b], [kb_.b])
            cx.dma("sp", [(sq_["kcT"][fc * 128:(fc + 1) * 128, j0:j0 + PC], kb_[:, :])], kb_.b, reads=[kb_.b])
            it += 1
    cx.barrier()


def hyena_spectrum_pass(cx, L, sq_, tabs):
    cx.phase_reset()
    ff = HyFFT(cx, L, tabs)
    N1 = ff.N1
    CG = ff.CG
    CB = 32
    xin = [cx.alloc([N1, CB, 128], BF16, "kin") for _ in range(2)]
    hst = [cx.alloc([128, 2, 512], BF16, "hst") for _ in range(2)]
    src = sq_["kcT"].rearrange("c (a b) -> a c b", b=128)
    inv_n = 1.0 / (2 * L)
    groups = []
    for cb0 in range(0, HYW, CB):
        for c0 in range(0, CB, CG):
            groups.append((cb0, c0))
    G = len(groups)
    Wd = CG * N1
    xcur = {}

    def st1(g):
        cb0, c0 = groups[g]
        if c0 == 0:
            x_ = xin[(cb0 // CB) % 2]
            cx.dma("sp", [(x_[:, :, :], src[:, cb0:cb0 + CB, :])], x_.b, writes=[x_.b])
            xcur[cb0] = x_
        x_ = xcur[cb0]
        ff.stage1(x_, x_.b, N1, c0, g)

    def st2(g):
        cb0, c0 = groups[g]
        pxr, pxi = ff.stage3(g)
        h_ = hst[g % 2]
        cx.act(h_[:, 0, 0:Wd], pxr[:, 0:Wd], AF.Copy, [pxr.b], [h_.b], scale=inv_n)
        cx.act(h_[:, 1, 0:Wd], pxi[:, 0:Wd], AF.Copy, [pxi.b], [h_.b], scale=inv_n)
        ch = cb0 + c0
        cx.dma("sp", [(sq_["Hr"][:, ch * N1:ch * N1 + Wd], h_[:, 0, 0:Wd]),
                      (sq_["Hi"][:, ch * N1:ch * N1 + Wd], h_[:, 1, 0:Wd])], h_.b, reads=[h_.b])

    for t in range(G + 1):
        if t < G:
            st1(t)
        if t >= 1:
            st2(t - 1)
    cx.barrier()


def hyena_conv_pass(cx, L, sq_, tabs):
    cx.phase_reset()
    ff = HyFFT(cx, L, tabs)
    N1 = ff.N1
    NH = N1 // 2
    CG = ff.CG
    CB = 32
    Wd = CG * N1
    xin = [cx.alloc([NH, CB, 128], BF16, "sin") for _ in range(2)]
    x0t = [cx.alloc([NH, CB, 128], F32, "x0in") for _ in range(2)]
    ht = [cx.alloc([128, 2, 512], BF16, "ht") for _ in range(4)]
    Yr = [cx.alloc([128, CG, N1], BF16, "Yr") for _ in range(4)]
    Yi = [cx.alloc([128, CG, N1], BF16, "Yi") for _ in range(4)]
    Dr = [cx.alloc([N1, CG, 128], BF16, "Dr") for _ in range(4)]
    Di = [cx.alloc([N1, CG, 128], BF16, "Di") for _ in range(4)]
    yst = [cx.alloc([NH, CG, 128], BF16, "yst") for _ in range(4)]
    ssrc = sq_["sT"].rearrange("c (a b) -> a c b", b=128)
    x0src = sq_["x0T"].rearrange("c (a b) -> a c b", b=128)
    ydst = sq_["ybT"].rearrange("c (a b) -> a c b", b=128)
    Ci = ff.F2i[:, 0:256]
    Si = ff.F2i[:, 256:512]
    groups = []
    for cb0 in range(0, HYW, CB):
        for c0 in range(0, CB, CG):
            groups.append((cb0, c0))
    G = len(groups)
    xcur = {}
    NS = ff.NS

    def st1(g):
        cb0, c0 = groups[g]
        if c0 == 0:
            x_ = xin[(cb0 // CB) % 2]
            x0_ = x0t[(cb0 // CB) % 2]
            cx.dma("sp", [(x_[:, :, :], ssrc[:, cb0:cb0 + CB, :])], x_.b, writes=[x_.b])
            cx.dma("sp", [(x0_[:, :, :], x0src[:, cb0:cb0 + CB, :])], x0_.b, writes=[x0_.b])
            xcur[cb0] = (x_, x0_)
        x_, x0_ = xcur[cb0]
        ch = cb0 + c0
        h_ = ht[g % NS]
        cx.dma("sp", [(h_[:, 0, 0:Wd], sq_["Hr"][:, ch * N1:ch * N1 + Wd]),
                      (h_[:, 1, 0:Wd], sq_["Hi"][:, ch * N1:ch * N1 + Wd])], h_.b, writes=[h_.b])
        ff.stage1(x_, x_.b, NH, c0, g)

    def st2(g):
        h_ = ht[g % NS]
        pxr, pxi = ff.stage3(g)
        yr = Yr[g % NS]
        yi = Yi[g % NS]
        cmul(cx, pxr[:, 0:Wd], pxi[:, 0:Wd], [pxr.b, pxi.b], h_[:, 0, 0:Wd], h_[:, 1, 0:Wd], [h_.b], +1,
             yr[:, :, :].rearrange("p c k -> p (c k)"), yi[:, :, :].rearrange("p c k -> p (c k)"), [yr.b], ff.tmps())

    def st3(g):
        yr = Yr[g % NS]
        yi = Yi[g % NS]
        dr = Dr[g % NS]
        di = Di[g % NS]
        for b0 in range(0, CG, 2):
            nb = min(2, CG - b0)
            pc_ = cx.ps()
            for j in range(nb):
                cx.mm(pc_[0:N1, j * 256:(j + 1) * 256], [(yr[:, b0 + j, :], Ci), (yi[:, b0 + j, :], Si)],
                      [yr.b, ff.cb], [pc_.b])
            Cv = pc_[0:N1, 0:nb * 256].rearrange("p (c r k) -> p c r k", c=nb, r=2)
            T2cb = ff.T2c[:, :].unsqueeze(1).to_broadcast([N1, nb, 128])
            T2sb = ff.T2s[:, :].unsqueeze(1).to_broadcast([N1, nb, 128])
            cmul(cx, Cv[:, :, 0, :], Cv[:, :, 1, :], [pc_.b], T2cb, T2sb, [ff.cb], +1,
                 dr[:, b0:b0 + nb, :], di[:, b0:b0 + nb, :], [dr.b], ff.tmps())

    def st4(g):
        cb0, c0 = groups[g]
        x_, x0_ = xcur[cb0]
        ch = cb0 + c0
        dr = Dr[g % NS]
        di = Di[g % NS]
        ys = yst[g % NS]
        for q0 in range(0, CG, 4):
            nq_ = min(4, CG - q0)
            py = cx.ps()
            cx.mm(py[0:NH, 0:nq_ * 128],
                  [(ff.G1[:, 0:NH], dr[:, q0:q0 + nq_, :].rearrange("p c k -> p (c k)")),
                   (ff.G1[:, NH:N1], di[:, q0:q0 + nq_, :].rearrange("p c k -> p (c k)"))], [dr.b, ff.cb], [py.b])
            cx.tt("dve", ys[:, q0:q0 + nq_, :], py[0:NH, 0:nq_ * 128].rearrange("p (c k) -> p c k", c=nq_),
                  x0_[:, c0 + q0:c0 + q0 + nq_, :], ALU.mult, [py.b, x0_.b], [ys.b])
        cx.dma("sp", [(ydst[:, ch:ch + CG, :], ys[:, :, :])], ys.b, reads=[ys.b])

    for t in range(G + 3):
        if t < G:
            st1(t)
        if 0 <= t - 1 < G:
            st2(t - 1)
        if 0 <= t - 2 < G:
            st3(t - 2)
        if 0 <= t - 3 < G:
            st4(t - 3)
    cx.barrier()


W_SHAPES = {
    "g_ffn1": [2, D], "w_ffn1_gate": [2, D, DFF], "w_ffn1_up": [2, D, DFF], "w_ffn1_down": [2, DFF, D],
    "g_mix": [2, D], "w_in": [2, D, IN_W], "g_qnorm": [2, 64], "g_knorm": [2, 64],
    "w_hy_short": [2, 3, 1536], "b_hy_short": [2, 1536], "w_hy_f1": [2, 33, 64], "b_hy_f1": [2, 64],
    "w_hy_f2": [2, 64, 64], "b_hy_f2": [2, 64], "w_hy_f3": [2, 64, 1024], "hy_sin_freq": [2, 64],
    "hy_bias": [2, 512], "g_mla_q": [2, 256], "w_mla_uq": [2, 256, 768], "g_mla_kv": [2, 128],
    "w_mla_ukv": [2, 128, 1024], "w_branch": [2, 3, 512, D], "w_out": [2, D, D],
    "g_ffn2": [2, D], "w_ffn2_gate": [2, D, DFF], "w_ffn2_up": [2, D, DFF], "w_ffn2_down": [2, DFF, D],
    "g_final": [D],
}
DEPTH = 2


def host_tables(L, tag):
    t = {}
    for k, v in rope_tables(L).items():
        t["%s_%s" % (tag, k)] = v
    for k, v in hyena_tables(L).items():
        t["%s_hy_%s" % (tag, k)] = v
    return t


def build_program(lens):
    from contextlib import ExitStack
    nc = bass.Bass("TRN2", target_bir_lowering=False)
    W = {n: nc.dram_tensor(n, shp, F32, kind="ExternalInput").ap() for n, shp in W_SHAPES.items()}
    ident = nc.dram_tensor("ident", [128, 128], F32, kind="ExternalInput").ap()
    seqs = []
    for tag, L in lens.items():
        N1 = 2 * L // 128
        sq = {"L": L, "tag": tag}
        sq["x"] = nc.dram_tensor("x_" + tag, [L, D], F32, kind="ExternalInput").ap()
        sq["y"] = nc.dram_tensor("y_" + tag, [L, D], F32, kind="ExternalOutput").ap()
        tabs = {}
        for k, v in host_tables(L, tag).items():
            ap = nc.dram_tensor(k, list(v.shape), F32, kind="ExternalInput").ap()
            short = k[len(tag) + 1:]
            if short.startswith("hy_"):
                tabs[short[3:]] = ap
            else:
                sq[short] = ap
        sq["tabs"] = tabs

        def scr(n, shape, dt):
            sq[n] = nc.dram_tensor("%s_%s" % (tag, n), shape, dt, kind="Internal").ap()
        for n in ("xa", "xb", "xc"):
            scr(n, [D, L], F32)
        scr("qT", [512, L], BF16)
        scr("kT", [128, L], BF16)
        scr("vA", [L, 2, 128], BF16)
        scr("x0T", [512, L], F32)
        scr("sT", [512, L], BF16)
        scr("qmT", [8, 96, L], BF16)
        scr("kmT", [8, 96, L], BF16)
        scr("vM", [L, 8, 128], BF16)
        for n in ("yaT", "ybT", "ycT"):
            scr(n, [512, L], BF16)
        scr("kcT", [512, 2 * L], BF16)
        scr("Hr", [128, 512 * N1], BF16)
        scr("Hi", [128, 512 * N1], BF16)
        seqs.append(sq)

    cx = Ctx(nc)
    consts = setup_consts(cx, ident)
    for l in range(DEPTH):
        last = (l == DEPTH - 1)
        if l == 0:
            ffn_pass(cx, [(s["L"], s["x"], s["xa"], None) for s in seqs], W["g_ffn1"][l], W["w_ffn1_gate"][l],
                     W["w_ffn1_up"][l], W["w_ffn1_down"][l], consts, src_tok=True)
        else:
            ffn_pass(cx, [(s["L"], s["xc"], s["xa"], None) for s in seqs], W["g_ffn1"][l], W["w_ffn1_gate"][l],
                     W["w_ffn1_up"][l], W["w_ffn1_down"][l], consts)
        inproj_pass(cx, [(s["L"], s) for s in seqs], l, W, consts)
        for s in seqs:
            L = s["L"]
            jobs = []
            for h in range(8):
                g = h // 4
                jobs.append(dict(q=s["qT"][h * 64:(h + 1) * 64, :], k=s["kT"][g * 64:(g + 1) * 64, :], v=s["vA"][:, g, :],
                                 d=64, scale=64 ** -0.5, out=s["yaT"][h * 64:(h + 1) * 64, :], kvid="a%d" % g))
            for h in range(8):
                jobs.append(dict(q=s["qmT"][h], k=s["kmT"][h], v=s["vM"][:, h, :], d=96, scale=96 ** -0.5,
                                 out=s["ycT"][h * 64:(h + 1) * 64, :], kvid="m%d" % h))
            attn_pass(cx, L, jobs)
            hyena_filter_pass(cx, L, l, s, W, consts, s["tabs"])
            hyena_spectrum_pass(cx, L, s, s["tabs"])
            hyena_conv_pass(cx, L, s, s["tabs"])
        merge_pass(cx, [(s["L"], s) for s in seqs], l, W, consts)
        if not last:
            ffn_pass(cx, [(s["L"], s["xb"], s["xc"], None) for s in seqs], W["g_ffn2"][l], W["w_ffn2_gate"][l],
                     W["w_ffn2_up"][l], W["w_ffn2_down"][l], consts)
        else:
            ffn_pass(cx, [(s["L"], s["xb"], None, s["y"]) for s in seqs], W["g_ffn2"][l], W["w_ffn2_gate"][l],
                     W["w_ffn2_up"][l], W["w_ffn2_down"][l], consts, final=W["g_final"])
    stack = ExitStack()
    cx.S.finalize(stack)
    with nc.Block() as block:
        cx.S.emit(block)
    stack.close()
    return nc, cx


_TABLE_CACHE = {}


def run_step(inputs, lens, n_cores, prompt_idx, sample_idx):
    nc, cx = build_program(lens)
    tabs = {}
    for tag, L in lens.items():
        key = (tag, L)
        if key not in _TABLE_CACHE:
            _TABLE_CACHE[key] = host_tables(L, tag)
        tabs.update(_TABLE_CACHE[key])
    ident = np.eye(128, dtype=np.float32)
    in_maps = []
    for c in range(n_cores):
        m = {n: np.ascontiguousarray(inputs[n], dtype=np.float32) for n in W_SHAPES}
        m["ident"] = ident
        m.update(tabs)
        m["x_p"] = np.ascontiguousarray(inputs["x_prompt"][prompt_idx[c]], dtype=np.float32)
        m["x_s"] = np.ascontiguousarray(inputs["x_sample"][sample_idx[c]], dtype=np.float32)
        in_maps.append(m)
    res = run_bass_kernel_spmd(nc, in_maps, core_ids=list(range(n_cores)))
    return res


def kernel(**inputs):
    xp = np.asarray(inputs["x_prompt"])
    xs = np.asarray(inputs["x_sample"])
    B, Lp, _ = xp.shape
    Bs, Ls, _ = xs.shape
    n = 8
    prompt_idx = [c % B for c in range(n)]
    sample_idx = [c % Bs for c in range(n)]
    res = run_step(inputs, {"p": Lp, "s": Ls}, n, prompt_idx, sample_idx)
    yp = np.stack([np.asarray(res.results[c]["y_p"], dtype=np.float32) for c in range(B)], axis=0)
    ys = np.stack([np.asarray(res.results[c]["y_s"], dtype=np.float32) for c in range(Bs)], axis=0)
    return (yp, ys)
```
